# Optimizing a Trainium2 kernel written in Bass

```python
import jax
import jax.numpy as jnp
from jax import lax
import numpy as np

D_MODEL = 2048
BATCH = 2
SEQ = 4096
DEPTH = 2

GRID_W = 64
CTX_LEN = 256
N_MIXERS = 2
MIXER_DELTANET = 0
MIXER_FOURIER = 1
N_DELTANET_LAYERS = (DEPTH + 1) // 2
N_FOURIER_LAYERS = DEPTH // 2

D_FF = 5632
FFN_RES = 0.5
N_SUB = 3
N_MOD = 3 * N_SUB

HK = 16
HV = 32
DK = 128
DV = 128
KEY_DIM = HK * DK
VAL_DIM = HV * DV
QKV_DIM = 2 * KEY_DIM + VAL_DIM
DN_IN_DIM = QKV_DIM + VAL_DIM + 4 * HV
CONV_W = 5
CONV_PAD = CONV_W // 2
CHUNK = 64

N_FGROUPS = 4
FGROUP_DIM = D_MODEL // N_FGROUPS

EPS = 1e-6
MOD_INIT = 0.5

kernel_name = "interleaved_deltanet_fourier_macaron_dit"


def rms_norm(x, g):
    xf = x.astype(jnp.float32)
    y = xf * lax.rsqrt(jnp.mean(xf * xf, axis=-1, keepdims=True) + EPS)
    return (y * g.astype(jnp.float32)).astype(x.dtype)


def ada_norm(x, g, shift, scale):
    return rms_norm(x, g) * (1 + scale) + shift


def modulation(cond, w_mod, b_mod):
    m = jax.nn.silu(cond) @ w_mod + b_mod
    return jnp.split(m[..., None, :], N_MOD, axis=-1)


def swiglu(h, w_gate, w_up, w_down):
    return (jax.nn.silu(h @ w_gate) * (h @ w_up)) @ w_down


def ffn_half(x, g, mod3, w):
    shift, scale, gate = mod3
    return x + FFN_RES * gate * swiglu(ada_norm(x, g, shift, scale), *w)


def centred_dwconv(x, w):
    ch = x.shape[-1]
    return lax.conv_general_dilated(
        x, w[:, None, :].astype(x.dtype), window_strides=(1,),
        padding=[(CONV_PAD, CONV_PAD)], dimension_numbers=("NWC", "WIO", "NWC"),
        feature_group_count=ch)


def l2_normalize(t):
    tf = t.astype(jnp.float32)
    return tf * lax.rsqrt(jnp.sum(tf * tf, axis=-1, keepdims=True) + EPS)


def chunk_gated_delta(q, k, v, g, beta, s0):
    b, seq_len, h = q.shape[:3]
    n = seq_len // CHUNK

    def to_chunks(t):
        t = t.astype(jnp.float32).reshape(b, n, CHUNK, h, *t.shape[3:])
        return jnp.moveaxis(t, 3, 1)

    q = to_chunks(q) * (DK ** -0.5)
    k, v, g, beta = to_chunks(k), to_chunks(v), to_chunks(g), to_chunks(beta)
    g = jnp.cumsum(g, axis=-1)
    lower = jnp.tril(jnp.ones((CHUNK, CHUNK), dtype=bool))
    diff = g[..., :, None] - g[..., None, :]
    decay = jnp.exp(jnp.where(lower, diff, -jnp.inf))
    k_beta = k * beta[..., None]
    v_beta = v * beta[..., None]
    lmat = jnp.einsum("bhnid,bhnjd->bhnij", k_beta, k) * decay
    rhs = jnp.concatenate([v_beta, k_beta * jnp.exp(g)[..., None]], axis=-1)
    sol = lax.linalg.triangular_solve(lmat, rhs, left_side=True, lower=True,
                                      unit_diagonal=True)
    u, w = sol[..., :DV], sol[..., DV:]
    attn = jnp.einsum("bhnid,bhnjd->bhnij", q, k) * decay
    q_dec = q * jnp.exp(g)[..., None]
    k_tail = k * jnp.exp(g[..., -1:] - g)[..., None]
    g_last = jnp.exp(g[..., -1])

    xs = tuple(jnp.moveaxis(t, 2, 0) for t in (u, w, q_dec, k_tail, attn, g_last))

    def step(state, inp):
        u_n, w_n, qd_n, kt_n, a_n, gl_n = inp
        v_new = u_n - jnp.einsum("bhcd,bhde->bhce", w_n, state)
        o_n = (jnp.einsum("bhcd,bhde->bhce", qd_n, state)
               + jnp.einsum("bhij,bhje->bhie", a_n, v_new))
        state = state * gl_n[..., None, None] + jnp.einsum("bhcd,bhce->bhde", kt_n, v_new)
        return state, o_n

    s_fin, o = lax.scan(step, s0.astype(jnp.float32), xs)
    o = jnp.transpose(o, (1, 0, 3, 2, 4)).reshape(b, seq_len, h, DV)
    return o, s_fin


def deltanet_inputs(h, w_in, conv_w, a_log, dt_bias):
    b, seq_len, _ = h.shape
    p = h @ w_in
    qkv, z, bb, aa = jnp.split(p, [QKV_DIM, QKV_DIM + VAL_DIM, QKV_DIM + VAL_DIM + 2 * HV], axis=-1)
    qkv = jax.nn.silu(centred_dwconv(qkv, conv_w))
    q, k, v = jnp.split(qkv, [KEY_DIM, 2 * KEY_DIM], axis=-1)
    q = jnp.repeat(l2_normalize(q.reshape(b, seq_len, HK, DK)), HV // HK, axis=2)
    k = jnp.repeat(l2_normalize(k.reshape(b, seq_len, HK, DK)), HV // HK, axis=2)
    v = v.reshape(b, seq_len, HV, DV)
    z = z.reshape(b, seq_len, HV, DV)
    beta = jax.nn.sigmoid(bb.reshape(b, seq_len, 2, HV).astype(jnp.float32))
    g = -jnp.exp(a_log.astype(jnp.float32)) * jax.nn.softplus(
        aa.reshape(b, seq_len, 2, HV).astype(jnp.float32) + dt_bias.astype(jnp.float32))
    return q, k, v, z, g, beta


def gated_out(o, z, norm_g, w_out):
    b, seq_len = o.shape[:2]
    y = rms_norm(o, norm_g) * jax.nn.silu(z.astype(jnp.float32))
    return y.reshape(b, seq_len, VAL_DIM).astype(z.dtype) @ w_out


def deltanet_mixer(u, uc, w_in, conv_w, a_log, dt_bias, norm_g, w_out, ctx_out):
    q, k, v, z, g, beta = deltanet_inputs(u, w_in, conv_w, a_log, dt_bias)
    qc, kc, vc, zc, gc, betac = deltanet_inputs(uc, w_in, conv_w, a_log, dt_bias)
    s_zero = jnp.zeros((u.shape[0], HV, DK, DV), jnp.float32)
    o_lat, o_ctx = [], []
    for d in range(2):
        rev = (lambda t: jnp.flip(t, axis=1)) if d == 1 else (lambda t: t)
        oc, s_ctx = chunk_gated_delta(rev(qc), rev(kc), rev(vc), rev(gc[:, :, d]),
                                      rev(betac[:, :, d]), s_zero)
        ol, _ = chunk_gated_delta(rev(q), rev(k), rev(v), rev(g[:, :, d]),
                                  rev(beta[:, :, d]), s_ctx)
        o_lat.append(rev(ol))
        o_ctx.append(rev(oc))
    y = gated_out(o_lat[0] + o_lat[1], z, norm_g, w_out)
    yc = gated_out(o_ctx[0] + o_ctx[1], zc, norm_g, w_out) if ctx_out else None
    return y, yc


def fourier_mixer(u, w_out):
    b, seq_len, d = u.shape
    ug = u.astype(jnp.float32).reshape(b, seq_len, N_FGROUPS, FGROUP_DIM)
    y = jnp.fft.fft2(ug, axes=(1, 3), norm="ortho").real
    return y.reshape(b, seq_len, d).astype(u.dtype) @ w_out


def setup_inputs(seed: int = 0) -> dict:
    key = jax.random.key(seed)
    ks = jax.random.split(key, 24)
    f32 = jnp.float32

    def dense(k, shape, fan_in):
        return jax.random.normal(k, shape, f32) * (fan_in ** -0.5)

    x = jax.random.normal(ks[0], (BATCH, SEQ, D_MODEL), f32)
    c = jax.random.normal(ks[1], (BATCH, D_MODEL), f32)
    ctx = jax.random.normal(ks[2], (BATCH, CTX_LEN, D_MODEL), f32)
    c_ctx = jax.random.normal(ks[3], (D_MODEL,), f32)
    norm_g = 1.0 + 0.02 * jax.random.normal(ks[4], (DEPTH, N_SUB, D_MODEL), f32)
    mod_w = MOD_INIT * dense(ks[5], (DEPTH, D_MODEL, N_MOD * D_MODEL), D_MODEL)
    mod_b = 0.02 * jax.random.normal(ks[6], (DEPTH, N_MOD * D_MODEL), f32)
    ffn_w_gate = dense(ks[7], (DEPTH, 2, D_MODEL, D_FF), D_MODEL)
    ffn_w_up = dense(ks[8], (DEPTH, 2, D_MODEL, D_FF), D_MODEL)
    ffn_w_down = dense(ks[9], (DEPTH, 2, D_FF, D_MODEL), D_FF)
    dn_w_in = dense(ks[10], (N_DELTANET_LAYERS, D_MODEL, DN_IN_DIM), D_MODEL)
    dn_conv_w = dense(ks[11], (N_DELTANET_LAYERS, CONV_W, QKV_DIM), CONV_W)
    a_init = jax.random.uniform(ks[12], (N_DELTANET_LAYERS, 2, HV), f32, 1.0, 16.0)
    dn_a_log = jnp.log(a_init)
    dt = jnp.exp(jax.random.uniform(ks[13], (N_DELTANET_LAYERS, 2, HV), f32,
                                    jnp.log(0.001), jnp.log(0.1)))
    dn_dt_bias = dt + jnp.log(-jnp.expm1(-dt))
    dn_norm_g = 1.0 + 0.02 * jax.random.normal(ks[14], (N_DELTANET_LAYERS, DV), f32)
    dn_w_out = dense(ks[15], (N_DELTANET_LAYERS, VAL_DIM, D_MODEL), VAL_DIM)
    fn_w_out = dense(ks[16], (N_FOURIER_LAYERS, D_MODEL, D_MODEL), D_MODEL)
    final_norm_g = 1.0 + 0.02 * jax.random.normal(ks[17], (D_MODEL,), f32)
    return {"x": x, "c": c, "ctx": ctx, "c_ctx": c_ctx, "norm_g": norm_g,
            "mod_w": mod_w, "mod_b": mod_b, "ffn_w_gate": ffn_w_gate,
            "ffn_w_up": ffn_w_up, "ffn_w_down": ffn_w_down, "dn_w_in": dn_w_in,
            "dn_conv_w": dn_conv_w, "dn_a_log": dn_a_log, "dn_dt_bias": dn_dt_bias,
            "dn_norm_g": dn_norm_g, "dn_w_out": dn_w_out, "fn_w_out": fn_w_out,
            "final_norm_g": final_norm_g}


def reference(x, c, ctx, c_ctx, norm_g, mod_w, mod_b, ffn_w_gate, ffn_w_up, ffn_w_down,
              dn_w_in, dn_conv_w, dn_a_log, dn_dt_bias, dn_norm_g, dn_w_out, fn_w_out,
              final_norm_g):
    h, hc = x, ctx
    for i in range(DEPTH):
        kind = i % N_MIXERS
        j = i // N_MIXERS
        ctx_next = any(l % N_MIXERS == MIXER_DELTANET for l in range(i + 1, DEPTH))
        ctx_here = ctx_next or kind == MIXER_DELTANET
        m = modulation(c, mod_w[i], mod_b[i])
        mc = modulation(c_ctx, mod_w[i], mod_b[i])
        ffn_a = (ffn_w_gate[i, 0], ffn_w_up[i, 0], ffn_w_down[i, 0])
        ffn_b = (ffn_w_gate[i, 1], ffn_w_up[i, 1], ffn_w_down[i, 1])

        h = ffn_half(h, norm_g[i, 0], m[0:3], ffn_a)
        if ctx_here:
            hc = ffn_half(hc, norm_g[i, 0], mc[0:3], ffn_a)

        u = ada_norm(h, norm_g[i, 1], m[3], m[4])
        if kind == MIXER_DELTANET:
            uc = ada_norm(hc, norm_g[i, 1], mc[3], mc[4])
            y, yc = deltanet_mixer(u, uc, dn_w_in[j], dn_conv_w[j], dn_a_log[j], dn_dt_bias[j],
                                   dn_norm_g[j], dn_w_out[j], ctx_next)
        else:
            y = fourier_mixer(u, fn_w_out[j])
            yc = (fourier_mixer(ada_norm(hc, norm_g[i, 1], mc[3], mc[4]), fn_w_out[j])
                  if ctx_next else None)
        h = h + m[5] * y
        if ctx_next:
            hc = hc + mc[5] * yc
            hc = ffn_half(hc, norm_g[i, 2], mc[6:9], ffn_b)

        h = ffn_half(h, norm_g[i, 2], m[6:9], ffn_b)
    return rms_norm(h, final_norm_g)
```

```python
import numpy as np
import concourse.bass as bass
import concourse.mybir as mybir
from concourse.bass_utils import run_bass_kernel_spmd

F32 = mybir.dt.float32
BF16 = mybir.dt.bfloat16
AF = mybir.ActivationFunctionType
ALU = mybir.AluOpType
AX = mybir.AxisListType


class T:
    __slots__ = ("name", "w", "rd")

    def __init__(self, name=""):
        self.name = name
        self.w = None
        self.rd = []


class DSem:
    __slots__ = ("h", "count", "last")

    def __init__(self, h):
        self.h = h
        self.count = 0
        self.last = None


class Prog:
    ENGS = ("pe", "act", "dve", "pool", "sp")

    def __init__(self, nc):
        self.nc = nc
        self.ins = []
        self.es = None
        self._ctx = []

    def enter(self, cm):
        v = cm.__enter__()
        self._ctx.append(cm)
        return v

    def sb(self, name, shape, dt):
        return self.enter(self.nc.sbuf_tensor("sb_" + name, list(shape), dt))

    def ps(self, name, shape, dt=F32):
        return self.enter(self.nc.psum_tensor("ps_" + name, list(shape), dt))

    def dsem(self, name):
        return DSem(self.enter(self.nc.semaphore(name)))

    def close(self):
        while self._ctx:
            self._ctx.pop().__exit__(None, None, None)

    def op(self, eng, fn, reads=(), writes=(), dsem=None, after=()):
        idx = len(self.ins)
        isdma = dsem is not None
        deps = set(after)
        if isdma and dsem.last is not None:
            deps.add(dsem.last)
        for t in reads:
            if t.w is not None:
                deps.add(t.w)
        for t in writes:
            if t.w is not None:
                deps.add(t.w)
            deps.update(t.rd)
        raw = set(t.w for t in reads if t.w is not None)
        keep = []
        for d in deps:
            di = self.ins[d]
            if (not di["dma"]) and (not isdma) and di["eng"] == eng and d not in raw:
                continue
            keep.append(d)
            di["sig"] = True
        ev = None
        if isdma:
            dsem.count += 16
            ev = (dsem.h, dsem.count)
        self.ins.append(dict(eng=eng, fn=fn, deps=keep, dma=isdma, sig=isdma, ev=ev))
        if isdma:
            dsem.last = idx
        for t in reads:
            t.rd.append(idx)
        for t in writes:
            t.w = idx
            t.rd = []
        return idx

    def emit(self):
        nc = self.nc
        es = {e: self.enter(nc.semaphore("es_" + e)) for e in self.ENGS}
        cnt = {e: 0 for e in self.ENGS}
        for it in self.ins:
            if not it["dma"] and it["sig"]:
                cnt[it["eng"]] += 1
                it["ev"] = (es[it["eng"]], cnt[it["eng"]])
        per = {e: [it for it in self.ins if it["eng"] == e] for e in self.ENGS}
        ins = self.ins

        def body(e):
            def f(eng):
                waited = {}
                for it in per[e]:
                    need = {}
                    for d in it["deps"]:
                        s, v = ins[d]["ev"]
                        k = id(s)
                        if waited.get(k, 0) < v and need.get(k, (None, 0))[1] < v:
                            need[k] = (s, v)
                    for k, (s, v) in need.items():
                        eng.wait_ge(s, v)
                        waited[k] = v
                    r = it["fn"](eng)
                    if it["sig"]:
                        s, v = it["ev"]
                        r.then_inc(s, 16 if it["dma"] else 1)
            return f

        with nc.Block() as block:
            block.tensor(body("pe"))
            block.scalar(body("act"))
            block.vector(body("dve"))
            block.gpsimd(body("pool"))
            block.sync(body("sp"))


D = 2048
KC = 16
DFF = 5632
FC = 44
EPS = 1e-6
G = 2


def I(meth, *a, **kw):
    return lambda e: getattr(e, meth)(*a, **kw)


def ttiles(Tn):
    out = []
    t = 0
    while t < Tn:
        n = min(512, Tn - t)
        out.append((t, n))
        t += n
    return out


class TokStage:
    def __init__(self, P, nc, Tlat, Tctx, NV):
        self.P, self.nc = P, nc
        self.Tlat, self.Tctx = Tlat, Tctx
        self.T = Tlat + Tctx
        self.tiles = ttiles(Tlat) + ([(Tlat, Tctx)] if Tctx else [])
        self.nt = len(self.tiles)
        T_ = self.T
        self.h = P.sb("h", [128, KC, T_], F32)
        self.xn = P.sb("xn", [128, KC, T_], BF16)
        self.mods = P.sb("mods", [128, NV, KC], F32)
        self.rstd = P.sb("rstd", [128, T_], F32)
        self.ones = P.sb("ones", [128, 128], F32)
        self.sq = [P.sb("sq%d" % i, [128, 512], F32) for i in range(2)]
        self.tmp = [P.sb("tmp%d" % i, [128, 512], F32) for i in range(2)]
        self.sg = [P.sb("sg%d" % i, [128, 512], BF16) for i in range(2)]
        self.act = [P.sb("act%d" % i, [128, G, T_], BF16) for i in range(2)]
        self.wg = [P.sb("wg%d" % i, [128, KC, G * 128], BF16) for i in range(2)]
        self.wu = [P.sb("wu%d" % i, [128, KC, G * 128], BF16) for i in range(2)]
        self.wd = [P.sb("wd%d" % i, [128, G, D], BF16) for i in range(2)]
        self.psA = [P.ps("psA%d" % i, [128, 512]) for i in range(2)]
        self.psB = [P.ps("psB%d" % i, [128, 512]) for i in range(2)]
        self.psC = [P.ps("psC%d" % i, [128, 512]) for i in range(2)]
        self.psS = [P.ps("psS%d" % i, [128, 512]) for i in range(2)]
        self.th = [[T("h") for _ in range(self.nt)] for _ in range(KC)]
        self.txn = [[T("xn") for _ in range(self.nt)] for _ in range(KC)]
        self.tmods = T("mods")
        self.trstd = [T("rstd") for _ in range(self.nt)]
        self.tones = T("ones")
        self.tsq = [T("sq") for _ in range(2)]
        self.ttmp = [T("tmp") for _ in range(2)]
        self.tsg = [T("sg") for _ in range(2)]
        self.tact = [[[T("act") for _ in range(self.nt)] for _ in range(G)] for _ in range(2)]
        self.twg = [T("wg") for _ in range(2)]
        self.twu = [T("wu") for _ in range(2)]
        self.twd = [T("wd") for _ in range(2)]
        self.tpsA = [T("psA") for _ in range(2)]
        self.tpsB = [T("psB") for _ in range(2)]
        self.tpsC = [T("psC") for _ in range(2)]
        self.tpsS = [T("psS") for _ in range(2)]
        self.swg = [P.dsem("swg%d" % i) for i in range(2)]
        self.swu = [P.dsem("swu%d" % i) for i in range(2)]
        self.swd = [P.dsem("swd%d" % i) for i in range(2)]
        self.sio = [P.dsem("sio%d" % i) for i in range(9)]
        self.cnt = 0
        self.grp = 0
        P.op("pool", I("memset", self.ones[:], 1.0), writes=[self.tones])

    def rr(self):
        self.cnt += 1
        return self.cnt % 2

    def load_h(self, h_dram, mods_dram):
        P = self.P
        P.op("sp", I("dma_start", out=self.mods[:], in_=mods_dram), writes=[self.tmods], dsem=self.sio[0])
        for k in range(KC):
            P.op("sp", I("dma_start", out=self.h[:, k, :], in_=h_dram[:, k, :]),
                 writes=self.th[k], dsem=self.sio[1 + k % 8])

    def store(self, out_dram, sb, tl):
        P = self.P
        ids = []
        for k in range(KC):
            ids.append(P.op("sp", I("dma_start", out=out_dram[:, k, :], in_=sb[:, k, :]),
                            reads=tl[k], dsem=self.sio[1 + k % 8]))
        return ids

    def mod_gw(self, vg, vscale, vout):
        m = self.mods
        self.P.op("dve", I("scalar_tensor_tensor", out=m[:, vout, :], in0=m[:, vscale, :], scalar=1.0,
                           in1=m[:, vg, :], op0=ALU.add, op1=ALU.mult),
                  reads=[self.tmods], writes=[self.tmods])

    def mod_scale(self, vin, vout, c):
        m = self.mods
        self.P.op("dve", I("tensor_scalar", out=m[:, vout, :], in0=m[:, vin, :], scalar1=c, scalar2=None, op0=ALU.mult),
                  reads=[self.tmods], writes=[self.tmods])

    def rms_stats(self):
        P = self.P
        for ti, (t0, n) in enumerate(self.tiles):
            s = self.rr()
            for k in range(KC):
                q = self.rr()
                P.op("act", I("activation", out=self.sq[q][:, 0:n], in_=self.h[:, k, t0:t0 + n], func=AF.Square),
                     reads=[self.th[k][ti]], writes=[self.tsq[q]])
                P.op("pe", I("matmul", self.psS[s][:, 0:n], lhsT=self.ones[:], rhs=self.sq[q][:, 0:n],
                             start=(k == 0), stop=(k == KC - 1)),
                     reads=[self.tsq[q], self.tones], writes=[self.tpsS[s]])
            q = self.rr()
            P.op("dve", I("tensor_scalar", out=self.tmp[q][:, 0:n], in0=self.psS[s][:, 0:n],
                          scalar1=1.0 / D, scalar2=EPS, op0=ALU.mult, op1=ALU.add),
                 reads=[self.tpsS[s]], writes=[self.ttmp[q]])
            P.op("act", I("activation", out=self.tmp[q][:, 0:n], in_=self.tmp[q][:, 0:n], func=AF.Sqrt),
                 reads=[self.ttmp[q]], writes=[self.ttmp[q]])
            P.op("dve", I("reciprocal", out=self.rstd[:, t0:t0 + n], in_=self.tmp[q][:, 0:n]),
                 reads=[self.ttmp[q]], writes=[self.trstd[ti]])

    def ada_norm(self, out_sb, out_tiles, vgw, vshift, vgw_c=None, vshift_c=None):
        P = self.P
        m = self.mods
        for ti, (t0, n) in enumerate(self.tiles):
            isctx = self.Tctx and ti == self.nt - 1
            g_ = vgw_c if isctx else vgw
            s_ = vshift_c if isctx else vshift
            for k in range(KC):
                q = self.rr()
                P.op("dve", I("scalar_tensor_tensor", out=self.tmp[q][:, 0:n], in0=self.h[:, k, t0:t0 + n],
                              scalar=m[:, g_, k:k + 1], in1=self.rstd[:, t0:t0 + n], op0=ALU.mult, op1=ALU.mult),
                     reads=[self.th[k][ti], self.trstd[ti], self.tmods], writes=[self.ttmp[q]])
                if s_ is None:
                    P.op("act", I("activation", out=out_sb[:, k, t0:t0 + n], in_=self.tmp[q][:, 0:n], func=AF.Copy),
                         reads=[self.ttmp[q]], writes=[out_tiles[k][ti]])
                else:
                    P.op("act", I("activation", out=out_sb[:, k, t0:t0 + n], in_=self.tmp[q][:, 0:n],
                                  func=AF.Identity, bias=m[:, s_, k:k + 1]),
                         reads=[self.ttmp[q], self.tmods], writes=[out_tiles[k][ti]])

    def load_w(self, wg, wu, wd, g):
        P = self.P
        s = self.grp % 2
        f0 = g * G * 128
        wgs = wg.rearrange("(k p) f -> p k f", p=128)
        wus = wu.rearrange("(k p) f -> p k f", p=128)
        wds = wd.rearrange("(j p) d -> p j d", p=128)
        P.op("pool", I("dma_start", out=self.wg[s][:, :, :], in_=wgs[:, :, f0:f0 + G * 128]),
             writes=[self.twg[s]], dsem=self.swg[s])
        P.op("pool", I("dma_start", out=self.wu[s][:, :, :], in_=wus[:, :, f0:f0 + G * 128]),
             writes=[self.twu[s]], dsem=self.swu[s])
        P.op("pool", I("dma_start", out=self.wd[s][:, :, :], in_=wds[:, g * G:(g + 1) * G, :]),
             writes=[self.twd[s]], dsem=self.swd[s])
        self.grp += 1
        return s

    def ffn_p1(self, s):
        P = self.P
        for j in range(G):
            for ti, (t0, n) in enumerate(self.tiles):
                a = self.rr()
                for k in range(KC):
                    P.op("pe", I("matmul", self.psA[a][:, 0:n], lhsT=self.wg[s][:, k, j * 128:(j + 1) * 128],
                                 rhs=self.xn[:, k, t0:t0 + n], start=(k == 0), stop=(k == KC - 1)),
                         reads=[self.twg[s], self.txn[k][ti]], writes=[self.tpsA[a]])
                for k in range(KC):
                    P.op("pe", I("matmul", self.psB[a][:, 0:n], lhsT=self.wu[s][:, k, j * 128:(j + 1) * 128],
                                 rhs=self.xn[:, k, t0:t0 + n], start=(k == 0), stop=(k == KC - 1)),
                         reads=[self.twu[s], self.txn[k][ti]], writes=[self.tpsB[a]])
                P.op("act", I("activation", out=self.sg[a][:, 0:n], in_=self.psA[a][:, 0:n], func=AF.Silu),
                     reads=[self.tpsA[a]], writes=[self.tsg[a]])
                P.op("dve", I("tensor_tensor", out=self.act[s][:, j, t0:t0 + n], in0=self.sg[a][:, 0:n],
                              in1=self.psB[a][:, 0:n], op=ALU.mult),
                     reads=[self.tsg[a], self.tpsB[a]], writes=[self.tact[s][j][ti]])

    def ffn_p2(self, s, vgate, vgate_c):
        P = self.P
        m = self.mods
        for d in range(KC):
            for ti, (t0, n) in enumerate(self.tiles):
                isctx = self.Tctx and ti == self.nt - 1
                gv = vgate_c if isctx else vgate
                c = self.rr()
                for j in range(G):
                    P.op("pe", I("matmul", self.psC[c][:, 0:n], lhsT=self.wd[s][:, j, d * 128:(d + 1) * 128],
                                 rhs=self.act[s][:, j, t0:t0 + n], start=(j == 0), stop=(j == G - 1)),
                         reads=[self.twd[s], self.tact[s][j][ti]], writes=[self.tpsC[c]])
                P.op("dve", I("scalar_tensor_tensor", out=self.h[:, d, t0:t0 + n], in0=self.psC[c][:, 0:n],
                              scalar=m[:, gv, d:d + 1], in1=self.h[:, d, t0:t0 + n], op0=ALU.mult, op1=ALU.add),
                     reads=[self.tpsC[c], self.th[d][ti], self.tmods], writes=[self.th[d][ti]])

    def ffn(self, wg, wu, wd, vgw, vshift, vgate, vgw_c=None, vshift_c=None, vgate_c=None):
        self.rms_stats()
        self.ada_norm(self.xn, self.txn, vgw, vshift, vgw_c, vshift_c)
        NG = FC // G
        prev = None
        for g in range(NG):
            s = self.load_w(wg, wu, wd, g)
            self.ffn_p1(s)
            if prev is not None:
                self.ffn_p2(prev, vgate, vgate_c)
            prev = s
        self.ffn_p2(prev, vgate, vgate_c)

    def mix(self, y_d, nfc, w_d, vgate):
        P = self.P
        m = self.mods
        ws = w_d.rearrange("(j p) d -> p j d", p=128)
        bufs = [(self.wg[0], self.twg[0], self.swg[0]), (self.wu[0], self.twu[0], self.swu[0]),
                (self.wg[1], self.twg[1], self.swg[1]), (self.wu[1], self.twu[1], self.swu[1])]
        lat_tiles = [(ti, t0, n) for ti, (t0, n) in enumerate(self.tiles) if t0 < self.Tlat]
        nb = 0
        for half in range(nfc // KC):
            for k in range(KC):
                P.op("sp", I("dma_start", out=self.xn[:, k, 0:self.Tlat], in_=y_d[:, half * KC + k, :]),
                     writes=[self.txn[k][ti] for ti, _, _ in lat_tiles], dsem=self.sio[1 + k % 8])
            for blk in range(D // (G * 128)):
                buf, tb, sb_ = bufs[nb % 4]
                nb += 1
                P.op("pool", I("dma_start", out=buf[:, :, :], in_=ws[:, half * KC:(half + 1) * KC, blk * G * 128:(blk + 1) * G * 128]),
                     writes=[tb], dsem=sb_)
                for dj in range(G):
                    d = blk * G + dj
                    for ti, t0, n in lat_tiles:
                        c = self.rr()
                        for k in range(KC):
                            P.op("pe", I("matmul", self.psC[c][:, 0:n], lhsT=buf[:, k, dj * 128:(dj + 1) * 128],
                                         rhs=self.xn[:, k, t0:t0 + n], start=(k == 0), stop=(k == KC - 1)),
                                 reads=[tb, self.txn[k][ti]], writes=[self.tpsC[c]])
                        P.op("dve", I("scalar_tensor_tensor", out=self.h[:, d, t0:t0 + n], in0=self.psC[c][:, 0:n],
                                      scalar=m[:, vgate, d:d + 1], in1=self.h[:, d, t0:t0 + n], op0=ALU.mult, op1=ALU.add),
                             reads=[self.tpsC[c], self.th[d][ti], self.tmods], writes=[self.th[d][ti]])


F32R = mybir.dt.float32r
C = 128
NCH = 34
NLAT = 32
DKS = 128 ** -0.5
NEG = -30000.0
LE, GE, GT, LT, ONES, IDENT, NEGF, NEGB, NM0 = range(9)
NCONST = 15
NSTAGE = 6


class RR:
    def __init__(self, items):
        self.items = items
        self.i = 0

    def get(self):
        it = self.items[self.i % len(self.items)]
        self.i += 1
        return it


def build_scan(nc, qT_d, kT_d, kM_d, vM_d, sz_d, g_d, b_d, ng_d, cst_d, y_d, NQH=4):
    P = Prog(nc)
    qT = P.sb("qT", [128, NCH * C], BF16)
    kT = P.sb("kT", [128, NCH * C], BF16)
    kM = P.sb("kM", [128, NCH, C], BF16)
    vM = P.sb("vM", [128, NCH, 2, C], BF16)
    sz = P.sb("sz", [128, NLAT, 2, C], BF16)
    gM = P.sb("gM", [128, NCH, 4], F32)
    bM = P.sb("bM", [128, NCH, 4], F32)
    oacc = P.sb("oacc", [128, NLAT, 2, C], F32)
    yout = P.sb("yout", [128, NLAT, 2, C], BF16)
    ss = P.sb("ss", [128, NLAT * 2], F32)
    cst = P.sb("cst", [128, NCONST, C], F32)
    ng = P.sb("ng", [128, C], F32)
    identr_ = P.sb("identr", [128, C], F32)
    identr = identr_[:].bitcast(F32R)
    S32 = [P.sb("S32_%d" % c, [128, C], F32) for c in range(4)]
    Sbf = [P.sb("Sbf_%d" % c, [128, C], BF16) for c in range(4)]
    tS32 = [T() for _ in range(4)]
    tSbf = [T() for _ in range(4)]
    tq, tk, tkM, tv, tsz, tg, tb, tcst, tng, tidr, tss, tyout = [T() for _ in range(12)]
    toacc = [[T() for _ in range(2)] for _ in range(NLAT)]
    dsem = [P.dsem("ld%d" % i) for i in range(12)]

    def mkpool(name, n, dt):
        return RR([(P.sb("%s%d" % (name, i), [128, C], dt), T()) for i in range(n)])

    p32 = mkpool("p32_", 32, F32)
    p32r = mkpool("p32r_", 48, F32)
    pbf = mkpool("pbf_", 32, BF16)
    pU = mkpool("pU_", 8, F32)
    pUt = mkpool("pUt_", 8, F32)
    pR = mkpool("pR_", 12, F32)
    psm = RR([(P.sb("sm%d" % i, [128, 16], F32), T()) for i in range(6)])
    banks = [P.ps("bank%d" % i, [128, 512]) for i in range(8)]
    pps = RR([(banks[i % 8][:, (i // 8) * C:(i // 8 + 1) * C], T()) for i in range(32)])

    def cs(i):
        return cst[:, i, :]

    P.op("sp", I("dma_start", out=cst[:], in_=cst_d), writes=[tcst], dsem=dsem[0])
    P.op("sp", I("dma_start", out=ng[:], in_=ng_d), writes=[tng], dsem=dsem[1])
    P.op("dve", I("tensor_copy", out=identr, in_=cs(IDENT)), reads=[tcst], writes=[tidr])

    fo = list(range(NCH))
    bo = [1, 0] + list(range(NCH - 1, 1, -1))
    out_ids = []

    for qh in range(NQH):
        P.op("sp", I("dma_start", out=qT[:], in_=qT_d[:, qh, :]), writes=[tq], dsem=dsem[2])
        P.op("sp", I("dma_start", out=kT[:], in_=kT_d[:, qh, :]), writes=[tk], dsem=dsem[3])
        P.op("sp", I("dma_start", out=kM[:], in_=kM_d[:, qh, :, :]), writes=[tkM], dsem=dsem[4])
        P.op("sp", I("dma_start", out=vM[:], in_=vM_d[:, qh, :, :, :]), writes=[tv], dsem=dsem[5])
        P.op("sp", I("dma_start", out=sz[:], in_=sz_d[:, qh, :, :, :]), writes=[tsz], dsem=dsem[6])
        P.op("sp", I("dma_start", out=gM[:], in_=g_d[:, qh, :, :]), writes=[tg], dsem=dsem[7])
        P.op("sp", I("dma_start", out=bM[:], in_=b_d[:, qh, :, :]), writes=[tb], dsem=dsem[8])
        for c in range(4):
            P.op("pool", I("memset", S32[c][:], 0.0), writes=[tS32[c]])
            P.op("pool", I("memset", Sbf[c][:], 0.0), writes=[tSbf[c]])
        first_o = [[True, True] for _ in range(NLAT)]

        def pre(d, n):
            lat = n >= 2
            kc = kT[:, n * C:(n + 1) * C]
            qc = qT[:, n * C:(n + 1) * C]
            res = {}
            kk, tkk = pps.get()
            P.op("pe", I("matmul", kk, lhsT=kc, rhs=kc, start=True, stop=True), reads=[tk], writes=[tkk])
            yield
            kks, tkks = p32.get()
            P.op("dve", I("tensor_tensor", out=kks[:], in0=kk, in1=cs(LT if d == 0 else GT), op=ALU.mult),
                 reads=[tkk, tcst], writes=[tkks])
            yield
            if lat:
                qk, tqk = pps.get()
                P.op("pe", I("matmul", qk, lhsT=kc, rhs=qc, start=True, stop=True), reads=[tk, tq], writes=[tqk])
                yield
                qks, tqks = p32.get()
                P.op("act", I("activation", out=qks[:], in_=qk, func=AF.Copy, scale=DKS), reads=[tqk], writes=[tqks])
                yield
            gcols = gM[:, n, 2 * d:2 * d + 2]
            st, tst = pps.get()
            P.op("pe", I("matmul", st[:, 0:2], lhsT=cs(LE if d == 0 else GE), rhs=gcols, start=True, stop=True),
                 reads=[tcst, tg], writes=[tst])
            P.op("pe", I("matmul", st[:, 2:4], lhsT=cs(GT if d == 0 else LT), rhs=gcols, start=True, stop=True),
                 reads=[tcst, tg], writes=[tst])
            P.op("pe", I("matmul", st[:, 4:6], lhsT=cs(ONES), rhs=gcols, start=True, stop=True),
                 reads=[tcst, tg], writes=[tst])
            yield
            sm, tsm = psm.get()
            P.op("act", I("activation", out=sm[:, 0:6], in_=st[:, 0:6], func=AF.Exp), reads=[tst], writes=[tsm])
            P.op("pool", I("tensor_scalar", out=sm[:, 6:8], in0=sm[:, 0:2], scalar1=-1.0, scalar2=0.0, op0=ALU.mult, op1=ALU.add),
                 reads=[tsm], writes=[tsm])
            P.op("pool", I("tensor_scalar", out=sm[:, 8:10], in0=sm[:, 0:2], scalar1=DKS, scalar2=0.0, op0=ALU.mult, op1=ALU.add),
                 reads=[tsm], writes=[tsm])
            P.op("pool", I("tensor_scalar", out=sm[:, 10:12], in0=bM[:, n, 2 * d:2 * d + 2], scalar1=-1.0, scalar2=0.0,
                           op0=ALU.mult, op1=ALU.add), reads=[tb], writes=[tsm])
            yield
            res["sm"] = (sm, tsm)
            ch = []
            for v in range(2):
                gx, tgx = p32.get()
                P.op("pool", I("tensor_scalar", out=gx[:], in0=cs(LE if d == 0 else GE), scalar1=gM[:, n, 2 * d + v:2 * d + v + 1],
                               scalar2=0.0, op0=ALU.mult, op1=ALU.add), reads=[tcst, tg], writes=[tgx])
                yield
                dt_, tdt = pps.get()
                P.op("pe", I("matmul", dt_, lhsT=cs(GT if d == 0 else LT), rhs=gx[:], start=True, stop=False),
                     reads=[tcst, tgx], writes=[tdt])
                P.op("pe", I("matmul", dt_, lhsT=cs(IDENT), rhs=cs(NEGF if d == 0 else NEGB), start=False, stop=True),
                     reads=[tcst], writes=[tdt])
                yield
                dec, tdec = p32.get()
                P.op("act", I("activation", out=dec[:], in_=dt_, func=AF.Exp), reads=[tdt], writes=[tdec])
                yield
                M, tM = pU.get()
                P.op("dve", I("scalar_tensor_tensor", out=M[:].bitcast(F32R), in0=kks[:], scalar=bM[:, n, 2 * d + v:2 * d + v + 1], in1=dec[:],
                              op0=ALU.mult, op1=ALU.mult), reads=[tkks, tb, tdec], writes=[tM])
                yield
                c = dict(M=(M, tM))
                if lat:
                    at, tat = pbf.get()
                    P.op("pool", I("tensor_tensor", out=at[:], in0=qks[:], in1=dec[:], op=ALU.mult),
                         reads=[tqks, tdec], writes=[tat])
                    c["at"] = (at, tat)
                    yield
                kd, tkd = pbf.get()
                P.op("pool", I("tensor_scalar", out=kd[:], in0=kM[:, n, :], scalar1=sm[:, 2 + v:3 + v], scalar2=0.0,
                               op0=ALU.mult, op1=ALU.add), reads=[tkM, tsm], writes=[tkd])
                c["kd"] = (kd, tkd)
                yield
                ch.append(c)
            for c in ch:
                U, tU = c["M"]
                ut_ps, tut = pps.get()
                P.op("pe", I("matmul", ut_ps, lhsT=U[:].bitcast(F32R), rhs=identr, start=True, stop=True), reads=[tU, tidr], writes=[tut])
                Ut, tUt = pUt.get()
                P.op("act", I("activation", out=Ut[:].bitcast(F32R), in_=ut_ps, func=AF.Copy), reads=[tut], writes=[tUt])
                yield
                X, tX = p32r.get()
                Xt, tXt = p32r.get()
                x0, tx0 = p32.get()
                P.op("pool", I("tensor_tensor", out=x0[:], in0=U[:], in1=cs(NM0), op=ALU.mult), reads=[tU, tcst], writes=[tx0])
                P.op("dve", I("tensor_tensor", out=X[:].bitcast(F32R), in0=x0[:], in1=cs(IDENT), op=ALU.add), reads=[tx0, tcst], writes=[tX])
                yield
                x1, tx1 = p32.get()
                P.op("pool", I("tensor_tensor", out=x1[:], in0=Ut[:], in1=cs(NM0), op=ALU.mult), reads=[tUt, tcst], writes=[tx1])
                P.op("dve", I("tensor_tensor", out=Xt[:].bitcast(F32R), in0=x1[:], in1=cs(IDENT), op=ALU.add), reads=[tx1, tcst], writes=[tXt])
                yield
                c["U"], c["Ut"], c["X"], c["Xt"] = (U, tU), (Ut, tUt), (X, tX), (Xt, tXt)
            for lv in range(1, 7):
                last = lv == 6
                for c in ch:
                    Ut, tUt = c["Ut"]
                    X, tX = c["X"]
                    y_ps, ty = pps.get()
                    P.op("pe", I("matmul", y_ps, lhsT=Ut[:].bitcast(F32R), rhs=X[:].bitcast(F32R), start=True, stop=True),
                         reads=[tUt, tX], writes=[ty])
                    nY, tnY = p32r.get()
                    P.op("dve", I("tensor_tensor", out=nY[:].bitcast(F32R), in0=y_ps, in1=cs(NM0 + lv), op=ALU.mult),
                         reads=[ty, tcst], writes=[tnY])
                    c["nY"] = (nY, tnY)
                    yield
                for c in ch:
                    X, tX = c["X"]
                    Xt, tXt = c["Xt"]
                    nY, tnY = c["nY"]
                    x_ps, tx = pps.get()
                    P.op("pe", I("matmul", x_ps, lhsT=identr, rhs=X[:].bitcast(F32R), start=True, stop=False), reads=[tidr, tX], writes=[tx])
                    P.op("pe", I("matmul", x_ps, lhsT=Xt[:].bitcast(F32R), rhs=nY[:].bitcast(F32R), start=False, stop=True),
                         reads=[tXt, tnY], writes=[tx])
                    if not last:
                        xt_ps, txt = pps.get()
                        P.op("pe", I("matmul", xt_ps, lhsT=identr, rhs=Xt[:].bitcast(F32R), start=True, stop=False), reads=[tidr, tXt], writes=[txt])
                        P.op("pe", I("matmul", xt_ps, lhsT=nY[:].bitcast(F32R), rhs=Xt[:].bitcast(F32R), start=False, stop=True),
                             reads=[tnY, tXt], writes=[txt])
                    nX, tnX = pR.get() if last else p32r.get()
                    P.op("act", I("activation", out=nX[:].bitcast(F32R), in_=x_ps, func=AF.Copy), reads=[tx], writes=[tnX])
                    c["X"] = (nX, tnX)
                    if not last:
                        nXt, tnXt = p32r.get()
                        P.op("dve", I("tensor_copy", out=nXt[:].bitcast(F32R), in_=xt_ps), reads=[txt], writes=[tnXt])
                        c["Xt"] = (nXt, tnXt)
                    yield
            for c in ch:
                c["R"] = c["X"]
            res["ch"] = ch
            return res

        def seq(d, n, res):
            lat = n >= 2
            kc = kT[:, n * C:(n + 1) * C]
            qc = qT[:, n * C:(n + 1) * C]
            sm, tsm = res["sm"]
            for v in range(2):
                ci = 2 * d + v
                c = res["ch"][v]
                ps1, t1 = pps.get()
                P.op("pe", I("matmul", ps1, lhsT=kc, rhs=Sbf[ci][:], start=True, stop=True), reads=[tk, tSbf[ci]], writes=[t1])
                if lat:
                    ps2, t2 = pps.get()
                    P.op("pe", I("matmul", ps2, lhsT=qc, rhs=Sbf[ci][:], start=True, stop=True), reads=[tq, tSbf[ci]], writes=[t2])
                yield
                r, tr = p32r.get()
                P.op("dve", I("scalar_tensor_tensor", out=r[:].bitcast(F32R), in0=ps1, scalar=sm[:, 6 + v:7 + v], in1=vM[:, n, v, :],
                              op0=ALU.mult, op1=ALU.add), reads=[t1, tsm, tv], writes=[tr])
                yield
                R, tR = c["R"]
                ps3, t3 = pps.get()
                P.op("pe", I("matmul", ps3, lhsT=R[:].bitcast(F32R), rhs=r[:].bitcast(F32R), start=True, stop=True), reads=[tR, tr], writes=[t3])
                yield
                vn, tvn = pbf.get()
                P.op("act", I("activation", out=vn[:], in_=ps3, func=AF.Copy, scale=bM[:, n, ci:ci + 1]),
                     reads=[t3, tb], writes=[tvn])
                yield
                kd, tkd = c["kd"]
                ps5, t5 = pps.get()
                P.op("pe", I("matmul", ps5, lhsT=kd[:], rhs=vn[:], start=True, stop=True), reads=[tkd, tvn], writes=[t5])
                if lat:
                    at, tat = c["at"]
                    ps4, t4 = pps.get()
                    P.op("pe", I("matmul", ps4, lhsT=at[:], rhs=vn[:], start=True, stop=True), reads=[tat, tvn], writes=[t4])
                yield
                P.op("dve", I("scalar_tensor_tensor", out=S32[ci][:], in0=S32[ci][:], scalar=sm[:, 4 + v:5 + v], in1=ps5,
                              op0=ALU.mult, op1=ALU.add), reads=[tS32[ci], tsm, t5], writes=[tS32[ci]])
                P.op("act", I("activation", out=Sbf[ci][:], in_=S32[ci][:], func=AF.Copy), reads=[tS32[ci]], writes=[tSbf[ci]])
                yield
                if lat:
                    l = n - 2
                    ov = oacc[:, l, v, :]
                    if first_o[l][v]:
                        first_o[l][v] = False
                        P.op("dve", I("tensor_scalar", out=ov, in0=ps2, scalar1=sm[:, 8 + v:9 + v], scalar2=None, op0=ALU.mult),
                             reads=[t2, tsm], writes=[toacc[l][v]])
                    else:
                        P.op("dve", I("scalar_tensor_tensor", out=ov, in0=ps2, scalar=sm[:, 8 + v:9 + v], in1=ov,
                                      op0=ALU.mult, op1=ALU.add), reads=[t2, tsm, toacc[l][v]], writes=[toacc[l][v]])
                    P.op("dve", I("tensor_tensor", out=ov, in0=ov, in1=ps4, op=ALU.add),
                         reads=[toacc[l][v], t4], writes=[toacc[l][v]])
                    yield

        def drive(gens):
            rets = [None] * len(gens)
            live = list(range(len(gens)))
            while live:
                for gi in list(live):
                    try:
                        next(gens[gi])
                    except StopIteration as e:
                        rets[gi] = e.value
                        live.remove(gi)
            return rets

        cur = drive([pre(0, fo[0]), pre(1, bo[0])])
        for s in range(NCH):
            gens = [seq(0, fo[s], cur[0]), seq(1, bo[s], cur[1])]
            if s + 1 < NCH:
                gens += [pre(0, fo[s + 1]), pre(1, bo[s + 1])]
            r = drive(gens)
            if s + 1 < NCH:
                cur = r[2:4]

        jk, tjk = p32.get()
        for l in range(NLAT):
            for v in range(2):
                P.op("act", I("activation", out=jk[:], in_=oacc[:, l, v, :], func=AF.Square,
                              accum_out=ss[:, 2 * l + v:2 * l + v + 1]), reads=[toacc[l][v]], writes=[tjk, tss])
        P.op("dve", I("tensor_scalar", out=ss[:], in0=ss[:], scalar1=1.0 / C, scalar2=1e-6, op0=ALU.mult, op1=ALU.add),
             reads=[tss], writes=[tss])
        P.op("act", I("activation", out=ss[:], in_=ss[:], func=AF.Sqrt), reads=[tss], writes=[tss])
        P.op("dve", I("reciprocal", out=ss[:], in_=ss[:]), reads=[tss], writes=[tss])
        for l in range(NLAT):
            for v in range(2):
                gz, tgz = p32.get()
                P.op("pool", I("tensor_tensor", out=gz[:], in0=sz[:, l, v, :], in1=ng[:], op=ALU.mult),
                     reads=[tsz, tng], writes=[tgz])
                P.op("dve", I("scalar_tensor_tensor", out=yout[:, l, v, :], in0=oacc[:, l, v, :],
                              scalar=ss[:, 2 * l + v:2 * l + v + 1], in1=gz[:], op0=ALU.mult, op1=ALU.mult),
                     reads=[toacc[l][v], tss, tgz], writes=[tyout])
        out_ids.append(P.op("sp", I("dma_start", out=y_d[:, qh, :, :, :], in_=yout[:]), reads=[tyout], dsem=dsem[9]))
    P.op("sp", I("nop"), after=out_ids)
    return P


TL = 1028
TCX = 68
TT = TL + TCX
NOUT = 1088
NCHK = 97
TILES = [(0, 512), (512, 512), (1024, TT - 1024)]


def build_inproj(nc, uT_d, w_d, cw_d, gbv_d, qk_d, v_d, sz_d, gb_d, only=None):
    P = Prog(nc)
    uT = P.sb("uT", [128, KC, TT], BF16)
    cw = P.sb("cw", [128, 64, 5], F32)
    gbv = P.sb("gbv", [128, 4], F32)
    ones = P.sb("ones", [128, 128], F32)
    wb = [P.sb("wb%d" % i, [128, KC, 512], BF16) for i in range(2)]
    pre = [P.sb("pre%d" % i, [128, TT], F32) for i in range(2)]
    acc = [P.sb("acc%d" % i, [128, TT], F32) for i in range(2)]
    sil = [P.sb("sil%d" % i, [128, NOUT], F32) for i in range(2)]
    sq = [P.sb("sq%d" % i, [128, NOUT], F32) for i in range(2)]
    rs = [P.sb("rs%d" % i, [128, NOUT], F32) for i in range(2)]
    ob = [P.sb("ob%d" % i, [128, NOUT], BF16) for i in range(3)]
    gbo = P.sb("gbo", [128, NOUT], F32)
    tuT, tcw, tgbv, tones, tgbo = [T() for _ in range(5)]
    twb = [T() for _ in range(2)]
    tpre, tacc, tsil, tsq, trs = [[T() for _ in range(2)] for _ in range(5)]
    tob = [T() for _ in range(3)]
    psA = [P.ps("psA%d" % i, [128, 512]) for i in range(6)]
    tpsA = [T() for _ in range(6)]
    psS = [P.ps("psS%d" % i, [128, 512]) for i in range(2)]
    tpsS = [T() for _ in range(2)]
    sw = [P.dsem("sw%d" % i) for i in range(2)]
    sio = [P.dsem("sio%d" % i) for i in range(8)]
    so = [P.dsem("so%d" % i) for i in range(3)]
    P.op("pool", I("memset", ones[:], 1.0), writes=[tones])
    P.op("sp", I("dma_start", out=cw[:], in_=cw_d), writes=[tcw], dsem=sio[0])
    P.op("sp", I("dma_start", out=gbv[:, 0:2], in_=gbv_d), writes=[tgbv], dsem=sio[1])
    for k in range(KC):
        P.op("sp", I("dma_start", out=uT[:, k, :], in_=uT_d[:, k, :]), writes=[tuT], dsem=sio[2 + k % 4])
    P.op("act", I("activation", out=gbv[:, 2:3], in_=gbv[:, 1:2], func=AF.Exp), reads=[tgbv], writes=[tgbv])
    P.op("dve", I("tensor_scalar", out=gbv[:, 2:3], in0=gbv[:, 2:3], scalar1=-1.0, scalar2=None, op0=ALU.mult),
         reads=[tgbv], writes=[tgbv])
    ws = w_d.rearrange("(k p) f -> p k f", p=128)
    out_ids = []
    cnt = [0, 0, 0, 0]

    def nxt(i, m):
        cnt[i] += 1
        return cnt[i] % m

    ngrp = (NCHK + 3) // 4
    for gi in range(ngrp):
        c0 = gi * 4
        ncol = min(4, NCHK - c0) * 128
        s = gi % 2
        P.op("pool", I("dma_start", out=wb[s][:, :, 0:ncol], in_=ws[:, :, c0 * 128:c0 * 128 + ncol]),
             writes=[twb[s]], dsem=sw[s])
        for cj in range(ncol // 128):
            c = c0 + cj
            if only is not None and c not in only:
                continue
            pi = nxt(0, 2)
            for ti, (t0, n) in enumerate(TILES):
                a = nxt(1, 6)
                for k in range(KC):
                    P.op("pe", I("matmul", psA[a][:, 0:n], lhsT=wb[s][:, k, cj * 128:(cj + 1) * 128], rhs=uT[:, k, t0:t0 + n],
                                 start=(k == 0), stop=(k == KC - 1)), reads=[twb[s], tuT], writes=[tpsA[a]])
                P.op("act", I("activation", out=pre[pi][:, t0:t0 + n], in_=psA[a][:, 0:n], func=AF.Copy),
                     reads=[tpsA[a]], writes=[tpre[pi]])
            oi = nxt(2, 3)
            if c < 64:
                ai = pi
                for (o0, n) in ((2, 1024), (TL + 2, 64)):
                    for w in range(5):
                        src = pre[pi][:, o0 - 2 + w:o0 - 2 + w + n]
                        if w == 0:
                            P.op("dve", I("tensor_scalar", out=acc[ai][:, o0:o0 + n], in0=src, scalar1=cw[:, c, 0:1], scalar2=None,
                                          op0=ALU.mult), reads=[tpre[pi], tcw], writes=[tacc[ai]])
                        else:
                            P.op("dve", I("scalar_tensor_tensor", out=acc[ai][:, o0:o0 + n], in0=src, scalar=cw[:, c, w:w + 1],
                                          in1=acc[ai][:, o0:o0 + n], op0=ALU.mult, op1=ALU.add),
                                 reads=[tpre[pi], tcw, tacc[ai]], writes=[tacc[ai]])
                if c < 32:
                    P.op("act", I("activation", out=sil[ai][:, 0:1024], in_=acc[ai][:, 2:1026], func=AF.Silu),
                         reads=[tacc[ai]], writes=[tsil[ai]])
                    P.op("act", I("activation", out=sil[ai][:, 1024:1088], in_=acc[ai][:, TL + 2:TL + 66], func=AF.Silu),
                         reads=[tacc[ai]], writes=[tsil[ai]])
                    P.op("pool", I("tensor_tensor", out=sq[ai][:], in0=sil[ai][:], in1=sil[ai][:], op=ALU.mult),
                         reads=[tsil[ai]], writes=[tsq[ai]])
                    for (t0, n) in ((0, 512), (512, 512), (1024, 64)):
                        si = nxt(3, 2)
                        P.op("pe", I("matmul", psS[si][:, 0:n], lhsT=ones[:], rhs=sq[ai][:, t0:t0 + n], start=True, stop=True),
                             reads=[tones, tsq[ai]], writes=[tpsS[si]])
                        P.op("dve", I("tensor_scalar", out=rs[ai][:, t0:t0 + n], in0=psS[si][:, 0:n], scalar1=1e-6, scalar2=None,
                                      op0=ALU.add), reads=[tpsS[si]], writes=[trs[ai]])
                    P.op("act", I("activation", out=rs[ai][:], in_=rs[ai][:], func=AF.Sqrt), reads=[trs[ai]], writes=[trs[ai]])
                    P.op("dve", I("reciprocal", out=rs[ai][:], in_=rs[ai][:]), reads=[trs[ai]], writes=[trs[ai]])
                    P.op("pool", I("tensor_tensor", out=ob[oi][:], in0=sil[ai][:], in1=rs[ai][:], op=ALU.mult),
                         reads=[tsil[ai], trs[ai]], writes=[tob[oi]])
                    out_ids.append(P.op("sp", I("dma_start", out=qk_d[:, c, :], in_=ob[oi][:]), reads=[tob[oi]], dsem=so[oi]))
                else:
                    P.op("act", I("activation", out=ob[oi][:, 0:1024], in_=acc[ai][:, 2:1026], func=AF.Silu),
                         reads=[tacc[ai]], writes=[tob[oi]])
                    P.op("act", I("activation", out=ob[oi][:, 1024:1088], in_=acc[ai][:, TL + 2:TL + 66], func=AF.Silu),
                         reads=[tacc[ai]], writes=[tob[oi]])
                    out_ids.append(P.op("sp", I("dma_start", out=v_d[:, c - 32, :], in_=ob[oi][:]), reads=[tob[oi]], dsem=so[oi]))
            elif c < 96:
                P.op("act", I("activation", out=ob[oi][:, 0:1024], in_=pre[pi][:, 2:1026], func=AF.Silu),
                     reads=[tpre[pi]], writes=[tob[oi]])
                out_ids.append(P.op("sp", I("dma_start", out=sz_d[:, c - 64, :], in_=ob[oi][:, 0:1024]), reads=[tob[oi]], dsem=so[oi]))
            else:
                for (o0, i0, n) in ((0, 2, 1024), (1024, TL + 2, 64)):
                    P.op("act", I("activation", out=gbo[0:64, o0:o0 + n], in_=pre[pi][0:64, i0:i0 + n], func=AF.Sigmoid),
                         reads=[tpre[pi]], writes=[tgbo])
                    P.op("act", I("activation", out=gbo[64:128, o0:o0 + n], in_=pre[pi][64:128, i0:i0 + n], func=AF.Exp,
                                  bias=gbv[64:128, 0:1]), reads=[tpre[pi], tgbv], writes=[tgbo])
                    P.op("act", I("activation", out=gbo[64:128, o0:o0 + n], in_=gbo[64:128, o0:o0 + n], func=AF.Ln, bias=1.0),
                         reads=[tgbo], writes=[tgbo])
                    P.op("dve", I("tensor_scalar", out=gbo[64:128, o0:o0 + n], in0=gbo[64:128, o0:o0 + n],
                                  scalar1=gbv[64:128, 2:3], scalar2=None, op0=ALU.mult), reads=[tgbo, tgbv], writes=[tgbo])
                out_ids.append(P.op("sp", I("dma_start", out=gb_d, in_=gbo[:]), reads=[tgbo], dsem=sio[6]))
    P.op("sp", I("nop"), after=out_ids)
    return P


L = 4096
TB = 32
FG = 512
TW = 256
NTW = L // TW


def build_fourier(nc, uT_d, cc_d, cs_d, y_d):
    P = Prog(nc)
    uT = P.sb("uT", [128, 4, L], BF16)
    cc = P.sb("cc", [128, 2, 4, FG], BF16)
    AB = P.sb("AB", [128, 2, TB, FG], BF16)
    cs = [P.sb("cs%d" % i, [128, 2, TB, TW], BF16) for i in range(2)]
    yo = [P.sb("yo%d" % i, [128, TW], BF16) for i in range(4)]
    tuT, tcc = T(), T()
    tAB = [[T() for _ in range(TB)] for _ in range(2)]
    tcs = [T() for _ in range(2)]
    tyo = [T() for _ in range(4)]
    psA = [P.ps("psA%d" % i, [128, 512]) for i in range(4)]
    tpsA = [T() for _ in range(4)]
    psY = [P.ps("psY%d" % i, [128, 512]) for i in range(4)]
    tpsY = [T() for _ in range(4)]
    sio = [P.dsem("sio%d" % i) for i in range(6)]
    scs = [P.dsem("scs%d" % i) for i in range(2)]
    so = [P.dsem("so%d" % i) for i in range(4)]
    P.op("sp", I("dma_start", out=cc[:], in_=cc_d), writes=[tcc], dsem=sio[0])
    for k in range(4):
        P.op("sp", I("dma_start", out=uT[:, k, :], in_=uT_d[:, k, :]), writes=[tuT], dsem=sio[1 + k])
    n = 0
    for tb in range(TB):
        for j in range(2):
            a = n % 4
            n += 1
            for k in range(4):
                P.op("pe", I("matmul", psA[a][:], lhsT=uT[:, k, tb * 128:(tb + 1) * 128], rhs=cc[:, j, k, :],
                             start=(k == 0), stop=(k == 3)), reads=[tuT, tcc], writes=[tpsA[a]])
            if j == 0:
                P.op("act", I("activation", out=AB[:, j, tb, :], in_=psA[a][:], func=AF.Copy), reads=[tpsA[a]], writes=[tAB[j][tb]])
            else:
                P.op("dve", I("tensor_copy", out=AB[:, j, tb, :], in_=psA[a][:]), reads=[tpsA[a]], writes=[tAB[j][tb]])
    out_ids = []
    m = 0
    for tw in range(NTW):
        s = tw % 2
        P.op("pool", I("dma_start", out=cs[s][:], in_=cs_d[tw]), writes=[tcs[s]], dsem=scs[s])
        for c in range(4):
            a = m % 4
            m += 1
            i = 0
            for j in range(2):
                for tb in range(TB):
                    P.op("pe", I("matmul", psY[a][:, 0:TW], lhsT=AB[:, j, tb, c * 128:(c + 1) * 128], rhs=cs[s][:, j, tb, :],
                                 start=(i == 0), stop=(i == 2 * TB - 1)), reads=[tAB[j][tb], tcs[s]], writes=[tpsY[a]])
                    i += 1
            if a % 2 == 0:
                P.op("act", I("activation", out=yo[a][:], in_=psY[a][:, 0:TW], func=AF.Copy), reads=[tpsY[a]], writes=[tyo[a]])
            else:
                P.op("dve", I("tensor_copy", out=yo[a][:], in_=psY[a][:, 0:TW]), reads=[tpsY[a]], writes=[tyo[a]])
            out_ids.append(P.op("sp", I("dma_start", out=y_d[:, c, tw * TW:(tw + 1) * TW], in_=yo[a][:]),
                                reads=[tyo[a]], dsem=so[a]))
    P.op("sp", I("nop"), after=out_ids)
    return P


NMC = 18432 // 8
MT = [(0, 512), (512, 512), (1024, 512), (1536, 512), (2048, 256)]


def build_mod(nc, ct_d, w_d, b_d, m_d):
    P = Prog(nc)
    ct = P.sb("ct", [128, KC, 3], F32)
    sc = P.sb("sc", [128, KC, 3], F32)
    bs = P.sb("bs", [3, 2, NMC], F32)
    mo = P.sb("mo", [3, 2, NMC], F32)
    wt = [P.sb("wt%d" % i, [128, NMC], F32) for i in range(4)]
    tct, tsc, tbs, tmo = [T() for _ in range(4)]
    twt = [T() for _ in range(4)]
    ps = [P.ps("ps%d" % i, [128, 512]) for i in range(5)]
    tps = [T() for _ in range(5)]
    sio = [P.dsem("sio%d" % i) for i in range(3)]
    sw = [P.dsem("sw%d" % i) for i in range(4)]
    P.op("sp", I("dma_start", out=ct[:], in_=ct_d), writes=[tct], dsem=sio[0])
    P.op("sp", I("dma_start", out=bs[:], in_=b_d), writes=[tbs], dsem=sio[1])
    P.op("act", I("activation", out=sc[:], in_=ct[:], func=AF.Silu), reads=[tct], writes=[tsc])
    n = 0
    for l in range(2):
        for k in range(KC):
            s = n % 4
            n += 1
            P.op("sp", I("dma_start", out=wt[s][:], in_=w_d[l, k * 128:(k + 1) * 128, :]), writes=[twt[s]], dsem=sw[s])
            for i, (c0, w) in enumerate(MT):
                P.op("pe", I("matmul", ps[i][0:3, 0:w], lhsT=sc[:, k, :], rhs=wt[s][:, c0:c0 + w], start=(k == 0), stop=(k == KC - 1)),
                     reads=[tsc, twt[s]], writes=[tps[i]])
        for i, (c0, w) in enumerate(MT):
            P.op("dve", I("tensor_tensor", out=mo[:, l, c0:c0 + w], in0=ps[i][0:3, 0:w], in1=bs[:, l, c0:c0 + w], op=ALU.add),
                 reads=[tps[i], tbs], writes=[tmo])
    o = P.op("sp", I("dma_start", out=m_d, in_=mo[:]), reads=[tmo], dsem=sio[2])
    P.op("sp", I("nop"), after=[o])
    return P


import ml_dtypes
BFN = ml_dtypes.bfloat16
NCORES = 8
_cache = {}


def _dt(nc, n, s, t, k="ExternalInput"):
    return nc.dram_tensor(n, list(s), t, kind=k).ap()


def _finish(P):
    P.emit()
    P.close()


def prog_mod():
    nc = bass.Bass("TRN2", target_bir_lowering=False)
    ct = _dt(nc, "ct", [128, KC, 3], F32); w = _dt(nc, "w", [2, 2048, NMC], F32); b = _dt(nc, "b", [3, 2, NMC], F32)
    m = _dt(nc, "m", [3, 2, NMC], F32, "ExternalOutput")
    _finish(build_mod(nc, ct, w, b, m))
    return nc


def _ffn_w(nc, tag):
    return (_dt(nc, "wg" + tag, [D, DFF], F32), _dt(nc, "wu" + tag, [D, DFF], F32), _dt(nc, "wd" + tag, [DFF, D], F32))


def prog_l1():
    nc = bass.Bass("TRN2", target_bir_lowering=False)
    NV = 18
    hin = _dt(nc, "hin", [128, KC, 1088], F32); mods = _dt(nc, "mods", [128, NV, KC], F32)
    W = _ffn_w(nc, "0")
    hout = _dt(nc, "hout", [128, KC, 1088], F32, "ExternalOutput")
    uout = _dt(nc, "uout", [128, KC, 1088], BF16, "ExternalOutput")
    P = Prog(nc)
    S = TokStage(P, nc, 1024, 64, NV)
    S.load_h(hin, mods)
    S.mod_gw(0, 2, 7); S.mod_gw(0, 5, 8); S.mod_scale(3, 9, 0.5); S.mod_scale(6, 10, 0.5)
    S.mod_gw(11, 13, 16); S.mod_gw(11, 15, 17)
    S.ffn(*W, 7, 1, 9, 8, 4, 10)
    S.rms_stats()
    S.ada_norm(S.xn, S.txn, 16, 12, 17, 14)
    ids = S.store(hout, S.h, S.th) + S.store(uout, S.xn, S.txn)
    P.op("sp", I("nop"), after=ids)
    _finish(P)
    return nc


def prog_l3():
    nc = bass.Bass("TRN2", target_bir_lowering=False)
    NV = 17
    hin = _dt(nc, "hin", [128, KC, 1024], F32); mods = _dt(nc, "mods", [128, NV, KC], F32)
    yin = _dt(nc, "yin", [128, 32, 1024], BF16); wo = _dt(nc, "wo", [4096, D], F32)
    W0 = _ffn_w(nc, "0"); W1 = _ffn_w(nc, "1")
    hout = _dt(nc, "hout", [128, KC, 1024], F32, "ExternalOutput")
    uout = _dt(nc, "uout", [128, KC, 1024], BF16, "ExternalOutput")
    P = Prog(nc)
    S = TokStage(P, nc, 1024, 0, NV)
    S.load_h(hin, mods)
    S.mod_gw(1, 3, 5); S.mod_scale(4, 6, 0.5); S.mod_gw(7, 9, 11); S.mod_scale(10, 12, 0.5); S.mod_gw(13, 15, 16)
    S.mix(yin, 32, wo, 0)
    S.ffn(*W0, 5, 2, 6)
    S.ffn(*W1, 11, 8, 12)
    S.rms_stats()
    S.ada_norm(S.xn, S.txn, 16, 14)
    ids = S.store(hout, S.h, S.th) + S.store(uout, S.xn, S.txn)
    P.op("sp", I("nop"), after=ids)
    _finish(P)
    return nc


def prog_l5():
    nc = bass.Bass("TRN2", target_bir_lowering=False)
    NV = 8
    hin = _dt(nc, "hin", [128, KC, 1024], F32); mods = _dt(nc, "mods", [128, NV, KC], F32)
    yin = _dt(nc, "yin", [128, 16, 1024], BF16); wo = _dt(nc, "wo", [2048, D], F32)
    W0 = _ffn_w(nc, "0")
    hout = _dt(nc, "hout", [128, KC, 1024], F32, "ExternalOutput")
    P = Prog(nc)
    S = TokStage(P, nc, 1024, 0, NV)
    S.load_h(hin, mods)
    S.mod_gw(1, 3, 5); S.mod_scale(4, 6, 0.5)
    S.mix(yin, 16, wo, 0)
    S.ffn(*W0, 5, 2, 6)
    S.rms_stats()
    S.ada_norm(S.h, S.th, 7, None)
    ids = S.store(hout, S.h, S.th)
    P.op("sp", I("nop"), after=ids)
    _finish(P)
    return nc


def prog_inproj():
    nc = bass.Bass("TRN2", target_bir_lowering=False)
    uT = _dt(nc, "uT", [128, KC, TT], BF16); w = _dt(nc, "w", [2048, 12416], F32)
    cw = _dt(nc, "cw", [128, 64, 5], F32); gbv = _dt(nc, "gbv", [128, 2], F32)
    qk = _dt(nc, "qk", [128, 32, NOUT], BF16, "ExternalOutput"); v = _dt(nc, "v", [128, 32, NOUT], BF16, "ExternalOutput")
    sz = _dt(nc, "sz", [128, 32, 1024], BF16, "ExternalOutput"); gb = _dt(nc, "gb", [128, NOUT], F32, "ExternalOutput")
    _finish(build_inproj(nc, uT, w, cw, gbv, qk, v, sz, gb))
    return nc


def prog_scan():
    nc = bass.Bass("TRN2", target_bir_lowering=False)
    qT = _dt(nc, "qT", [128, 4, 4352], BF16); kT = _dt(nc, "kT", [128, 4, 4352], BF16)
    kM = _dt(nc, "kM", [128, 4, 34, 128], BF16); vM = _dt(nc, "vM", [128, 4, 34, 2, 128], BF16)
    sz = _dt(nc, "sz", [128, 4, 32, 2, 128], BF16)
    g = _dt(nc, "g", [128, 4, 34, 4], F32); b = _dt(nc, "b", [128, 4, 34, 4], F32)
    ng = _dt(nc, "ng", [128, 128], F32); cst = _dt(nc, "cst", [128, NCONST, 128], F32)
    y = _dt(nc, "y", [128, 4, 32, 2, 128], BF16, "ExternalOutput")
    _finish(build_scan(nc, qT, kT, kM, vM, sz, g, b, ng, cst, y, NQH=4))
    return nc


def prog_fourier():
    nc = bass.Bass("TRN2", target_bir_lowering=False)
    uT = _dt(nc, "uT", [128, 4, L], BF16); cc = _dt(nc, "cc", [128, 2, 4, FG], BF16); cs = _dt(nc, "cs", [16, 128, 2, 32, 256], BF16)
    y = _dt(nc, "y", [128, 4, L], BF16, "ExternalOutput")
    _finish(build_fourier(nc, uT, cc, cs, y))
    return nc


def fm(a):
    t, d = a.shape
    return np.ascontiguousarray(a.reshape(t, d // 128, 128).transpose(2, 1, 0))


def tm(a):
    p, n, t = a.shape
    return np.ascontiguousarray(a.transpose(2, 1, 0)).reshape(t, n * 128)


def vec(v):
    return v.reshape(KC, 128).T


def scan_consts():
    m = np.arange(128)[:, None]; i = np.arange(128)[None, :]
    c = np.zeros((128, NCONST, 128), np.float32)
    c[:, LE] = m <= i; c[:, GE] = m >= i; c[:, GT] = m > i; c[:, LT] = m < i
    c[:, ONES] = 1; c[:, IDENT] = m == i
    c[:, NEGF] = np.where(i < m, NEG, 0); c[:, NEGB] = np.where(i > m, NEG, 0)
    for lv in range(7):
        c[:, NM0 + lv] = -(((m >> (lv + 1)) == (i >> (lv + 1))) & ((m >> lv) != (i >> lv))).astype(np.float32)
    return c


def fourier_consts():
    c = np.arange(512)
    ang = 2 * np.pi * ((c[:, None] * c[None, :]) % 512) / 512
    CC = np.stack([np.cos(ang), np.sin(ang)], 0) / np.sqrt(512.0)
    cc = np.ascontiguousarray(CC.reshape(2, 4, 128, 512).transpose(2, 0, 1, 3)).astype(BFN)
    t = np.arange(4096, dtype=np.int64)
    ang = 2 * np.pi * ((t[:, None] * t[None, :]) % 4096) / 4096
    tab = np.stack([np.cos(ang), -np.sin(ang)], 0).astype(np.float32) / 64.0
    cs = np.ascontiguousarray(tab.reshape(2, 32, 128, 16, 256).transpose(3, 2, 0, 1, 4)).astype(BFN)
    return cc, cs


def halo(a, lo, hi):
    n = a.shape[0]
    out = np.zeros((hi - lo,) + a.shape[1:], a.dtype)
    s, e = max(lo, 0), min(hi, n)
    out[s - lo:e - lo] = a[s:e]
    return out


def scan_inputs(q, k, v, sz, g, beta, hg):
    qs = q[:, 4 * hg:4 * hg + 4]; ks = k[:, 4 * hg:4 * hg + 4]
    vs = v[:, 8 * hg:8 * hg + 8].reshape(34, 128, 4, 2, 128)
    szs = sz[:, 8 * hg:8 * hg + 8].reshape(32, 128, 4, 2, 128)
    gs = g[:, :, 8 * hg:8 * hg + 8].reshape(34, 128, 2, 4, 2)
    bs = beta[:, :, 8 * hg:8 * hg + 8].reshape(34, 128, 2, 4, 2)
    return {
        "qT": np.ascontiguousarray(qs.transpose(2, 1, 0)),
        "kT": np.ascontiguousarray(ks.transpose(2, 1, 0)),
        "kM": np.ascontiguousarray(ks.reshape(34, 128, 4, 128).transpose(1, 2, 0, 3)),
        "vM": np.ascontiguousarray(vs.transpose(1, 2, 0, 3, 4)),
        "sz": np.ascontiguousarray(szs.transpose(1, 2, 0, 3, 4)),
        "g": np.ascontiguousarray(gs.transpose(1, 3, 0, 2, 4)).reshape(128, 4, 34, 4),
        "b": np.ascontiguousarray(bs.transpose(1, 3, 0, 2, 4)).reshape(128, 4, 34, 4),
    }


def _run(name, builder, in_maps):
    if name not in _cache:
        _cache[name] = builder()
    res = run_bass_kernel_spmd(_cache[name], in_maps, core_ids=list(range(NCORES)))
    return res.results


def kernel(x, c, ctx, c_ctx, norm_g, mod_w, mod_b, ffn_w_gate, ffn_w_up, ffn_w_down, dn_w_in, dn_conv_w,
           dn_a_log, dn_dt_bias, dn_norm_g, dn_w_out, fn_w_out, final_norm_g):
    f32 = np.float32
    A = lambda a: np.asarray(a, dtype=f32)
    x, c, ctx, c_ctx, norm_g, mod_b = A(x), A(c), A(ctx), A(c_ctx), A(norm_g), A(mod_b)
    mod_w = np.asarray(mod_w); ffn_w_gate = np.asarray(ffn_w_gate); ffn_w_up = np.asarray(ffn_w_up)
    ffn_w_down = np.asarray(ffn_w_down)
    cores = range(NCORES)
    cond = np.stack([c[0], c[1], c_ctx], 0)
    ct = np.ascontiguousarray(cond.reshape(3, KC, 128).transpose(2, 1, 0))
    ims = []
    for ci in cores:
        sl = slice(ci * NMC, (ci + 1) * NMC)
        ims.append({"ct": ct, "w": np.ascontiguousarray(mod_w[:, :, sl]),
                    "b": np.ascontiguousarray(np.broadcast_to(mod_b[None, :, sl], (3, 2, NMC)))})
    r = _run("mod", prog_mod, ims)
    m = np.concatenate([r[ci]["m"] for ci in cores], 2)
    M = lambda l, row, j: m[row, l, j * 2048:(j + 1) * 2048]

    ims = []
    for ci in cores:
        b, q = ci // 4, ci % 4
        a = np.concatenate([x[b, q * 1024:(q + 1) * 1024], ctx[b, q * 64:(q + 1) * 64]], 0)
        md = np.zeros((128, 18, KC), f32)
        md[:, 0] = vec(norm_g[0, 0]); md[:, 1] = vec(M(0, b, 0)); md[:, 2] = vec(M(0, b, 1)); md[:, 3] = vec(M(0, b, 2))
        md[:, 4] = vec(M(0, 2, 0)); md[:, 5] = vec(M(0, 2, 1)); md[:, 6] = vec(M(0, 2, 2))
        md[:, 11] = vec(norm_g[0, 1]); md[:, 12] = vec(M(0, b, 3)); md[:, 13] = vec(M(0, b, 4))
        md[:, 14] = vec(M(0, 2, 3)); md[:, 15] = vec(M(0, 2, 4))
        ims.append({"hin": fm(a), "mods": md, "wg0": ffn_w_gate[0, 0], "wu0": ffn_w_up[0, 0], "wd0": ffn_w_down[0, 0]})
    r = _run("l1", prog_l1, ims)
    h_fm = [r[ci]["hout"][:, :, 0:1024] for ci in cores]
    u_lat = [np.concatenate([tm(r[b * 4 + q]["uout"][:, :, 0:1024]) for q in range(4)], 0) for b in range(2)]
    u_ctx = [np.concatenate([tm(r[b * 4 + q]["uout"][:, :, 1024:1088]) for q in range(4)], 0) for b in range(2)]

    cw = np.ascontiguousarray(A(dn_conv_w)[0].reshape(5, 64, 128).transpose(2, 1, 0))
    gbv = np.zeros((128, 2), f32)
    gbv[64:, 0] = A(dn_dt_bias)[0].reshape(64); gbv[64:, 1] = A(dn_a_log)[0].reshape(64)
    w_in = np.asarray(dn_w_in)[0]
    ims = []
    for ci in cores:
        b, q = ci // 4, ci % 4
        a = np.concatenate([halo(u_lat[b], q * 1024 - 2, q * 1024 + 1026), halo(u_ctx[b], q * 64 - 2, q * 64 + 66)], 0)
        ims.append({"uT": fm(a), "w": w_in, "cw": cw, "gbv": gbv})
    r = _run("inproj", prog_inproj, ims)
    cst = scan_consts()
    ng = np.ascontiguousarray(np.broadcast_to(A(dn_norm_g)[0][None, :], (128, 128)))
    ims = []
    for b in range(2):
        def gather(key, lo, hi):
            return np.concatenate([r[b * 4 + q][key][..., lo:hi] for q in range(4)], -1)
        qk = np.concatenate([gather("qk", 1024, 1088), gather("qk", 0, 1024)], -1)
        vv = np.concatenate([gather("v", 1024, 1088), gather("v", 0, 1024)], -1)
        szz = gather("sz", 0, 1024)
        gb = np.concatenate([gather("gb", 1024, 1088), gather("gb", 0, 1024)], -1)
        q_tm = qk[:, 0:16].transpose(2, 1, 0)
        k_tm = qk[:, 16:32].transpose(2, 1, 0)
        v_tm = vv.transpose(2, 1, 0)
        sz_tm = szz.transpose(2, 1, 0)
        beta_tm = gb[0:64].T.reshape(4352, 2, 32)
        g_tm = gb[64:128].T.reshape(4352, 2, 32)
        for hg in range(4):
            d = scan_inputs(q_tm, k_tm, v_tm, sz_tm, g_tm, beta_tm, hg)
            d["ng"] = ng; d["cst"] = cst
            ims.append(d)
    r = _run("scan", prog_scan, ims)
    ypre = []
    for b in range(2):
        yb = np.stack([r[b * 4 + hg]["y"] for hg in range(4)], 0)
        ypre.append(np.ascontiguousarray(yb.transpose(3, 1, 0, 2, 4, 5)).reshape(4096, 4096))

    ims = []
    for ci in cores:
        b, q = ci // 4, ci % 4
        md = np.zeros((128, 17, KC), f32)
        md[:, 0] = vec(M(0, b, 5))
        md[:, 1] = vec(norm_g[0, 2]); md[:, 2] = vec(M(0, b, 6)); md[:, 3] = vec(M(0, b, 7)); md[:, 4] = vec(M(0, b, 8))
        md[:, 7] = vec(norm_g[1, 0]); md[:, 8] = vec(M(1, b, 0)); md[:, 9] = vec(M(1, b, 1)); md[:, 10] = vec(M(1, b, 2))
        md[:, 13] = vec(norm_g[1, 1]); md[:, 14] = vec(M(1, b, 3)); md[:, 15] = vec(M(1, b, 4))
        ims.append({"hin": np.ascontiguousarray(h_fm[ci]), "mods": md, "yin": fm(ypre[b][q * 1024:(q + 1) * 1024]),
                    "wo": np.asarray(dn_w_out)[0],
                    "wg0": ffn_w_gate[0, 1], "wu0": ffn_w_up[0, 1], "wd0": ffn_w_down[0, 1],
                    "wg1": ffn_w_gate[1, 0], "wu1": ffn_w_up[1, 0], "wd1": ffn_w_down[1, 0]})
    r = _run("l3", prog_l3, ims)
    h_fm = [r[ci]["hout"] for ci in cores]
    u1 = [np.concatenate([tm(r[b * 4 + q]["uout"]) for q in range(4)], 0) for b in range(2)]

    cc, cs = fourier_consts()
    ims = []
    for ci in cores:
        b, g = ci // 4, ci % 4
        ims.append({"uT": fm(u1[b][:, g * 512:(g + 1) * 512]), "cc": cc, "cs": cs})
    r = _run("fourier", prog_fourier, ims)
    yf = [np.concatenate([tm(r[b * 4 + g]["y"]) for g in range(4)], 1) for b in range(2)]

    ims = []
    for ci in cores:
        b, q = ci // 4, ci % 4
        md = np.zeros((128, 8, KC), f32)
        md[:, 0] = vec(M(1, b, 5))
        md[:, 1] = vec(norm_g[1, 2]); md[:, 2] = vec(M(1, b, 6)); md[:, 3] = vec(M(1, b, 7)); md[:, 4] = vec(M(1, b, 8))
        md[:, 7] = vec(A(final_norm_g))
        ims.append({"hin": np.ascontiguousarray(h_fm[ci]), "mods": md, "yin": fm(yf[b][q * 1024:(q + 1) * 1024]),
                    "wo": np.asarray(fn_w_out)[0],
                    "wg0": ffn_w_gate[1, 1], "wu0": ffn_w_up[1, 1], "wd0": ffn_w_down[1, 1]})
    r = _run("l5", prog_l5, ims)
    out = np.zeros((2, 4096, 2048), f32)
    for ci in cores:
        b, q = ci // 4, ci % 4
        out[b, q * 1024:(q + 1) * 1024] = tm(r[ci]["hout"])
    return out
```

```python
import numpy as np
import concourse.bass as bass
import concourse.mybir as mybir
from concourse.bass_utils import run_bass_kernel_spmd

F32 = mybir.dt.float32
BF16 = mybir.dt.bfloat16
AF = mybir.ActivationFunctionType
ALU = mybir.AluOpType
AX = mybir.AxisListType


class T:
    __slots__ = ("name", "w", "rd")

    def __init__(self, name=""):
        self.name = name
        self.w = None
        self.rd = []


class DSem:
    __slots__ = ("h", "count", "last")

    def __init__(self, h):
        self.h = h
        self.count = 0
        self.last = None


class Prog:
    ENGS = ("pe", "act", "dve", "pool", "sp")

    def __init__(self, nc):
        self.nc = nc
        self.ins = []
        self.es = None
        self._ctx = []

    def enter(self, cm):
        v = cm.__enter__()
        self._ctx.append(cm)
        return v

    def sb(self, name, shape, dt):
        self.nalloc = getattr(self, "nalloc", 0) + 1
        cm = self.nc.sbuf_tensor("sb%d_%s" % (self.nalloc, name), list(shape), dt)
        v = cm.__enter__()
        self._stage = getattr(self, "_stage", [])
        self._stage.append(cm)
        return v

    def ps(self, name, shape, dt=F32):
        self.nalloc = getattr(self, "nalloc", 0) + 1
        cm = self.nc.psum_tensor("ps%d_%s" % (self.nalloc, name), list(shape), dt)
        v = cm.__enter__()
        self._stage = getattr(self, "_stage", [])
        self._stage.append(cm)
        return v

    def dsem(self, name):
        if not hasattr(self, "sems"):
            self.sems = []
            self.soff = 0
        if self.soff == len(self.sems):
            self.sems.append(DSem(self.enter(self.nc.semaphore("ds%d" % len(self.sems)))))
        self.soff += 1
        return self.sems[self.soff - 1]

    def barrier(self):
        last = {}
        for i, it in enumerate(self.ins):
            last[it["eng"]] = i
        deps = list(last.values()) + [d.last for d in getattr(self, "sems", []) if d.last is not None]
        for e in self.ENGS:
            self.op(e, lambda eng: eng.nop(), after=deps)
        st = getattr(self, "_stage", [])
        while st:
            st.pop().__exit__(None, None, None)
        self.soff = 0

    def close(self):
        st = getattr(self, "_stage", [])
        while st:
            st.pop().__exit__(None, None, None)
        while self._ctx:
            self._ctx.pop().__exit__(None, None, None)

    def op(self, eng, fn, reads=(), writes=(), dsem=None, after=(), inc=16):
        idx = len(self.ins)
        isdma = dsem is not None
        deps = set(after)
        if isdma and dsem.last is not None:
            deps.add(dsem.last)
        for t in reads:
            if t.w is not None:
                deps.add(t.w)
        for t in writes:
            if t.w is not None:
                deps.add(t.w)
            deps.update(t.rd)
        raw = set(t.w for t in reads if t.w is not None)
        keep = []
        for d in deps:
            di = self.ins[d]
            if (not di["dma"]) and (not isdma) and di["eng"] == eng and d not in raw:
                continue
            keep.append(d)
            di["sig"] = True
        ev = None
        if isdma:
            dsem.count += inc
            ev = (dsem.h, dsem.count)
        self.ins.append(dict(eng=eng, fn=fn, deps=keep, dma=isdma, sig=isdma, ev=ev, inc=inc))
        if isdma:
            dsem.last = idx
        for t in reads:
            t.rd.append(idx)
        for t in writes:
            t.w = idx
            t.rd = []
        return idx

    def emit(self):
        nc = self.nc
        es = {e: self.enter(nc.semaphore("es_" + e)) for e in self.ENGS}
        cnt = {e: 0 for e in self.ENGS}
        for it in self.ins:
            if not it["dma"] and it["sig"]:
                cnt[it["eng"]] += 1
                it["ev"] = (es[it["eng"]], cnt[it["eng"]])
        per = {e: [it for it in self.ins if it["eng"] == e] for e in self.ENGS}
        ins = self.ins

        def body(e):
            def f(eng):
                waited = {}
                for it in per[e]:
                    need = {}
                    for d in it["deps"]:
                        s, v = ins[d]["ev"]
                        k = id(s)
                        if waited.get(k, 0) < v and need.get(k, (None, 0))[1] < v:
                            need[k] = (s, v)
                    for k, (s, v) in need.items():
                        eng.wait_ge(s, v)
                        waited[k] = v
                    r = it["fn"](eng)
                    if it["sig"]:
                        s, v = it["ev"]
                        r.then_inc(s, it["inc"] if it["dma"] else 1)
            return f

        with nc.Block() as block:
            block.tensor(body("pe"))
            block.scalar(body("act"))
            block.vector(body("dve"))
            block.gpsimd(body("pool"))
            block.sync(body("sp"))


D = 2048
KC = 16
DFF = 5632
FC = 44
EPS = 1e-6
G = 2


def I(meth, *a, **kw):
    return lambda e: getattr(e, meth)(*a, **kw)


def ttiles(Tn):
    out = []
    t = 0
    while t < Tn:
        n = min(512, Tn - t)
        out.append((t, n))
        t += n
    return out


class TokStage:
    def __init__(self, P, nc, Tlat, Tctx, NV):
        self.P, self.nc = P, nc
        self.Tlat, self.Tctx = Tlat, Tctx
        self.T = Tlat + Tctx
        self.tiles = ttiles(Tlat) + ([(Tlat, Tctx)] if Tctx else [])
        self.nt = len(self.tiles)
        T_ = self.T
        self.h = P.sb("h", [128, KC, T_], F32)
        self.xn = P.sb("xn", [128, KC, T_], BF16)
        self.mods = P.sb("mods", [128, NV, KC], F32)
        self.rstd = P.sb("rstd", [128, T_], F32)
        self.ones = P.sb("ones", [128, 128], F32)
        self.sq = [P.sb("sq%d" % i, [128, 512], F32) for i in range(2)]
        self.tmp = [P.sb("tmp%d" % i, [128, 512], F32) for i in range(2)]
        self.sg = [P.sb("sg%d" % i, [128, 512], BF16) for i in range(2)]
        self.act = [P.sb("act%d" % i, [128, G, T_], BF16) for i in range(2)]
        self.wg = [P.sb("wg%d" % i, [128, KC, G * 128], BF16) for i in range(2)]
        self.wu = [P.sb("wu%d" % i, [128, KC, G * 128], BF16) for i in range(2)]
        self.wd = [P.sb("wd%d" % i, [128, G, D], BF16) for i in range(2)]
        self.psA = [P.ps("psA%d" % i, [128, 512]) for i in range(2)]
        self.psB = [P.ps("psB%d" % i, [128, 512]) for i in range(2)]
        self.psC = [P.ps("psC%d" % i, [128, 512]) for i in range(2)]
        self.psS = [P.ps("psS%d" % i, [128, 512]) for i in range(2)]
        self.th = [[T("h") for _ in range(self.nt)] for _ in range(KC)]
        self.txn = [[T("xn") for _ in range(self.nt)] for _ in range(KC)]
        self.tmods = T("mods")
        self.trstd = [T("rstd") for _ in range(self.nt)]
        self.tones = T("ones")
        self.tsq = [T("sq") for _ in range(2)]
        self.ttmp = [T("tmp") for _ in range(2)]
        self.tsg = [T("sg") for _ in range(2)]
        self.tact = [[[T("act") for _ in range(self.nt)] for _ in range(G)] for _ in range(2)]
        self.twg = [T("wg") for _ in range(2)]
        self.twu = [T("wu") for _ in range(2)]
        self.twd = [T("wd") for _ in range(2)]
        self.tpsA = [T("psA") for _ in range(2)]
        self.tpsB = [T("psB") for _ in range(2)]
        self.tpsC = [T("psC") for _ in range(2)]
        self.tpsS = [T("psS") for _ in range(2)]
        self.swg = [P.dsem("swg%d" % i) for i in range(2)]
        self.swu = [P.dsem("swu%d" % i) for i in range(2)]
        self.swd = [P.dsem("swd%d" % i) for i in range(2)]
        self.sio = [P.dsem("sio%d" % i) for i in range(9)]
        self.cnt = 0
        self.grp = 0
        P.op("pool", I("memset", self.ones[:], 1.0), writes=[self.tones])

    def rr(self):
        self.cnt += 1
        return self.cnt % 2

    def load_h(self, h_dram, mods_dram):
        P = self.P
        P.op("sp", I("dma_start", out=self.mods[:], in_=mods_dram), writes=[self.tmods], dsem=self.sio[0])
        for k in range(KC):
            P.op("sp", I("dma_start", out=self.h[:, k, :], in_=h_dram[:, k, :]),
                 writes=self.th[k], dsem=self.sio[1 + k % 8])

    def store(self, out_dram, sb, tl):
        P = self.P
        ids = []
        for k in range(KC):
            ids.append(P.op("sp", I("dma_start", out=out_dram[:, k, :], in_=sb[:, k, :]),
                            reads=tl[k], dsem=self.sio[1 + k % 8]))
        return ids

    def mod_gw(self, vg, vscale, vout):
        m = self.mods
        self.P.op("dve", I("scalar_tensor_tensor", out=m[:, vout, :], in0=m[:, vscale, :], scalar=1.0,
                           in1=m[:, vg, :], op0=ALU.add, op1=ALU.mult),
                  reads=[self.tmods], writes=[self.tmods])

    def mod_scale(self, vin, vout, c):
        m = self.mods
        self.P.op("dve", I("tensor_scalar", out=m[:, vout, :], in0=m[:, vin, :], scalar1=c, scalar2=None, op0=ALU.mult),
                  reads=[self.tmods], writes=[self.tmods])

    def rms_stats(self):
        P = self.P
        for ti, (t0, n) in enumerate(self.tiles):
            s = self.rr()
            for k in range(KC):
                q = self.rr()
                P.op("act", I("activation", out=self.sq[q][:, 0:n], in_=self.h[:, k, t0:t0 + n], func=AF.Square),
                     reads=[self.th[k][ti]], writes=[self.tsq[q]])
                P.op("pe", I("matmul", self.psS[s][:, 0:n], lhsT=self.ones[:], rhs=self.sq[q][:, 0:n],
                             start=(k == 0), stop=(k == KC - 1)),
                     reads=[self.tsq[q], self.tones], writes=[self.tpsS[s]])
            q = self.rr()
            P.op("dve", I("tensor_scalar", out=self.tmp[q][:, 0:n], in0=self.psS[s][:, 0:n],
                          scalar1=1.0 / D, scalar2=EPS, op0=ALU.mult, op1=ALU.add),
                 reads=[self.tpsS[s]], writes=[self.ttmp[q]])
            P.op("act", I("activation", out=self.tmp[q][:, 0:n], in_=self.tmp[q][:, 0:n], func=AF.Sqrt),
                 reads=[self.ttmp[q]], writes=[self.ttmp[q]])
            P.op("dve", I("reciprocal", out=self.rstd[:, t0:t0 + n], in_=self.tmp[q][:, 0:n]),
                 reads=[self.ttmp[q]], writes=[self.trstd[ti]])

    def ada_norm(self, out_sb, out_tiles, vgw, vshift, vgw_c=None, vshift_c=None):
        P = self.P
        m = self.mods
        for ti, (t0, n) in enumerate(self.tiles):
            isctx = self.Tctx and ti == self.nt - 1
            g_ = vgw_c if isctx else vgw
            s_ = vshift_c if isctx else vshift
            for k in range(KC):
                q = self.rr()
                P.op("dve", I("scalar_tensor_tensor", out=self.tmp[q][:, 0:n], in0=self.h[:, k, t0:t0 + n],
                              scalar=m[:, g_, k:k + 1], in1=self.rstd[:, t0:t0 + n], op0=ALU.mult, op1=ALU.mult),
                     reads=[self.th[k][ti], self.trstd[ti], self.tmods], writes=[self.ttmp[q]])
                if s_ is None:
                    P.op("act", I("activation", out=out_sb[:, k, t0:t0 + n], in_=self.tmp[q][:, 0:n], func=AF.Copy),
                         reads=[self.ttmp[q]], writes=[out_tiles[k][ti]])
                else:
                    P.op("act", I("activation", out=out_sb[:, k, t0:t0 + n], in_=self.tmp[q][:, 0:n],
                                  func=AF.Identity, bias=m[:, s_, k:k + 1]),
                         reads=[self.ttmp[q], self.tmods], writes=[out_tiles[k][ti]])

    def load_w(self, wg, wu, wd, g):
        P = self.P
        s = self.grp % 2
        f0 = g * G * 128
        wgs = wg.rearrange("(k p) f -> p k f", p=128)
        wus = wu.rearrange("(k p) f -> p k f", p=128)
        wds = wd.rearrange("(j p) d -> p j d", p=128)
        P.op("pool", I("dma_start", out=self.wg[s][:, :, :], in_=wgs[:, :, f0:f0 + G * 128]),
             writes=[self.twg[s]], dsem=self.swg[s])
        P.op("pool", I("dma_start", out=self.wu[s][:, :, :], in_=wus[:, :, f0:f0 + G * 128]),
             writes=[self.twu[s]], dsem=self.swu[s])
        P.op("pool", I("dma_start", out=self.wd[s][:, :, :], in_=wds[:, g * G:(g + 1) * G, :]),
             writes=[self.twd[s]], dsem=self.swd[s])
        self.grp += 1
        return s

    def ffn_p1(self, s):
        P = self.P
        for j in range(G):
            for ti, (t0, n) in enumerate(self.tiles):
                a = self.rr()
                for k in range(KC):
                    P.op("pe", I("matmul", self.psA[a][:, 0:n], lhsT=self.wg[s][:, k, j * 128:(j + 1) * 128],
                                 rhs=self.xn[:, k, t0:t0 + n], start=(k == 0), stop=(k == KC - 1)),
                         reads=[self.twg[s], self.txn[k][ti]], writes=[self.tpsA[a]])
                for k in range(KC):
                    P.op("pe", I("matmul", self.psB[a][:, 0:n], lhsT=self.wu[s][:, k, j * 128:(j + 1) * 128],
                                 rhs=self.xn[:, k, t0:t0 + n], start=(k == 0), stop=(k == KC - 1)),
                         reads=[self.twu[s], self.txn[k][ti]], writes=[self.tpsB[a]])
                P.op("act", I("activation", out=self.sg[a][:, 0:n], in_=self.psA[a][:, 0:n], func=AF.Silu),
                     reads=[self.tpsA[a]], writes=[self.tsg[a]])
                P.op("dve", I("tensor_tensor", out=self.act[s][:, j, t0:t0 + n], in0=self.sg[a][:, 0:n],
                              in1=self.psB[a][:, 0:n], op=ALU.mult),
                     reads=[self.tsg[a], self.tpsB[a]], writes=[self.tact[s][j][ti]])
                yield

    def ffn_p2(self, s, vgate, vgate_c):
        P = self.P
        m = self.mods
        for d in range(KC):
            for ti, (t0, n) in enumerate(self.tiles):
                isctx = self.Tctx and ti == self.nt - 1
                gv = vgate_c if isctx else vgate
                self.c4 = (getattr(self, "c4", 0) + 1) % 4
                pc, tpc = ((self.psC[0], self.tpsC[0]), (self.psC[1], self.tpsC[1]),
                           (self.psS[0], self.tpsS[0]), (self.psS[1], self.tpsS[1]))[self.c4]
                for j in range(G):
                    P.op("pe", I("matmul", pc[:, 0:n], lhsT=self.wd[s][:, j, d * 128:(d + 1) * 128],
                                 rhs=self.act[s][:, j, t0:t0 + n], start=(j == 0), stop=(j == G - 1)),
                         reads=[self.twd[s], self.tact[s][j][ti]], writes=[tpc])
                P.op("dve", I("scalar_tensor_tensor", out=self.h[:, d, t0:t0 + n], in0=pc[:, 0:n],
                              scalar=m[:, gv, d:d + 1], in1=self.h[:, d, t0:t0 + n], op0=ALU.mult, op1=ALU.add),
                     reads=[tpc, self.th[d][ti], self.tmods], writes=[self.th[d][ti]])
                yield

    def ffn(self, wg, wu, wd, vgw, vshift, vgate, vgw_c=None, vshift_c=None, vgate_c=None):
        self.rms_stats()
        self.ada_norm(self.xn, self.txn, vgw, vshift, vgw_c, vshift_c)
        NG = FC // G
        prev = None
        per = (KC * self.nt + G * self.nt - 1) // (G * self.nt)
        for g in range(NG):
            s = self.load_w(wg, wu, wd, g)
            g1 = self.ffn_p1(s)
            g2 = self.ffn_p2(prev, vgate, vgate_c) if prev is not None else iter(())
            for _ in g1:
                for _ in range(per):
                    next(g2, None)
            for _ in g2:
                pass
            prev = s
        for _ in self.ffn_p2(prev, vgate, vgate_c):
            pass

    def mix(self, y_d, nfc, w_d, vgate):
        P = self.P
        m = self.mods
        ws = w_d.rearrange("(j p) d -> p j d", p=128)
        bufs = [(self.wg[0], self.twg[0], self.swg[0]), (self.wu[0], self.twu[0], self.swu[0]),
                (self.wg[1], self.twg[1], self.swg[1]), (self.wu[1], self.twu[1], self.swu[1])]
        lat_tiles = [(ti, t0, n) for ti, (t0, n) in enumerate(self.tiles) if t0 < self.Tlat]
        nb = 0
        for half in range(nfc // KC):
            for k in range(KC):
                P.op("sp", I("dma_start", out=self.xn[:, k, 0:self.Tlat], in_=(y_d(half * KC + k) if callable(y_d) else y_d[:, half * KC + k, :])),
                     writes=[self.txn[k][ti] for ti, _, _ in lat_tiles], dsem=self.sio[1 + k % 8])
            for blk in range(D // (G * 128)):
                buf, tb, sb_ = bufs[nb % 4]
                nb += 1
                P.op("pool", I("dma_start", out=buf[:, :, :], in_=ws[:, half * KC:(half + 1) * KC, blk * G * 128:(blk + 1) * G * 128]),
                     writes=[tb], dsem=sb_)
                for dj in range(G):
                    d = blk * G + dj
                    for ti, t0, n in lat_tiles:
                        c = self.rr()
                        for k in range(KC):
                            P.op("pe", I("matmul", self.psC[c][:, 0:n], lhsT=buf[:, k, dj * 128:(dj + 1) * 128],
                                         rhs=self.xn[:, k, t0:t0 + n], start=(k == 0), stop=(k == KC - 1)),
                                 reads=[tb, self.txn[k][ti]], writes=[self.tpsC[c]])
                        P.op("dve", I("scalar_tensor_tensor", out=self.h[:, d, t0:t0 + n], in0=self.psC[c][:, 0:n],
                                      scalar=m[:, vgate, d:d + 1], in1=self.h[:, d, t0:t0 + n], op0=ALU.mult, op1=ALU.add),
                             reads=[self.tpsC[c], self.th[d][ti], self.tmods], writes=[self.th[d][ti]])


F32R = mybir.dt.float32r
C = 128
NCH = 34
NLAT = 32
DKS = 128 ** -0.5
NEG = -30000.0
LE, GE, GT, LT, ONES, IDENT, NEGF, NEGB, NM0 = range(9)
NCONST = 15
NSTAGE = 6


class RR:
    def __init__(self, items):
        self.items = items
        self.i = 0

    def get(self):
        it = self.items[self.i % len(self.items)]
        self.i += 1
        return it


def build_scan(nc, qT_d, kT_d, kM_d, vM_d, sz_d, g_d, b_d, ng_d, cst_d, y_d, NQH=4):
    P = Prog(nc)
    qT = P.sb("qT", [128, NCH * C], BF16)
    kT = P.sb("kT", [128, NCH * C], BF16)
    kM = P.sb("kM", [128, NCH, C], BF16)
    vM = P.sb("vM", [128, NCH, 2, C], BF16)
    sz = P.sb("sz", [128, NLAT, 2, C], BF16)
    gM = P.sb("gM", [128, NCH, 4], F32)
    bM = P.sb("bM", [128, NCH, 4], F32)
    oacc = P.sb("oacc", [128, NLAT, 2, C], F32)
    yout = P.sb("yout", [128, NLAT, 2, C], BF16)
    ss = P.sb("ss", [128, NLAT * 2], F32)
    cst = P.sb("cst", [128, NCONST, C], F32)
    ng = P.sb("ng", [128, C], F32)
    identr_ = P.sb("identr", [128, C], F32)
    identr = identr_[:].bitcast(F32R)
    S32 = [P.sb("S32_%d" % c, [128, C], F32) for c in range(4)]
    Sbf = [P.sb("Sbf_%d" % c, [128, C], BF16) for c in range(4)]
    tS32 = [T() for _ in range(4)]
    tSbf = [T() for _ in range(4)]
    tq, tk, tkM, tv, tsz, tg, tb, tcst, tng, tidr, tss, tyout = [T() for _ in range(12)]
    toacc = [[T() for _ in range(2)] for _ in range(NLAT)]
    dsem = [P.dsem("ld%d" % i) for i in range(12)]

    def mkpool(name, n, dt):
        return RR([(P.sb("%s%d" % (name, i), [128, C], dt), T()) for i in range(n)])

    p32 = mkpool("p32_", 32, F32)
    p32r = mkpool("p32r_", 48, F32)
    pbf = mkpool("pbf_", 32, BF16)
    pU = mkpool("pU_", 8, F32)
    pUt = mkpool("pUt_", 8, F32)
    pR = mkpool("pR_", 12, F32)
    psm = RR([(P.sb("sm%d" % i, [128, 16], F32), T()) for i in range(6)])
    banks = [P.ps("bank%d" % i, [128, 512]) for i in range(8)]
    pps = RR([(banks[i % 8][:, (i // 8) * C:(i // 8 + 1) * C], T()) for i in range(32)])

    def cs(i):
        return cst[:, i, :]

    P.op("sp", I("dma_start", out=cst[:], in_=cst_d), writes=[tcst], dsem=dsem[0])
    P.op("sp", I("dma_start", out=ng[:], in_=ng_d), writes=[tng], dsem=dsem[1])
    P.op("dve", I("tensor_copy", out=identr, in_=cs(IDENT)), reads=[tcst], writes=[tidr])

    fo = list(range(NCH))
    bo = [1, 0] + list(range(NCH - 1, 1, -1))
    out_ids = []

    for qh in range(NQH):
        P.op("sp", I("dma_start", out=qT[:], in_=qT_d[:, qh, :]), writes=[tq], dsem=dsem[2])
        P.op("sp", I("dma_start", out=kT[:], in_=kT_d[:, qh, :]), writes=[tk], dsem=dsem[3])
        P.op("sp", I("dma_start", out=kM[:], in_=kM_d[:, qh, :, :]), writes=[tkM], dsem=dsem[4])
        P.op("sp", I("dma_start", out=vM[:], in_=vM_d[:, qh, :, :, :]), writes=[tv], dsem=dsem[5])
        P.op("sp", I("dma_start", out=sz[:], in_=sz_d[:, qh, :, :, :]), writes=[tsz], dsem=dsem[6])
        P.op("sp", I("dma_start", out=gM[:], in_=g_d[:, qh, :, :]), writes=[tg], dsem=dsem[7])
        P.op("sp", I("dma_start", out=bM[:], in_=b_d[:, qh, :, :]), writes=[tb], dsem=dsem[8])
        for c in range(4):
            P.op("pool", I("memset", S32[c][:], 0.0), writes=[tS32[c]])
            P.op("pool", I("memset", Sbf[c][:], 0.0), writes=[tSbf[c]])
        first_o = [[True, True] for _ in range(NLAT)]

        def pre(d, n):
            lat = n >= 2
            kc = kT[:, n * C:(n + 1) * C]
            qc = qT[:, n * C:(n + 1) * C]
            res = {}
            kk, tkk = pps.get()
            P.op("pe", I("matmul", kk, lhsT=kc, rhs=kc, start=True, stop=True), reads=[tk], writes=[tkk])
            yield
            kks, tkks = p32.get()
            P.op("dve", I("tensor_tensor", out=kks[:], in0=kk, in1=cs(LT if d == 0 else GT), op=ALU.mult),
                 reads=[tkk, tcst], writes=[tkks])
            yield
            if lat:
                qk, tqk = pps.get()
                P.op("pe", I("matmul", qk, lhsT=kc, rhs=qc, start=True, stop=True), reads=[tk, tq], writes=[tqk])
                yield
                qks, tqks = p32.get()
                P.op("act", I("activation", out=qks[:], in_=qk, func=AF.Copy, scale=DKS), reads=[tqk], writes=[tqks])
                yield
            gcols = gM[:, n, 2 * d:2 * d + 2]
            st, tst = pps.get()
            P.op("pe", I("matmul", st[:, 0:2], lhsT=cs(LE if d == 0 else GE), rhs=gcols, start=True, stop=True),
                 reads=[tcst, tg], writes=[tst])
            P.op("pe", I("matmul", st[:, 2:4], lhsT=cs(GT if d == 0 else LT), rhs=gcols, start=True, stop=True),
                 reads=[tcst, tg], writes=[tst])
            P.op("pe", I("matmul", st[:, 4:6], lhsT=cs(ONES), rhs=gcols, start=True, stop=True),
                 reads=[tcst, tg], writes=[tst])
            yield
            sm, tsm = psm.get()
            P.op("act", I("activation", out=sm[:, 0:6], in_=st[:, 0:6], func=AF.Exp), reads=[tst], writes=[tsm])
            P.op("pool", I("tensor_scalar", out=sm[:, 6:8], in0=sm[:, 0:2], scalar1=-1.0, scalar2=0.0, op0=ALU.mult, op1=ALU.add),
                 reads=[tsm], writes=[tsm])
            P.op("pool", I("tensor_scalar", out=sm[:, 8:10], in0=sm[:, 0:2], scalar1=DKS, scalar2=0.0, op0=ALU.mult, op1=ALU.add),
                 reads=[tsm], writes=[tsm])
            P.op("pool", I("tensor_scalar", out=sm[:, 10:12], in0=bM[:, n, 2 * d:2 * d + 2], scalar1=-1.0, scalar2=0.0,
                           op0=ALU.mult, op1=ALU.add), reads=[tb], writes=[tsm])
            yield
            res["sm"] = (sm, tsm)
            ch = []
            for v in range(2):
                gx, tgx = p32.get()
                P.op("pool", I("tensor_scalar", out=gx[:], in0=cs(LE if d == 0 else GE), scalar1=gM[:, n, 2 * d + v:2 * d + v + 1],
                               scalar2=0.0, op0=ALU.mult, op1=ALU.add), reads=[tcst, tg], writes=[tgx])
                yield
                dt_, tdt = pps.get()
                P.op("pe", I("matmul", dt_, lhsT=cs(GT if d == 0 else LT), rhs=gx[:], start=True, stop=False),
                     reads=[tcst, tgx], writes=[tdt])
                P.op("pe", I("matmul", dt_, lhsT=cs(IDENT), rhs=cs(NEGF if d == 0 else NEGB), start=False, stop=True),
                     reads=[tcst], writes=[tdt])
                yield
                dec, tdec = p32.get()
                P.op("act", I("activation", out=dec[:], in_=dt_, func=AF.Exp), reads=[tdt], writes=[tdec])
                yield
                M, tM = pU.get()
                P.op("dve", I("scalar_tensor_tensor", out=M[:].bitcast(F32R), in0=kks[:], scalar=bM[:, n, 2 * d + v:2 * d + v + 1], in1=dec[:],
                              op0=ALU.mult, op1=ALU.mult), reads=[tkks, tb, tdec], writes=[tM])
                yield
                c = dict(M=(M, tM))
                if lat:
                    at, tat = pbf.get()
                    P.op("pool", I("tensor_tensor", out=at[:], in0=qks[:], in1=dec[:], op=ALU.mult),
                         reads=[tqks, tdec], writes=[tat])
                    c["at"] = (at, tat)
                    yield
                kd, tkd = pbf.get()
                P.op("pool", I("tensor_scalar", out=kd[:], in0=kM[:, n, :], scalar1=sm[:, 2 + v:3 + v], scalar2=0.0,
                               op0=ALU.mult, op1=ALU.add), reads=[tkM, tsm], writes=[tkd])
                c["kd"] = (kd, tkd)
                yield
                ch.append(c)
            for c in ch:
                U, tU = c["M"]
                ut_ps, tut = pps.get()
                P.op("pe", I("matmul", ut_ps, lhsT=U[:].bitcast(F32R), rhs=identr, start=True, stop=True), reads=[tU, tidr], writes=[tut])
                Gt, tGt = pUt.get()
                P.op("dve", I("tensor_tensor", out=Gt[:].bitcast(F32R), in0=ut_ps, in1=cs(IDENT), op=ALU.add), reads=[tut, tcst], writes=[tGt])
                yield
                X, tX = p32r.get()
                Xt, tXt = p32r.get()
                x0, tx0 = p32.get()
                P.op("pool", I("tensor_tensor", out=x0[:], in0=U[:], in1=cs(NM0), op=ALU.mult), reads=[tU, tcst], writes=[tx0])
                P.op("dve", I("tensor_tensor", out=X[:].bitcast(F32R), in0=x0[:], in1=cs(IDENT), op=ALU.add), reads=[tx0, tcst], writes=[tX])
                yield
                x1, tx1 = p32.get()
                P.op("pool", I("tensor_tensor", out=x1[:], in0=Gt[:], in1=cs(NM0), op=ALU.mult), reads=[tGt, tcst], writes=[tx1])
                P.op("dve", I("tensor_tensor", out=Xt[:].bitcast(F32R), in0=x1[:], in1=cs(IDENT), op=ALU.add), reads=[tx1, tcst], writes=[tXt])
                yield
                c["Gt"], c["X"], c["Xt"] = (Gt, tGt), (X, tX), (Xt, tXt)
            for lv in range(1, 7):
                last = lv == 6
                for c in ch:
                    Gt, tGt = c["Gt"]
                    X, tX = c["X"]
                    y_ps, ty = pps.get()
                    P.op("pe", I("matmul", y_ps, lhsT=Gt[:].bitcast(F32R), rhs=X[:].bitcast(F32R), start=True, stop=True),
                         reads=[tGt, tX], writes=[ty])
                    W, tW = p32r.get()
                    P.op("dve", I("tensor_tensor", out=W[:].bitcast(F32R), in0=y_ps, in1=cs(NM0 + lv), op=ALU.mult),
                         reads=[ty, tcst], writes=[tW])
                    c["W"] = (W, tW)
                    yield
                for c in ch:
                    X, tX = c["X"]
                    Xt, tXt = c["Xt"]
                    W, tW = c["W"]
                    x_ps, tx = pps.get()
                    P.op("pe", I("matmul", x_ps, lhsT=Xt[:].bitcast(F32R), rhs=W[:].bitcast(F32R), start=True, stop=True),
                         reads=[tXt, tW], writes=[tx])
                    if not last:
                        xt_ps, txt = pps.get()
                        P.op("pe", I("matmul", xt_ps, lhsT=W[:].bitcast(F32R), rhs=Xt[:].bitcast(F32R), start=True, stop=True),
                             reads=[tW, tXt], writes=[txt])
                    nX, tnX = pR.get() if last else p32r.get()
                    P.op("act", I("activation", out=nX[:].bitcast(F32R), in_=x_ps, func=AF.Copy), reads=[tx], writes=[tnX])
                    c["X"] = (nX, tnX)
                    if not last:
                        nXt, tnXt = p32r.get()
                        P.op("act" if lv % 2 else "dve", (I("activation", out=nXt[:].bitcast(F32R), in_=xt_ps, func=AF.Copy) if lv % 2
                                                       else I("tensor_copy", out=nXt[:].bitcast(F32R), in_=xt_ps)), reads=[txt], writes=[tnXt])
                        c["Xt"] = (nXt, tnXt)
                    yield
            for c in ch:
                c["R"] = c["X"]
            res["ch"] = ch
            return res

        def seq(d, n, res):
            lat = n >= 2
            kc = kT[:, n * C:(n + 1) * C]
            qc = qT[:, n * C:(n + 1) * C]
            sm, tsm = res["sm"]
            for v in range(2):
                ci = 2 * d + v
                c = res["ch"][v]
                ps1, t1 = pps.get()
                P.op("pe", I("matmul", ps1, lhsT=kc, rhs=Sbf[ci][:], start=True, stop=True), reads=[tk, tSbf[ci]], writes=[t1])
                if lat:
                    ps2, t2 = pps.get()
                    P.op("pe", I("matmul", ps2, lhsT=qc, rhs=Sbf[ci][:], start=True, stop=True), reads=[tq, tSbf[ci]], writes=[t2])
                yield
                r, tr = p32r.get()
                P.op("dve", I("scalar_tensor_tensor", out=r[:].bitcast(F32R), in0=ps1, scalar=sm[:, 6 + v:7 + v], in1=vM[:, n, v, :],
                              op0=ALU.mult, op1=ALU.add), reads=[t1, tsm, tv], writes=[tr])
                yield
                R, tR = c["R"]
                ps3, t3 = pps.get()
                P.op("pe", I("matmul", ps3, lhsT=R[:].bitcast(F32R), rhs=r[:].bitcast(F32R), start=True, stop=True), reads=[tR, tr], writes=[t3])
                yield
                vn, tvn = pbf.get()
                P.op("act", I("activation", out=vn[:], in_=ps3, func=AF.Copy, scale=bM[:, n, ci:ci + 1]),
                     reads=[t3, tb], writes=[tvn])
                yield
                kd, tkd = c["kd"]
                ps5, t5 = pps.get()
                P.op("pe", I("matmul", ps5, lhsT=kd[:], rhs=vn[:], start=True, stop=True), reads=[tkd, tvn], writes=[t5])
                if lat:
                    at, tat = c["at"]
                    ps4, t4 = pps.get()
                    P.op("pe", I("matmul", ps4, lhsT=at[:], rhs=vn[:], start=True, stop=True), reads=[tat, tvn], writes=[t4])
                yield
                P.op("dve", I("scalar_tensor_tensor", out=S32[ci][:], in0=S32[ci][:], scalar=sm[:, 4 + v:5 + v], in1=ps5,
                              op0=ALU.mult, op1=ALU.add), reads=[tS32[ci], tsm, t5], writes=[tS32[ci]])
                P.op("act", I("activation", out=Sbf[ci][:], in_=S32[ci][:], func=AF.Copy), reads=[tS32[ci]], writes=[tSbf[ci]])
                yield
                if lat:
                    l = n - 2
                    ov = oacc[:, l, v, :]
                    if first_o[l][v]:
                        first_o[l][v] = False
                        P.op("dve", I("tensor_scalar", out=ov, in0=ps2, scalar1=sm[:, 8 + v:9 + v], scalar2=None, op0=ALU.mult),
                             reads=[t2, tsm], writes=[toacc[l][v]])
                    else:
                        P.op("dve", I("scalar_tensor_tensor", out=ov, in0=ps2, scalar=sm[:, 8 + v:9 + v], in1=ov,
                                      op0=ALU.mult, op1=ALU.add), reads=[t2, tsm, toacc[l][v]], writes=[toacc[l][v]])
                    P.op("dve", I("tensor_tensor", out=ov, in0=ov, in1=ps4, op=ALU.add),
                         reads=[toacc[l][v], t4], writes=[toacc[l][v]])
                    yield

        def drive(gens):
            rets = [None] * len(gens)
            live = list(range(len(gens)))
            while live:
                for gi in list(live):
                    try:
                        next(gens[gi])
                    except StopIteration as e:
                        rets[gi] = e.value
                        live.remove(gi)
            return rets

        cur = drive([pre(0, fo[0]), pre(1, bo[0])])
        for s in range(NCH):
            gens = [seq(0, fo[s], cur[0]), seq(1, bo[s], cur[1])]
            if s + 1 < NCH:
                gens += [pre(0, fo[s + 1]), pre(1, bo[s + 1])]
            r = drive(gens)
            if s + 1 < NCH:
                cur = r[2:4]

        jk, tjk = p32.get()
        for l in range(NLAT):
            for v in range(2):
                P.op("act", I("activation", out=jk[:], in_=oacc[:, l, v, :], func=AF.Square,
                              accum_out=ss[:, 2 * l + v:2 * l + v + 1]), reads=[toacc[l][v]], writes=[tjk, tss])
        P.op("dve", I("tensor_scalar", out=ss[:], in0=ss[:], scalar1=1.0 / C, scalar2=1e-6, op0=ALU.mult, op1=ALU.add),
             reads=[tss], writes=[tss])
        P.op("act", I("activation", out=ss[:], in_=ss[:], func=AF.Sqrt), reads=[tss], writes=[tss])
        P.op("dve", I("reciprocal", out=ss[:], in_=ss[:]), reads=[tss], writes=[tss])
        for l in range(NLAT):
            for v in range(2):
                gz, tgz = p32.get()
                P.op("pool", I("tensor_tensor", out=gz[:], in0=sz[:, l, v, :], in1=ng[:], op=ALU.mult),
                     reads=[tsz, tng], writes=[tgz])
                P.op("dve", I("scalar_tensor_tensor", out=yout[:, l, v, :], in0=oacc[:, l, v, :],
                              scalar=ss[:, 2 * l + v:2 * l + v + 1], in1=gz[:], op0=ALU.mult, op1=ALU.mult),
                     reads=[toacc[l][v], tss, tgz], writes=[tyout])
        out_ids.append(P.op("sp", I("dma_start", out=y_d[:, qh, :, :, :], in_=yout[:]), reads=[tyout], dsem=dsem[9]))
    P.op("sp", I("nop"), after=out_ids)
    return P


TL = 1028
TCX = 68
TT = TL + TCX
NOUT = 1088
NCHK = 97
TILES = [(0, 512), (512, 512), (1024, TT - 1024)]


def build_inproj(nc, uT_d, w_d, cw_d, gbv_d, qk_d, v_d, sz_d, gb_d, only=None):
    P = Prog(nc)
    uT = P.sb("uT", [128, KC, TT], BF16)
    cw = P.sb("cw", [128, 64, 5], F32)
    gbv = P.sb("gbv", [128, 4], F32)
    ones = P.sb("ones", [128, 128], F32)
    wb = [P.sb("wb%d" % i, [128, KC, 512], BF16) for i in range(2)]
    pre = [P.sb("pre%d" % i, [128, TT], F32) for i in range(2)]
    acc = [P.sb("acc%d" % i, [128, TT], F32) for i in range(2)]
    sil = [P.sb("sil%d" % i, [128, NOUT], F32) for i in range(2)]
    sq = [P.sb("sq%d" % i, [128, NOUT], F32) for i in range(2)]
    rs = [P.sb("rs%d" % i, [128, NOUT], F32) for i in range(2)]
    ob = [P.sb("ob%d" % i, [128, NOUT], BF16) for i in range(3)]
    gbo = P.sb("gbo", [128, NOUT], F32)
    tuT, tcw, tgbv, tones, tgbo = [T() for _ in range(5)]
    twb = [T() for _ in range(2)]
    tpre, tacc, tsil, tsq, trs = [[T() for _ in range(2)] for _ in range(5)]
    tob = [T() for _ in range(3)]
    psA = [P.ps("psA%d" % i, [128, 512]) for i in range(6)]
    tpsA = [T() for _ in range(6)]
    psS = [P.ps("psS%d" % i, [128, 512]) for i in range(2)]
    tpsS = [T() for _ in range(2)]
    sw = [P.dsem("sw%d" % i) for i in range(2)]
    sio = [P.dsem("sio%d" % i) for i in range(8)]
    so = [P.dsem("so%d" % i) for i in range(3)]
    P.op("pool", I("memset", ones[:], 1.0), writes=[tones])
    P.op("sp", I("dma_start", out=cw[:], in_=cw_d), writes=[tcw], dsem=sio[0])
    P.op("sp", I("dma_start", out=gbv[:, 0:2], in_=gbv_d), writes=[tgbv], dsem=sio[1])
    for k in range(KC):
        P.op("sp", I("dma_start", out=uT[:, k, :], in_=uT_d[:, k, :]), writes=[tuT], dsem=sio[2 + k % 4])
    P.op("act", I("activation", out=gbv[:, 2:3], in_=gbv[:, 1:2], func=AF.Exp), reads=[tgbv], writes=[tgbv])
    P.op("dve", I("tensor_scalar", out=gbv[:, 2:3], in0=gbv[:, 2:3], scalar1=-1.0, scalar2=None, op0=ALU.mult),
         reads=[tgbv], writes=[tgbv])
    ws = w_d.rearrange("(k p) f -> p k f", p=128)
    out_ids = []
    cnt = [0, 0, 0, 0]

    def nxt(i, m):
        cnt[i] += 1
        return cnt[i] % m

    ngrp = (NCHK + 3) // 4
    for gi in range(ngrp):
        c0 = gi * 4
        ncol = min(4, NCHK - c0) * 128
        s = gi % 2
        P.op("pool", I("dma_start", out=wb[s][:, :, 0:ncol], in_=ws[:, :, c0 * 128:c0 * 128 + ncol]),
             writes=[twb[s]], dsem=sw[s])
        for cj in range(ncol // 128):
            c = c0 + cj
            if only is not None and c not in only:
                continue
            pi = nxt(0, 2)
            for ti, (t0, n) in enumerate(TILES):
                a = nxt(1, 6)
                for k in range(KC):
                    P.op("pe", I("matmul", psA[a][:, 0:n], lhsT=wb[s][:, k, cj * 128:(cj + 1) * 128], rhs=uT[:, k, t0:t0 + n],
                                 start=(k == 0), stop=(k == KC - 1)), reads=[twb[s], tuT], writes=[tpsA[a]])
                P.op("act", I("activation", out=pre[pi][:, t0:t0 + n], in_=psA[a][:, 0:n], func=AF.Copy),
                     reads=[tpsA[a]], writes=[tpre[pi]])
            oi = nxt(2, 3)
            if c < 64:
                ai = pi
                for (o0, n) in ((2, 1024), (TL + 2, 64)):
                    for w in range(5):
                        src = pre[pi][:, o0 - 2 + w:o0 - 2 + w + n]
                        if w == 0:
                            P.op("dve", I("tensor_scalar", out=acc[ai][:, o0:o0 + n], in0=src, scalar1=cw[:, c, 0:1], scalar2=None,
                                          op0=ALU.mult), reads=[tpre[pi], tcw], writes=[tacc[ai]])
                        else:
                            P.op("dve", I("scalar_tensor_tensor", out=acc[ai][:, o0:o0 + n], in0=src, scalar=cw[:, c, w:w + 1],
                                          in1=acc[ai][:, o0:o0 + n], op0=ALU.mult, op1=ALU.add),
                                 reads=[tpre[pi], tcw, tacc[ai]], writes=[tacc[ai]])
                if c < 32:
                    P.op("act", I("activation", out=sil[ai][:, 0:1024], in_=acc[ai][:, 2:1026], func=AF.Silu),
                         reads=[tacc[ai]], writes=[tsil[ai]])
                    P.op("act", I("activation", out=sil[ai][:, 1024:1088], in_=acc[ai][:, TL + 2:TL + 66], func=AF.Silu),
                         reads=[tacc[ai]], writes=[tsil[ai]])
                    P.op("pool", I("tensor_tensor", out=sq[ai][:], in0=sil[ai][:], in1=sil[ai][:], op=ALU.mult),
                         reads=[tsil[ai]], writes=[tsq[ai]])
                    for (t0, n) in ((0, 512), (512, 512), (1024, 64)):
                        si = nxt(3, 2)
                        P.op("pe", I("matmul", psS[si][:, 0:n], lhsT=ones[:], rhs=sq[ai][:, t0:t0 + n], start=True, stop=True),
                             reads=[tones, tsq[ai]], writes=[tpsS[si]])
                        P.op("dve", I("tensor_scalar", out=rs[ai][:, t0:t0 + n], in0=psS[si][:, 0:n], scalar1=1e-6, scalar2=None,
                                      op0=ALU.add), reads=[tpsS[si]], writes=[trs[ai]])
                    P.op("act", I("activation", out=rs[ai][:], in_=rs[ai][:], func=AF.Sqrt), reads=[trs[ai]], writes=[trs[ai]])
                    P.op("dve", I("reciprocal", out=rs[ai][:], in_=rs[ai][:]), reads=[trs[ai]], writes=[trs[ai]])
                    P.op("pool", I("tensor_tensor", out=ob[oi][:], in0=sil[ai][:], in1=rs[ai][:], op=ALU.mult),
                         reads=[tsil[ai], trs[ai]], writes=[tob[oi]])
                    out_ids.append(P.op("sp", I("dma_start", out=qk_d[:, c, :], in_=ob[oi][:]), reads=[tob[oi]], dsem=so[oi]))
                else:
                    P.op("act", I("activation", out=ob[oi][:, 0:1024], in_=acc[ai][:, 2:1026], func=AF.Silu),
                         reads=[tacc[ai]], writes=[tob[oi]])
                    P.op("act", I("activation", out=ob[oi][:, 1024:1088], in_=acc[ai][:, TL + 2:TL + 66], func=AF.Silu),
                         reads=[tacc[ai]], writes=[tob[oi]])
                    out_ids.append(P.op("sp", I("dma_start", out=v_d[:, c - 32, :], in_=ob[oi][:]), reads=[tob[oi]], dsem=so[oi]))
            elif c < 96:
                P.op("act", I("activation", out=ob[oi][:, 0:1024], in_=pre[pi][:, 2:1026], func=AF.Silu),
                     reads=[tpre[pi]], writes=[tob[oi]])
                out_ids.append(P.op("sp", I("dma_start", out=sz_d[:, c - 64, :], in_=ob[oi][:, 0:1024]), reads=[tob[oi]], dsem=so[oi]))
            else:
                for (o0, i0, n) in ((0, 2, 1024), (1024, TL + 2, 64)):
                    P.op("act", I("activation", out=gbo[0:64, o0:o0 + n], in_=pre[pi][0:64, i0:i0 + n], func=AF.Sigmoid),
                         reads=[tpre[pi]], writes=[tgbo])
                    P.op("act", I("activation", out=gbo[64:128, o0:o0 + n], in_=pre[pi][64:128, i0:i0 + n], func=AF.Exp,
                                  bias=gbv[64:128, 0:1]), reads=[tpre[pi], tgbv], writes=[tgbo])
                    P.op("act", I("activation", out=gbo[64:128, o0:o0 + n], in_=gbo[64:128, o0:o0 + n], func=AF.Ln, bias=1.0),
                         reads=[tgbo], writes=[tgbo])
                    P.op("dve", I("tensor_scalar", out=gbo[64:128, o0:o0 + n], in0=gbo[64:128, o0:o0 + n],
                                  scalar1=gbv[64:128, 2:3], scalar2=None, op0=ALU.mult), reads=[tgbo, tgbv], writes=[tgbo])
                out_ids.append(P.op("sp", I("dma_start", out=gb_d, in_=gbo[:]), reads=[tgbo], dsem=sio[6]))
    P.op("sp", I("nop"), after=out_ids)
    return P


L = 4096
TB = 32
FG = 512
TW = 256
NTW = L // TW


def build_fourier(nc, uT_d, cc_d, cs_d, y_d):
    P = Prog(nc)
    uT = P.sb("uT", [128, 4, L], BF16)
    cc = P.sb("cc", [128, 2, 4, FG], BF16)
    AB = P.sb("AB", [128, 2, TB, FG], BF16)
    cs = [P.sb("cs%d" % i, [128, 2, TB, TW], BF16) for i in range(2)]
    yo = [P.sb("yo%d" % i, [128, TW], BF16) for i in range(4)]
    tuT, tcc = T(), T()
    tAB = [[T() for _ in range(TB)] for _ in range(2)]
    tcs = [T() for _ in range(2)]
    tyo = [T() for _ in range(4)]
    psA = [P.ps("psA%d" % i, [128, 512]) for i in range(4)]
    tpsA = [T() for _ in range(4)]
    psY = [P.ps("psY%d" % i, [128, 512]) for i in range(4)]
    tpsY = [T() for _ in range(4)]
    sio = [P.dsem("sio%d" % i) for i in range(6)]
    scs = [P.dsem("scs%d" % i) for i in range(2)]
    so = [P.dsem("so%d" % i) for i in range(4)]
    P.op("sp", I("dma_start", out=cc[:], in_=cc_d), writes=[tcc], dsem=sio[0])
    for k in range(4):
        P.op("sp", I("dma_start", out=uT[:, k, :], in_=uT_d[:, k, :]), writes=[tuT], dsem=sio[1 + k])
    n = 0
    for tb in range(TB):
        for j in range(2):
            a = n % 4
            n += 1
            for k in range(4):
                P.op("pe", I("matmul", psA[a][:], lhsT=uT[:, k, tb * 128:(tb + 1) * 128], rhs=cc[:, j, k, :],
                             start=(k == 0), stop=(k == 3)), reads=[tuT, tcc], writes=[tpsA[a]])
            if j == 0:
                P.op("act", I("activation", out=AB[:, j, tb, :], in_=psA[a][:], func=AF.Copy), reads=[tpsA[a]], writes=[tAB[j][tb]])
            else:
                P.op("dve", I("tensor_copy", out=AB[:, j, tb, :], in_=psA[a][:]), reads=[tpsA[a]], writes=[tAB[j][tb]])
    out_ids = []
    m = 0
    for tw in range(NTW):
        s = tw % 2
        P.op("pool", I("dma_start", out=cs[s][:], in_=cs_d[tw]), writes=[tcs[s]], dsem=scs[s])
        for c in range(4):
            a = m % 4
            m += 1
            i = 0
            for j in range(2):
                for tb in range(TB):
                    P.op("pe", I("matmul", psY[a][:, 0:TW], lhsT=AB[:, j, tb, c * 128:(c + 1) * 128], rhs=cs[s][:, j, tb, :],
                                 start=(i == 0), stop=(i == 2 * TB - 1)), reads=[tAB[j][tb], tcs[s]], writes=[tpsY[a]])
                    i += 1
            if a % 2 == 0:
                P.op("act", I("activation", out=yo[a][:], in_=psY[a][:, 0:TW], func=AF.Copy), reads=[tpsY[a]], writes=[tyo[a]])
            else:
                P.op("dve", I("tensor_copy", out=yo[a][:], in_=psY[a][:, 0:TW]), reads=[tpsY[a]], writes=[tyo[a]])
            out_ids.append(P.op("sp", I("dma_start", out=y_d[:, c, tw * TW:(tw + 1) * TW], in_=yo[a][:]),
                                reads=[tyo[a]], dsem=so[a]))
    P.op("sp", I("nop"), after=out_ids)
    return P


NMC = 18432 // 8
MT = [(0, 512), (512, 512), (1024, 512), (1536, 512), (2048, 256)]


def build_mod(nc, ct_d, w_d, b_d, m_d):
    P = Prog(nc)
    ct = P.sb("ct", [128, KC, 3], F32)
    sc = P.sb("sc", [128, KC, 3], F32)
    bs = P.sb("bs", [3, 2, NMC], F32)
    mo = P.sb("mo", [3, 2, NMC], F32)
    wt = [P.sb("wt%d" % i, [128, NMC], F32) for i in range(4)]
    tct, tsc, tbs, tmo = [T() for _ in range(4)]
    twt = [T() for _ in range(4)]
    ps = [P.ps("ps%d" % i, [128, 512]) for i in range(5)]
    tps = [T() for _ in range(5)]
    sio = [P.dsem("sio%d" % i) for i in range(3)]
    sw = [P.dsem("sw%d" % i) for i in range(4)]
    P.op("sp", I("dma_start", out=ct[:], in_=ct_d), writes=[tct], dsem=sio[0])
    P.op("sp", I("dma_start", out=bs[:], in_=b_d), writes=[tbs], dsem=sio[1])
    P.op("act", I("activation", out=sc[:], in_=ct[:], func=AF.Silu), reads=[tct], writes=[tsc])
    n = 0
    for l in range(2):
        for k in range(KC):
            s = n % 4
            n += 1
            P.op("sp", I("dma_start", out=wt[s][:], in_=w_d[l, k * 128:(k + 1) * 128, :]), writes=[twt[s]], dsem=sw[s])
            for i, (c0, w) in enumerate(MT):
                P.op("pe", I("matmul", ps[i][0:3, 0:w], lhsT=sc[:, k, :], rhs=wt[s][:, c0:c0 + w], start=(k == 0), stop=(k == KC - 1)),
                     reads=[tsc, twt[s]], writes=[tps[i]])
        for i, (c0, w) in enumerate(MT):
            P.op("dve", I("tensor_tensor", out=mo[:, l, c0:c0 + w], in0=ps[i][0:3, 0:w], in1=bs[:, l, c0:c0 + w], op=ALU.add),
                 reads=[tps[i], tbs], writes=[tmo])
    o = P.op("sp", I("dma_start", out=m_d, in_=mo[:]), reads=[tmo], dsem=sio[2])
    P.op("sp", I("nop"), after=[o])
    return P


import ml_dtypes
BFN = ml_dtypes.bfloat16
NCORES = 8
_cache = {}


def _dt(nc, n, s, t, k="ExternalInput"):
    return nc.dram_tensor(n, list(s), t, kind=k).ap()


def _finish(P):
    P.emit()
    P.close()


def prog_mod():
    nc = bass.Bass("TRN2", target_bir_lowering=False)
    ct = _dt(nc, "ct", [128, KC, 3], F32); w = _dt(nc, "w", [2, 2048, NMC], F32); b = _dt(nc, "b", [3, 2, NMC], F32)
    m = _dt(nc, "m", [3, 2, NMC], F32, "ExternalOutput")
    _finish(build_mod(nc, ct, w, b, m))
    return nc


def _ffn_w(nc, tag):
    return (_dt(nc, "wg" + tag, [D, DFF], F32), _dt(nc, "wu" + tag, [D, DFF], F32), _dt(nc, "wd" + tag, [DFF, D], F32))


def prog_l1():
    nc = bass.Bass("TRN2", target_bir_lowering=False)
    NV = 18
    hin = _dt(nc, "hin", [128, KC, 1088], F32); mods = _dt(nc, "mods", [128, NV, KC], F32)
    W = _ffn_w(nc, "0")
    hout = _dt(nc, "hout", [128, KC, 1088], F32, "ExternalOutput")
    uout = _dt(nc, "uout", [128, KC, 1088], BF16, "ExternalOutput")
    P = Prog(nc)
    S = TokStage(P, nc, 1024, 64, NV)
    S.load_h(hin, mods)
    S.mod_gw(0, 2, 7); S.mod_gw(0, 5, 8); S.mod_scale(3, 9, 0.5); S.mod_scale(6, 10, 0.5)
    S.mod_gw(11, 13, 16); S.mod_gw(11, 15, 17)
    S.ffn(*W, 7, 1, 9, 8, 4, 10)
    S.rms_stats()
    S.ada_norm(S.xn, S.txn, 16, 12, 17, 14)
    ids = S.store(hout, S.h, S.th) + S.store(uout, S.xn, S.txn)
    P.op("sp", I("nop"), after=ids)
    _finish(P)
    return nc


def prog_l3():
    nc = bass.Bass("TRN2", target_bir_lowering=False)
    NV = 17
    hin = _dt(nc, "hin", [128, KC, 1024], F32); mods = _dt(nc, "mods", [128, NV, KC], F32)
    yin = _dt(nc, "yin", [128, 32, 1024], BF16); wo = _dt(nc, "wo", [4096, D], F32)
    W0 = _ffn_w(nc, "0"); W1 = _ffn_w(nc, "1")
    hout = _dt(nc, "hout", [128, KC, 1024], F32, "ExternalOutput")
    uout = _dt(nc, "uout", [128, KC, 1024], BF16, "ExternalOutput")
    P = Prog(nc)
    S = TokStage(P, nc, 1024, 0, NV)
    S.load_h(hin, mods)
    S.mod_gw(1, 3, 5); S.mod_scale(4, 6, 0.5); S.mod_gw(7, 9, 11); S.mod_scale(10, 12, 0.5); S.mod_gw(13, 15, 16)
    S.mix(yin, 32, wo, 0)
    S.ffn(*W0, 5, 2, 6)
    S.ffn(*W1, 11, 8, 12)
    S.rms_stats()
    S.ada_norm(S.xn, S.txn, 16, 14)
    ids = S.store(hout, S.h, S.th) + S.store(uout, S.xn, S.txn)
    P.op("sp", I("nop"), after=ids)
    _finish(P)
    return nc


def prog_l5():
    nc = bass.Bass("TRN2", target_bir_lowering=False)
    NV = 8
    hin = _dt(nc, "hin", [128, KC, 1024], F32); mods = _dt(nc, "mods", [128, NV, KC], F32)
    yin = _dt(nc, "yin", [128, 16, 1024], BF16); wo = _dt(nc, "wo", [2048, D], F32)
    W0 = _ffn_w(nc, "0")
    hout = _dt(nc, "hout", [128, KC, 1024], F32, "ExternalOutput")
    P = Prog(nc)
    S = TokStage(P, nc, 1024, 0, NV)
    S.load_h(hin, mods)
    S.mod_gw(1, 3, 5); S.mod_scale(4, 6, 0.5)
    S.mix(yin, 16, wo, 0)
    S.ffn(*W0, 5, 2, 6)
    S.rms_stats()
    S.ada_norm(S.h, S.th, 7, None)
    ids = S.store(hout, S.h, S.th)
    P.op("sp", I("nop"), after=ids)
    _finish(P)
    return nc


def prog_inproj():
    nc = bass.Bass("TRN2", target_bir_lowering=False)
    uT = _dt(nc, "uT", [128, KC, TT], BF16); w = _dt(nc, "w", [2048, 12416], F32)
    cw = _dt(nc, "cw", [128, 64, 5], F32); gbv = _dt(nc, "gbv", [128, 2], F32)
    qk = _dt(nc, "qk", [128, 32, NOUT], BF16, "ExternalOutput"); v = _dt(nc, "v", [128, 32, NOUT], BF16, "ExternalOutput")
    sz = _dt(nc, "sz", [128, 32, 1024], BF16, "ExternalOutput"); gb = _dt(nc, "gb", [128, NOUT], F32, "ExternalOutput")
    _finish(build_inproj(nc, uT, w, cw, gbv, qk, v, sz, gb))
    return nc


def prog_scan():
    nc = bass.Bass("TRN2", target_bir_lowering=False)
    qT = _dt(nc, "qT", [128, 4, 4352], BF16); kT = _dt(nc, "kT", [128, 4, 4352], BF16)
    kM = _dt(nc, "kM", [128, 4, 34, 128], BF16); vM = _dt(nc, "vM", [128, 4, 34, 2, 128], BF16)
    sz = _dt(nc, "sz", [128, 4, 32, 2, 128], BF16)
    g = _dt(nc, "g", [128, 4, 34, 4], F32); b = _dt(nc, "b", [128, 4, 34, 4], F32)
    ng = _dt(nc, "ng", [128, 128], F32); cst = _dt(nc, "cst", [128, NCONST, 128], F32)
    y = _dt(nc, "y", [128, 4, 32, 2, 128], BF16, "ExternalOutput")
    _finish(build_scan(nc, qT, kT, kM, vM, sz, g, b, ng, cst, y, NQH=4))
    return nc


def prog_fourier():
    nc = bass.Bass("TRN2", target_bir_lowering=False)
    uT = _dt(nc, "uT", [128, 4, L], BF16); cc = _dt(nc, "cc", [128, 2, 4, FG], BF16); cs = _dt(nc, "cs", [16, 128, 2, 32, 256], BF16)
    y = _dt(nc, "y", [128, 4, L], BF16, "ExternalOutput")
    _finish(build_fourier(nc, uT, cc, cs, y))
    return nc


def fm(a):
    t, d = a.shape
    return np.ascontiguousarray(a.reshape(t, d // 128, 128).transpose(2, 1, 0))


def tm(a):
    p, n, t = a.shape
    return np.ascontiguousarray(a.transpose(2, 1, 0)).reshape(t, n * 128)


def vec(v):
    return v.reshape(KC, 128).T


def scan_consts():
    m = np.arange(128)[:, None]; i = np.arange(128)[None, :]
    c = np.zeros((128, NCONST, 128), np.float32)
    c[:, LE] = m <= i; c[:, GE] = m >= i; c[:, GT] = m > i; c[:, LT] = m < i
    c[:, ONES] = 1; c[:, IDENT] = m == i
    c[:, NEGF] = np.where(i < m, NEG, 0); c[:, NEGB] = np.where(i > m, NEG, 0)
    for lv in range(7):
        c[:, NM0 + lv] = -(((m >> (lv + 1)) == (i >> (lv + 1))) & ((m >> lv) != (i >> lv))).astype(np.float32) + (0 if lv == 0 else (m == i))
    return c


def fourier_consts():
    c = np.arange(512)
    ang = 2 * np.pi * ((c[:, None] * c[None, :]) % 512) / 512
    CC = np.stack([np.cos(ang), np.sin(ang)], 0) / np.sqrt(512.0)
    cc = np.ascontiguousarray(CC.reshape(2, 4, 128, 512).transpose(2, 0, 1, 3)).astype(BFN)
    t = np.arange(4096, dtype=np.int64)
    ang = 2 * np.pi * ((t[:, None] * t[None, :]) % 4096) / 4096
    tab = np.stack([np.cos(ang), -np.sin(ang)], 0).astype(np.float32) / 64.0
    cs = np.ascontiguousarray(tab.reshape(2, 32, 128, 16, 256).transpose(3, 2, 0, 1, 4)).astype(BFN)
    return cc, cs


def halo(a, lo, hi):
    n = a.shape[0]
    out = np.zeros((hi - lo,) + a.shape[1:], a.dtype)
    s, e = max(lo, 0), min(hi, n)
    out[s - lo:e - lo] = a[s:e]
    return out


def scan_inputs(q, k, v, sz, g, beta, hg):
    qs = q[:, 4 * hg:4 * hg + 4]; ks = k[:, 4 * hg:4 * hg + 4]
    vs = v[:, 8 * hg:8 * hg + 8].reshape(34, 128, 4, 2, 128)
    szs = sz[:, 8 * hg:8 * hg + 8].reshape(32, 128, 4, 2, 128)
    gs = g[:, :, 8 * hg:8 * hg + 8].reshape(34, 128, 2, 4, 2)
    bs = beta[:, :, 8 * hg:8 * hg + 8].reshape(34, 128, 2, 4, 2)
    return {
        "qT": np.ascontiguousarray(qs.transpose(2, 1, 0)),
        "kT": np.ascontiguousarray(ks.transpose(2, 1, 0)),
        "kM": np.ascontiguousarray(ks.reshape(34, 128, 4, 128).transpose(1, 2, 0, 3)),
        "vM": np.ascontiguousarray(vs.transpose(1, 2, 0, 3, 4)),
        "sz": np.ascontiguousarray(szs.transpose(1, 2, 0, 3, 4)),
        "g": np.ascontiguousarray(gs.transpose(1, 3, 0, 2, 4)).reshape(128, 4, 34, 4),
        "b": np.ascontiguousarray(bs.transpose(1, 3, 0, 2, 4)).reshape(128, 4, 34, 4),
    }


def _run(name, builder, in_maps):
    if name not in _cache:
        _cache[name] = builder()
    res = run_bass_kernel_spmd(_cache[name], in_maps, core_ids=list(range(NCORES)))
    return res.results


def kernel(x, c, ctx, c_ctx, norm_g, mod_w, mod_b, ffn_w_gate, ffn_w_up, ffn_w_down, dn_w_in, dn_conv_w,
           dn_a_log, dn_dt_bias, dn_norm_g, dn_w_out, fn_w_out, final_norm_g):
    f32 = np.float32
    A = lambda a: np.asarray(a, dtype=f32)
    x, c, ctx, c_ctx, norm_g, mod_b = A(x), A(c), A(ctx), A(c_ctx), A(norm_g), A(mod_b)
    mod_w = np.asarray(mod_w); ffn_w_gate = np.asarray(ffn_w_gate); ffn_w_up = np.asarray(ffn_w_up)
    ffn_w_down = np.asarray(ffn_w_down)
    cores = range(NCORES)
    cond = np.stack([c[0], c[1], c_ctx], 0)
    ct = np.ascontiguousarray(cond.reshape(3, KC, 128).transpose(2, 1, 0))
    ims = []
    for ci in cores:
        sl = slice(ci * NMC, (ci + 1) * NMC)
        ims.append({"ct": ct, "w": np.ascontiguousarray(mod_w[:, :, sl]),
                    "b": np.ascontiguousarray(np.broadcast_to(mod_b[None, :, sl], (3, 2, NMC)))})
    r = _run("mod", prog_mod, ims)
    m = np.concatenate([r[ci]["m"] for ci in cores], 2)
    M = lambda l, row, j: m[row, l, j * 2048:(j + 1) * 2048]

    ims = []
    for ci in cores:
        b, q = ci // 4, ci % 4
        a = np.concatenate([x[b, q * 1024:(q + 1) * 1024], ctx[b, q * 64:(q + 1) * 64]], 0)
        md = np.zeros((128, 18, KC), f32)
        md[:, 0] = vec(norm_g[0, 0]); md[:, 1] = vec(M(0, b, 0)); md[:, 2] = vec(M(0, b, 1)); md[:, 3] = vec(M(0, b, 2))
        md[:, 4] = vec(M(0, 2, 0)); md[:, 5] = vec(M(0, 2, 1)); md[:, 6] = vec(M(0, 2, 2))
        md[:, 11] = vec(norm_g[0, 1]); md[:, 12] = vec(M(0, b, 3)); md[:, 13] = vec(M(0, b, 4))
        md[:, 14] = vec(M(0, 2, 3)); md[:, 15] = vec(M(0, 2, 4))
        ims.append({"hin": fm(a), "mods": md, "wg0": ffn_w_gate[0, 0], "wu0": ffn_w_up[0, 0], "wd0": ffn_w_down[0, 0]})
    r = _run("l1", prog_l1, ims)
    h_fm = [r[ci]["hout"][:, :, 0:1024] for ci in cores]
    u_lat = [np.concatenate([tm(r[b * 4 + q]["uout"][:, :, 0:1024]) for q in range(4)], 0) for b in range(2)]
    u_ctx = [np.concatenate([tm(r[b * 4 + q]["uout"][:, :, 1024:1088]) for q in range(4)], 0) for b in range(2)]

    cw = np.ascontiguousarray(A(dn_conv_w)[0].reshape(5, 64, 128).transpose(2, 1, 0))
    gbv = np.zeros((128, 2), f32)
    gbv[64:, 0] = A(dn_dt_bias)[0].reshape(64); gbv[64:, 1] = A(dn_a_log)[0].reshape(64)
    w_in = np.asarray(dn_w_in)[0]
    ims = []
    for ci in cores:
        b, q = ci // 4, ci % 4
        a = np.concatenate([halo(u_lat[b], q * 1024 - 2, q * 1024 + 1026), halo(u_ctx[b], q * 64 - 2, q * 64 + 66)], 0)
        ims.append({"uT": fm(a), "w": w_in, "cw": cw, "gbv": gbv})
    r = _run("inproj", prog_inproj, ims)
    cst = scan_consts()
    ng = np.ascontiguousarray(np.broadcast_to(A(dn_norm_g)[0][None, :], (128, 128)))
    ims = []
    for b in range(2):
        def gather(key, lo, hi):
            return np.concatenate([r[b * 4 + q][key][..., lo:hi] for q in range(4)], -1)
        qk = np.concatenate([gather("qk", 1024, 1088), gather("qk", 0, 1024)], -1)
        vv = np.concatenate([gather("v", 1024, 1088), gather("v", 0, 1024)], -1)
        szz = gather("sz", 0, 1024)
        gb = np.concatenate([gather("gb", 1024, 1088), gather("gb", 0, 1024)], -1)
        q_tm = qk[:, 0:16].transpose(2, 1, 0)
        k_tm = qk[:, 16:32].transpose(2, 1, 0)
        v_tm = vv.transpose(2, 1, 0)
        sz_tm = szz.transpose(2, 1, 0)
        beta_tm = gb[0:64].T.reshape(4352, 2, 32)
        g_tm = gb[64:128].T.reshape(4352, 2, 32)
        for hg in range(4):
            d = scan_inputs(q_tm, k_tm, v_tm, sz_tm, g_tm, beta_tm, hg)
            d["ng"] = ng; d["cst"] = cst
            ims.append(d)
    r = _run("scan", prog_scan, ims)
    ypre = []
    for b in range(2):
        yb = np.stack([r[b * 4 + hg]["y"] for hg in range(4)], 0)
        ypre.append(np.ascontiguousarray(yb.transpose(3, 1, 0, 2, 4, 5)).reshape(4096, 4096))

    ims = []
    for ci in cores:
        b, q = ci // 4, ci % 4
        md = np.zeros((128, 17, KC), f32)
        md[:, 0] = vec(M(0, b, 5))
        md[:, 1] = vec(norm_g[0, 2]); md[:, 2] = vec(M(0, b, 6)); md[:, 3] = vec(M(0, b, 7)); md[:, 4] = vec(M(0, b, 8))
        md[:, 7] = vec(norm_g[1, 0]); md[:, 8] = vec(M(1, b, 0)); md[:, 9] = vec(M(1, b, 1)); md[:, 10] = vec(M(1, b, 2))
        md[:, 13] = vec(norm_g[1, 1]); md[:, 14] = vec(M(1, b, 3)); md[:, 15] = vec(M(1, b, 4))
        ims.append({"hin": np.ascontiguousarray(h_fm[ci]), "mods": md, "yin": fm(ypre[b][q * 1024:(q + 1) * 1024]),
                    "wo": np.asarray(dn_w_out)[0],
                    "wg0": ffn_w_gate[0, 1], "wu0": ffn_w_up[0, 1], "wd0": ffn_w_down[0, 1],
                    "wg1": ffn_w_gate[1, 0], "wu1": ffn_w_up[1, 0], "wd1": ffn_w_down[1, 0]})
    r = _run("l3", prog_l3, ims)
    h_fm = [r[ci]["hout"] for ci in cores]
    u1 = [np.concatenate([tm(r[b * 4 + q]["uout"]) for q in range(4)], 0) for b in range(2)]

    cc, cs = fourier_consts()
    ims = []
    for ci in cores:
        b, g = ci // 4, ci % 4
        ims.append({"uT": fm(u1[b][:, g * 512:(g + 1) * 512]), "cc": cc, "cs": cs})
    r = _run("fourier", prog_fourier, ims)
    yf = [np.concatenate([tm(r[b * 4 + g]["y"]) for g in range(4)], 1) for b in range(2)]

    ims = []
    for ci in cores:
        b, q = ci // 4, ci % 4
        md = np.zeros((128, 8, KC), f32)
        md[:, 0] = vec(M(1, b, 5))
        md[:, 1] = vec(norm_g[1, 2]); md[:, 2] = vec(M(1, b, 6)); md[:, 3] = vec(M(1, b, 7)); md[:, 4] = vec(M(1, b, 8))
        md[:, 7] = vec(A(final_norm_g))
        ims.append({"hin": np.ascontiguousarray(h_fm[ci]), "mods": md, "yin": fm(yf[b][q * 1024:(q + 1) * 1024]),
                    "wo": np.asarray(fn_w_out)[0],
                    "wg0": ffn_w_gate[1, 1], "wu0": ffn_w_up[1, 1], "wd0": ffn_w_down[1, 1]})
    r = _run("l5", prog_l5, ims)
    out = np.zeros((2, 4096, 2048), f32)
    for ci in cores:
        b, q = ci // 4, ci % 4
        out[b, q * 1024:(q + 1) * 1024] = tm(r[ci]["hout"])
    return out
```

```python
import numpy as np
import concourse.bass as bass
import concourse.mybir as mybir
from concourse.bass_utils import run_bass_kernel_spmd

F32 = mybir.dt.float32
BF16 = mybir.dt.bfloat16
AF = mybir.ActivationFunctionType
ALU = mybir.AluOpType
AX = mybir.AxisListType


class T:
    __slots__ = ("name", "w", "rd")

    def __init__(self, name=""):
        self.name = name
        self.w = None
        self.rd = []


class DSem:
    __slots__ = ("h", "count", "last")

    def __init__(self, h):
        self.h = h
        self.count = 0
        self.last = None


class Prog:
    ENGS = ("pe", "act", "dve", "pool", "sp")

    def __init__(self, nc):
        self.nc = nc
        self.ins = []
        self.es = None
        self._ctx = []

    def enter(self, cm):
        v = cm.__enter__()
        self._ctx.append(cm)
        return v

    def sb(self, name, shape, dt):
        self.nalloc = getattr(self, "nalloc", 0) + 1
        cm = self.nc.sbuf_tensor("sb%d_%s" % (self.nalloc, name), list(shape), dt)
        v = cm.__enter__()
        self._stage = getattr(self, "_stage", [])
        self._stage.append(cm)
        return v

    def ps(self, name, shape, dt=F32):
        self.nalloc = getattr(self, "nalloc", 0) + 1
        cm = self.nc.psum_tensor("ps%d_%s" % (self.nalloc, name), list(shape), dt)
        v = cm.__enter__()
        self._stage = getattr(self, "_stage", [])
        self._stage.append(cm)
        return v

    def dsem(self, name):
        if not hasattr(self, "sems"):
            self.sems = []
            self.soff = 0
        if self.soff == len(self.sems):
            self.sems.append(DSem(self.enter(self.nc.semaphore("ds%d" % len(self.sems)))))
        self.soff += 1
        return self.sems[self.soff - 1]

    def barrier(self):
        last = {}
        for i, it in enumerate(self.ins):
            last[it["eng"]] = i
        deps = list(last.values()) + [d.last for d in getattr(self, "sems", []) if d.last is not None]
        for e in self.ENGS:
            self.op(e, lambda eng: eng.nop(), after=deps)
        st = getattr(self, "_stage", [])
        while st:
            st.pop().__exit__(None, None, None)
        self.soff = 0

    def close(self):
        st = getattr(self, "_stage", [])
        while st:
            st.pop().__exit__(None, None, None)
        while self._ctx:
            self._ctx.pop().__exit__(None, None, None)

    def op(self, eng, fn, reads=(), writes=(), dsem=None, after=(), inc=16):
        idx = len(self.ins)
        isdma = dsem is not None
        deps = set(after)
        if isdma and dsem.last is not None:
            deps.add(dsem.last)
        for t in reads:
            if t.w is not None:
                deps.add(t.w)
        for t in writes:
            if t.w is not None:
                deps.add(t.w)
            deps.update(t.rd)
        raw = set(t.w for t in reads if t.w is not None)
        keep = []
        for d in deps:
            di = self.ins[d]
            if (not di["dma"]) and (not isdma) and di["eng"] == eng and d not in raw:
                continue
            keep.append(d)
            di["sig"] = True
        ev = None
        if isdma:
            dsem.count += inc
            ev = (dsem.h, dsem.count)
        self.ins.append(dict(eng=eng, fn=fn, deps=keep, dma=isdma, sig=isdma, ev=ev, inc=inc))
        if isdma:
            dsem.last = idx
        for t in reads:
            t.rd.append(idx)
        for t in writes:
            t.w = idx
            t.rd = []
        return idx

    def emit(self):
        nc = self.nc
        es = {e: self.enter(nc.semaphore("es_" + e)) for e in self.ENGS}
        cnt = {e: 0 for e in self.ENGS}
        for it in self.ins:
            if not it["dma"] and it["sig"]:
                cnt[it["eng"]] += 1
                it["ev"] = (es[it["eng"]], cnt[it["eng"]])
        per = {e: [it for it in self.ins if it["eng"] == e] for e in self.ENGS}
        ins = self.ins

        def body(e):
            def f(eng):
                waited = {}
                for it in per[e]:
                    need = {}
                    for d in it["deps"]:
                        s, v = ins[d]["ev"]
                        k = id(s)
                        if waited.get(k, 0) < v and need.get(k, (None, 0))[1] < v:
                            need[k] = (s, v)
                    for k, (s, v) in need.items():
                        eng.wait_ge(s, v)
                        waited[k] = v
                    r = it["fn"](eng)
                    if it["sig"]:
                        s, v = it["ev"]
                        r.then_inc(s, it["inc"] if it["dma"] else 1)
            return f

        with nc.Block() as block:
            block.tensor(body("pe"))
            block.scalar(body("act"))
            block.vector(body("dve"))
            block.gpsimd(body("pool"))
            block.sync(body("sp"))


D = 2048
KC = 16
DFF = 5632
FC = 44
EPS = 1e-6
G = 2


def I(meth, *a, **kw):
    return lambda e: getattr(e, meth)(*a, **kw)


def ttiles(Tn):
    out = []
    t = 0
    while t < Tn:
        n = min(512, Tn - t)
        out.append((t, n))
        t += n
    return out


class TokStage:
    def __init__(self, P, nc, Tlat, Tctx, NV):
        self.P, self.nc = P, nc
        self.Tlat, self.Tctx = Tlat, Tctx
        self.T = Tlat + Tctx
        self.tiles = ttiles(Tlat) + ([(Tlat, Tctx)] if Tctx else [])
        self.nt = len(self.tiles)
        T_ = self.T
        self.h = P.sb("h", [128, KC, T_], F32)
        self.xn = P.sb("xn", [128, KC, T_], BF16)
        self.mods = P.sb("mods", [128, NV, KC], F32)
        self.rstd = P.sb("rstd", [128, T_], F32)
        self.ones = P.sb("ones", [128, 128], F32)
        self.sq = [P.sb("sq%d" % i, [128, 512], F32) for i in range(2)]
        self.tmp = [P.sb("tmp%d" % i, [128, 512], F32) for i in range(2)]
        self.sg = [P.sb("sg%d" % i, [128, 512], BF16) for i in range(2)]
        self.act = [P.sb("act%d" % i, [128, G, T_], BF16) for i in range(2)]
        self.wg = [P.sb("wg%d" % i, [128, KC, G * 128], BF16) for i in range(2)]
        self.wu = [P.sb("wu%d" % i, [128, KC, G * 128], BF16) for i in range(2)]
        self.wd = [P.sb("wd%d" % i, [128, G, D], BF16) for i in range(2)]
        self.psA = [P.ps("psA%d" % i, [128, 512]) for i in range(2)]
        self.psB = [P.ps("psB%d" % i, [128, 512]) for i in range(2)]
        self.psC = [P.ps("psC%d" % i, [128, 512]) for i in range(2)]
        self.psS = [P.ps("psS%d" % i, [128, 512]) for i in range(2)]
        self.th = [[T("h") for _ in range(self.nt)] for _ in range(KC)]
        self.txn = [[T("xn") for _ in range(self.nt)] for _ in range(KC)]
        self.tmods = T("mods")
        self.trstd = [T("rstd") for _ in range(self.nt)]
        self.tones = T("ones")
        self.tsq = [T("sq") for _ in range(2)]
        self.ttmp = [T("tmp") for _ in range(2)]
        self.tsg = [T("sg") for _ in range(2)]
        self.tact = [[[T("act") for _ in range(self.nt)] for _ in range(G)] for _ in range(2)]
        self.twg = [T("wg") for _ in range(2)]
        self.twu = [T("wu") for _ in range(2)]
        self.twd = [T("wd") for _ in range(2)]
        self.tpsA = [T("psA") for _ in range(2)]
        self.tpsB = [T("psB") for _ in range(2)]
        self.tpsC = [T("psC") for _ in range(2)]
        self.tpsS = [T("psS") for _ in range(2)]
        self.swg = [P.dsem("swg%d" % i) for i in range(2)]
        self.swu = [P.dsem("swu%d" % i) for i in range(2)]
        self.swd = [P.dsem("swd%d" % i) for i in range(2)]
        self.sio = [P.dsem("sio%d" % i) for i in range(9)]
        self.cnt = 0
        self.grp = 0
        P.op("pool", I("memset", self.ones[:], 1.0), writes=[self.tones])

    def rr(self):
        self.cnt += 1
        return self.cnt % 2

    def load_h(self, h_dram, mods_dram):
        P = self.P
        P.op("sp", I("dma_start", out=self.mods[:], in_=mods_dram), writes=[self.tmods], dsem=self.sio[0])
        for k in range(KC):
            P.op("sp", I("dma_start", out=self.h[:, k, :], in_=h_dram[:, k, :]),
                 writes=self.th[k], dsem=self.sio[1 + k % 8])

    def store(self, out_dram, sb, tl):
        P = self.P
        ids = []
        for k in range(KC):
            ids.append(P.op("sp", I("dma_start", out=out_dram[:, k, :], in_=sb[:, k, :]),
                            reads=tl[k], dsem=self.sio[1 + k % 8]))
        return ids

    def mod_gw(self, vg, vscale, vout):
        m = self.mods
        self.P.op("dve", I("scalar_tensor_tensor", out=m[:, vout, :], in0=m[:, vscale, :], scalar=1.0,
                           in1=m[:, vg, :], op0=ALU.add, op1=ALU.mult),
                  reads=[self.tmods], writes=[self.tmods])

    def mod_scale(self, vin, vout, c):
        m = self.mods
        self.P.op("dve", I("tensor_scalar", out=m[:, vout, :], in0=m[:, vin, :], scalar1=c, scalar2=None, op0=ALU.mult),
                  reads=[self.tmods], writes=[self.tmods])

    def rms_stats(self):
        P = self.P
        for ti, (t0, n) in enumerate(self.tiles):
            s = self.rr()
            for k in range(KC):
                q = self.rr()
                P.op("act", I("activation", out=self.sq[q][:, 0:n], in_=self.h[:, k, t0:t0 + n], func=AF.Square),
                     reads=[self.th[k][ti]], writes=[self.tsq[q]])
                P.op("pe", I("matmul", self.psS[s][:, 0:n], lhsT=self.ones[:], rhs=self.sq[q][:, 0:n],
                             start=(k == 0), stop=(k == KC - 1)),
                     reads=[self.tsq[q], self.tones], writes=[self.tpsS[s]])
            q = self.rr()
            P.op("dve", I("tensor_scalar", out=self.tmp[q][:, 0:n], in0=self.psS[s][:, 0:n],
                          scalar1=1.0 / D, scalar2=EPS, op0=ALU.mult, op1=ALU.add),
                 reads=[self.tpsS[s]], writes=[self.ttmp[q]])
            P.op("act", I("activation", out=self.tmp[q][:, 0:n], in_=self.tmp[q][:, 0:n], func=AF.Sqrt),
                 reads=[self.ttmp[q]], writes=[self.ttmp[q]])
            P.op("dve", I("reciprocal", out=self.rstd[:, t0:t0 + n], in_=self.tmp[q][:, 0:n]),
                 reads=[self.ttmp[q]], writes=[self.trstd[ti]])

    def ada_norm(self, out_sb, out_tiles, vgw, vshift, vgw_c=None, vshift_c=None):
        P = self.P
        m = self.mods
        for ti, (t0, n) in enumerate(self.tiles):
            isctx = self.Tctx and ti == self.nt - 1
            g_ = vgw_c if isctx else vgw
            s_ = vshift_c if isctx else vshift
            for k in range(KC):
                q = self.rr()
                P.op("dve", I("scalar_tensor_tensor", out=self.tmp[q][:, 0:n], in0=self.h[:, k, t0:t0 + n],
                              scalar=m[:, g_, k:k + 1], in1=self.rstd[:, t0:t0 + n], op0=ALU.mult, op1=ALU.mult),
                     reads=[self.th[k][ti], self.trstd[ti], self.tmods], writes=[self.ttmp[q]])
                if s_ is None:
                    P.op("act", I("activation", out=out_sb[:, k, t0:t0 + n], in_=self.tmp[q][:, 0:n], func=AF.Copy),
                         reads=[self.ttmp[q]], writes=[out_tiles[k][ti]])
                else:
                    P.op("act", I("activation", out=out_sb[:, k, t0:t0 + n], in_=self.tmp[q][:, 0:n],
                                  func=AF.Identity, bias=m[:, s_, k:k + 1]),
                         reads=[self.ttmp[q], self.tmods], writes=[out_tiles[k][ti]])

    def load_w(self, wg, wu, wd, g):
        P = self.P
        s = self.grp % 2
        f0 = g * G * 128
        wgs = wg.rearrange("(k p) f -> p k f", p=128)
        wus = wu.rearrange("(k p) f -> p k f", p=128)
        wds = wd.rearrange("(j p) d -> p j d", p=128)
        P.op("pool", I("dma_start", out=self.wg[s][:, :, :], in_=wgs[:, :, f0:f0 + G * 128]),
             writes=[self.twg[s]], dsem=self.swg[s])
        P.op("pool", I("dma_start", out=self.wu[s][:, :, :], in_=wus[:, :, f0:f0 + G * 128]),
             writes=[self.twu[s]], dsem=self.swu[s])
        P.op("pool", I("dma_start", out=self.wd[s][:, :, :], in_=wds[:, g * G:(g + 1) * G, :]),
             writes=[self.twd[s]], dsem=self.swd[s])
        self.grp += 1
        return s

    def ffn_p1(self, s):
        P = self.P
        for j in range(G):
            for ti, (t0, n) in enumerate(self.tiles):
                a = self.rr()
                for k in range(KC):
                    P.op("pe", I("matmul", self.psA[a][:, 0:n], lhsT=self.wg[s][:, k, j * 128:(j + 1) * 128],
                                 rhs=self.xn[:, k, t0:t0 + n], start=(k == 0), stop=(k == KC - 1)),
                         reads=[self.twg[s], self.txn[k][ti]], writes=[self.tpsA[a]])
                for k in range(KC):
                    P.op("pe", I("matmul", self.psB[a][:, 0:n], lhsT=self.wu[s][:, k, j * 128:(j + 1) * 128],
                                 rhs=self.xn[:, k, t0:t0 + n], start=(k == 0), stop=(k == KC - 1)),
                         reads=[self.twu[s], self.txn[k][ti]], writes=[self.tpsB[a]])
                P.op("act", I("activation", out=self.sg[a][:, 0:n], in_=self.psA[a][:, 0:n], func=AF.Silu),
                     reads=[self.tpsA[a]], writes=[self.tsg[a]])
                P.op("dve", I("tensor_tensor", out=self.act[s][:, j, t0:t0 + n], in0=self.sg[a][:, 0:n],
                              in1=self.psB[a][:, 0:n], op=ALU.mult),
                     reads=[self.tsg[a], self.tpsB[a]], writes=[self.tact[s][j][ti]])
                yield

    def ffn_p2(self, s, vgate, vgate_c):
        P = self.P
        m = self.mods
        for d in range(KC):
            for ti, (t0, n) in enumerate(self.tiles):
                isctx = self.Tctx and ti == self.nt - 1
                gv = vgate_c if isctx else vgate
                self.c4 = (getattr(self, "c4", 0) + 1) % 4
                pc, tpc = ((self.psC[0], self.tpsC[0]), (self.psC[1], self.tpsC[1]),
                           (self.psS[0], self.tpsS[0]), (self.psS[1], self.tpsS[1]))[self.c4]
                for j in range(G):
                    P.op("pe", I("matmul", pc[:, 0:n], lhsT=self.wd[s][:, j, d * 128:(d + 1) * 128],
                                 rhs=self.act[s][:, j, t0:t0 + n], start=(j == 0), stop=(j == G - 1)),
                         reads=[self.twd[s], self.tact[s][j][ti]], writes=[tpc])
                P.op("dve", I("scalar_tensor_tensor", out=self.h[:, d, t0:t0 + n], in0=pc[:, 0:n],
                              scalar=m[:, gv, d:d + 1], in1=self.h[:, d, t0:t0 + n], op0=ALU.mult, op1=ALU.add),
                     reads=[tpc, self.th[d][ti], self.tmods], writes=[self.th[d][ti]])
                yield

    def ffn(self, wg, wu, wd, vgw, vshift, vgate, vgw_c=None, vshift_c=None, vgate_c=None):
        self.rms_stats()
        self.ada_norm(self.xn, self.txn, vgw, vshift, vgw_c, vshift_c)
        NG = FC // G
        prev = None
        per = (KC * self.nt + G * self.nt - 1) // (G * self.nt)
        for g in range(NG):
            s = self.load_w(wg, wu, wd, g)
            g1 = self.ffn_p1(s)
            g2 = self.ffn_p2(prev, vgate, vgate_c) if prev is not None else iter(())
            for _ in g1:
                for _ in range(per):
                    next(g2, None)
            for _ in g2:
                pass
            prev = s
        for _ in self.ffn_p2(prev, vgate, vgate_c):
            pass

    def mix(self, y_d, nfc, w_d, vgate):
        P = self.P
        m = self.mods
        ws = w_d.rearrange("(j p) d -> p j d", p=128)
        bufs = [(self.wg[0], self.twg[0], self.swg[0]), (self.wu[0], self.twu[0], self.swu[0]),
                (self.wg[1], self.twg[1], self.swg[1]), (self.wu[1], self.twu[1], self.swu[1])]
        lat_tiles = [(ti, t0, n) for ti, (t0, n) in enumerate(self.tiles) if t0 < self.Tlat]
        nb = 0
        for half in range(nfc // KC):
            for k in range(KC):
                P.op("sp", I("dma_start", out=self.xn[:, k, 0:self.Tlat], in_=(y_d(half * KC + k) if callable(y_d) else y_d[:, half * KC + k, :])),
                     writes=[self.txn[k][ti] for ti, _, _ in lat_tiles], dsem=self.sio[1 + k % 8])
            for blk in range(D // (G * 128)):
                buf, tb, sb_ = bufs[nb % 4]
                nb += 1
                P.op("pool", I("dma_start", out=buf[:, :, :], in_=ws[:, half * KC:(half + 1) * KC, blk * G * 128:(blk + 1) * G * 128]),
                     writes=[tb], dsem=sb_)
                for dj in range(G):
                    d = blk * G + dj
                    for ti, t0, n in lat_tiles:
                        c = self.rr()
                        for k in range(KC):
                            P.op("pe", I("matmul", self.psC[c][:, 0:n], lhsT=buf[:, k, dj * 128:(dj + 1) * 128],
                                         rhs=self.xn[:, k, t0:t0 + n], start=(k == 0), stop=(k == KC - 1)),
                                 reads=[tb, self.txn[k][ti]], writes=[self.tpsC[c]])
                        P.op("dve", I("scalar_tensor_tensor", out=self.h[:, d, t0:t0 + n], in0=self.psC[c][:, 0:n],
                                      scalar=m[:, vgate, d:d + 1], in1=self.h[:, d, t0:t0 + n], op0=ALU.mult, op1=ALU.add),
                             reads=[self.tpsC[c], self.th[d][ti], self.tmods], writes=[self.th[d][ti]])


F32R = mybir.dt.float32r
C = 128
NCH = 34
NLAT = 32
DKS = 128 ** -0.5
NEG = -30000.0
LE, GE, GT, LT, ONES, IDENT, NEGF, NEGB, NM0 = range(9)
NCONST = 15
NSTAGE = 6
PRE_REP = 1


class RR:
    def __init__(self, items):
        self.items = items
        self.i = 0

    def get(self):
        it = self.items[self.i % len(self.items)]
        self.i += 1
        return it


def build_scan(nc, qT_d, kT_d, kM_d, vM_d, sz_d, g_d, b_d, ng_d, cst_d, y_d, NQH=4):
    P = Prog(nc)
    qT = P.sb("qT", [128, NCH * C], BF16)
    kT = P.sb("kT", [128, NCH * C], BF16)
    kM = P.sb("kM", [128, NCH, C], BF16)
    vM = P.sb("vM", [128, NCH, 2, C], BF16)
    sz = P.sb("sz", [128, NLAT, 2, C], BF16)
    gM = P.sb("gM", [128, NCH, 4], F32)
    bM = P.sb("bM", [128, NCH, 4], F32)
    oacc = P.sb("oacc", [128, NLAT, 2, C], F32)
    yout = P.sb("yout", [128, NLAT, 2, C], BF16)
    ss = P.sb("ss", [128, NLAT * 2], F32)
    cst = P.sb("cst", [128, NCONST, C], F32)
    ng = P.sb("ng", [128, C], F32)
    identr_ = P.sb("identr", [128, C], BF16)
    identr = identr_[:]
    S32 = [P.sb("S32_%d" % c, [128, C], F32) for c in range(4)]
    Sbf = [P.sb("Sbf_%d" % c, [128, C], BF16) for c in range(4)]
    tS32 = [T() for _ in range(4)]
    tSbf = [T() for _ in range(4)]
    tq, tk, tkM, tv, tsz, tg, tb, tcst, tng, tidr, tss, tyout = [T() for _ in range(12)]
    toacc = [[T() for _ in range(2)] for _ in range(NLAT)]
    dsem = [P.dsem("ld%d" % i) for i in range(12)]

    def mkpool(name, n, dt):
        return RR([(P.sb("%s%d" % (name, i), [128, C], dt), T()) for i in range(n)])

    p32 = mkpool("p32_", 32, F32)
    p32r = mkpool("p32r_", 48, BF16)
    pbf = mkpool("pbf_", 32, BF16)
    pU = mkpool("pU_", 8, BF16)
    pUt = mkpool("pUt_", 8, BF16)
    pR = mkpool("pR_", 12, BF16)
    psm = RR([(P.sb("sm%d" % i, [128, 16], F32), T()) for i in range(6)])
    banks = [P.ps("bank%d" % i, [128, 512]) for i in range(8)]
    pps = RR([(banks[i % 8][:, (i // 8) * C:(i // 8 + 1) * C], T()) for i in range(32)])

    def cs(i):
        return cst[:, i, :]

    P.op("sp", I("dma_start", out=cst[:], in_=cst_d), writes=[tcst], dsem=dsem[0])
    P.op("sp", I("dma_start", out=ng[:], in_=ng_d), writes=[tng], dsem=dsem[1])
    P.op("dve", I("tensor_copy", out=identr, in_=cs(IDENT)), reads=[tcst], writes=[tidr])

    fo = list(range(NCH))
    bo = [1, 0] + list(range(NCH - 1, 1, -1))
    out_ids = []

    for qh in range(NQH):
        P.op("sp", I("dma_start", out=qT[:], in_=qT_d[:, qh, :]), writes=[tq], dsem=dsem[2])
        P.op("sp", I("dma_start", out=kT[:], in_=kT_d[:, qh, :]), writes=[tk], dsem=dsem[3])
        P.op("sp", I("dma_start", out=kM[:], in_=kM_d[:, qh, :, :]), writes=[tkM], dsem=dsem[4])
        P.op("sp", I("dma_start", out=vM[:], in_=vM_d[:, qh, :, :, :]), writes=[tv], dsem=dsem[5])
        P.op("sp", I("dma_start", out=sz[:], in_=sz_d[:, qh, :, :, :]), writes=[tsz], dsem=dsem[6])
        P.op("sp", I("dma_start", out=gM[:], in_=g_d[:, qh, :, :]), writes=[tg], dsem=dsem[7])
        P.op("sp", I("dma_start", out=bM[:], in_=b_d[:, qh, :, :]), writes=[tb], dsem=dsem[8])
        for c in range(4):
            P.op("pool", I("memset", S32[c][:], 0.0), writes=[tS32[c]])
            P.op("pool", I("memset", Sbf[c][:], 0.0), writes=[tSbf[c]])
        first_o = [[True, True] for _ in range(NLAT)]

        def pre(d, n):
            lat = n >= 2
            kc = kT[:, n * C:(n + 1) * C]
            qc = qT[:, n * C:(n + 1) * C]
            res = {}
            kk, tkk = pps.get()
            P.op("pe", I("matmul", kk, lhsT=kc, rhs=kc, start=True, stop=True), reads=[tk], writes=[tkk])
            yield
            kks, tkks = p32.get()
            P.op("dve", I("tensor_tensor", out=kks[:], in0=kk, in1=cs(LT if d == 0 else GT), op=ALU.mult),
                 reads=[tkk, tcst], writes=[tkks])
            yield
            if lat:
                qk, tqk = pps.get()
                P.op("pe", I("matmul", qk, lhsT=kc, rhs=qc, start=True, stop=True), reads=[tk, tq], writes=[tqk])
                yield
                qks, tqks = p32.get()
                P.op("act", I("activation", out=qks[:], in_=qk, func=AF.Copy, scale=DKS), reads=[tqk], writes=[tqks])
                yield
            gcols = gM[:, n, 2 * d:2 * d + 2]
            st, tst = pps.get()
            P.op("pe", I("matmul", st[:, 0:2], lhsT=cs(LE if d == 0 else GE), rhs=gcols, start=True, stop=True),
                 reads=[tcst, tg], writes=[tst])
            P.op("pe", I("matmul", st[:, 2:4], lhsT=cs(GT if d == 0 else LT), rhs=gcols, start=True, stop=True),
                 reads=[tcst, tg], writes=[tst])
            P.op("pe", I("matmul", st[:, 4:6], lhsT=cs(ONES), rhs=gcols, start=True, stop=True),
                 reads=[tcst, tg], writes=[tst])
            yield
            sm, tsm = psm.get()
            P.op("act", I("activation", out=sm[:, 0:6], in_=st[:, 0:6], func=AF.Exp), reads=[tst], writes=[tsm])
            P.op("pool", I("tensor_scalar", out=sm[:, 6:8], in0=sm[:, 0:2], scalar1=-1.0, scalar2=0.0, op0=ALU.mult, op1=ALU.add),
                 reads=[tsm], writes=[tsm])
            P.op("pool", I("tensor_scalar", out=sm[:, 8:10], in0=sm[:, 0:2], scalar1=DKS, scalar2=0.0, op0=ALU.mult, op1=ALU.add),
                 reads=[tsm], writes=[tsm])
            P.op("pool", I("tensor_scalar", out=sm[:, 10:12], in0=bM[:, n, 2 * d:2 * d + 2], scalar1=-1.0, scalar2=0.0,
                           op0=ALU.mult, op1=ALU.add), reads=[tb], writes=[tsm])
            yield
            res["sm"] = (sm, tsm)
            ch = []
            for v in range(2):
                gx, tgx = p32.get()
                P.op("pool", I("tensor_scalar", out=gx[:], in0=cs(LE if d == 0 else GE), scalar1=gM[:, n, 2 * d + v:2 * d + v + 1],
                               scalar2=0.0, op0=ALU.mult, op1=ALU.add), reads=[tcst, tg], writes=[tgx])
                yield
                dt_, tdt = pps.get()
                P.op("pe", I("matmul", dt_, lhsT=cs(GT if d == 0 else LT), rhs=gx[:], start=True, stop=False),
                     reads=[tcst, tgx], writes=[tdt])
                P.op("pe", I("matmul", dt_, lhsT=cs(IDENT), rhs=cs(NEGF if d == 0 else NEGB), start=False, stop=True),
                     reads=[tcst], writes=[tdt])
                yield
                dec, tdec = p32.get()
                P.op("act", I("activation", out=dec[:], in_=dt_, func=AF.Exp), reads=[tdt], writes=[tdec])
                yield
                M, tM = pU.get()
                P.op("dve", I("scalar_tensor_tensor", out=M[:], in0=kks[:], scalar=bM[:, n, 2 * d + v:2 * d + v + 1], in1=dec[:],
                              op0=ALU.mult, op1=ALU.mult), reads=[tkks, tb, tdec], writes=[tM])
                yield
                c = dict(M=(M, tM))
                if lat:
                    at, tat = pbf.get()
                    P.op("pool", I("tensor_tensor", out=at[:], in0=qks[:], in1=dec[:], op=ALU.mult),
                         reads=[tqks, tdec], writes=[tat])
                    c["at"] = (at, tat)
                    yield
                kd, tkd = pbf.get()
                P.op("pool", I("tensor_scalar", out=kd[:], in0=kM[:, n, :], scalar1=sm[:, 2 + v:3 + v], scalar2=0.0,
                               op0=ALU.mult, op1=ALU.add), reads=[tkM, tsm], writes=[tkd])
                c["kd"] = (kd, tkd)
                yield
                ch.append(c)
            for c in ch:
                U, tU = c["M"]
                ut_ps, tut = pps.get()
                P.op("pe", I("matmul", ut_ps, lhsT=U[:], rhs=identr, start=True, stop=True), reads=[tU, tidr], writes=[tut])
                Gt, tGt = pUt.get()
                P.op("dve", I("tensor_tensor", out=Gt[:], in0=ut_ps, in1=cs(IDENT), op=ALU.add), reads=[tut, tcst], writes=[tGt])
                yield
                X, tX = p32r.get()
                Xt, tXt = p32r.get()
                x0, tx0 = p32.get()
                P.op("pool", I("tensor_tensor", out=x0[:], in0=U[:], in1=cs(NM0), op=ALU.mult), reads=[tU, tcst], writes=[tx0])
                P.op("dve", I("tensor_tensor", out=X[:], in0=x0[:], in1=cs(IDENT), op=ALU.add), reads=[tx0, tcst], writes=[tX])
                yield
                x1, tx1 = p32.get()
                P.op("pool", I("tensor_tensor", out=x1[:], in0=Gt[:], in1=cs(NM0), op=ALU.mult), reads=[tGt, tcst], writes=[tx1])
                P.op("dve", I("tensor_tensor", out=Xt[:], in0=x1[:], in1=cs(IDENT), op=ALU.add), reads=[tx1, tcst], writes=[tXt])
                yield
                c["Gt"], c["X"], c["Xt"] = (Gt, tGt), (X, tX), (Xt, tXt)
            for lv in range(1, 7):
                last = lv == 6
                for c in ch:
                    Gt, tGt = c["Gt"]
                    X, tX = c["X"]
                    y_ps, ty = pps.get()
                    P.op("pe", I("matmul", y_ps, lhsT=Gt[:], rhs=X[:], start=True, stop=True),
                         reads=[tGt, tX], writes=[ty])
                    W, tW = p32r.get()
                    P.op("dve", I("tensor_tensor", out=W[:], in0=y_ps, in1=cs(NM0 + lv), op=ALU.mult),
                         reads=[ty, tcst], writes=[tW])
                    c["W"] = (W, tW)
                    yield
                for c in ch:
                    X, tX = c["X"]
                    Xt, tXt = c["Xt"]
                    W, tW = c["W"]
                    x_ps, tx = pps.get()
                    P.op("pe", I("matmul", x_ps, lhsT=Xt[:], rhs=W[:], start=True, stop=True),
                         reads=[tXt, tW], writes=[tx])
                    if not last:
                        xt_ps, txt = pps.get()
                        P.op("pe", I("matmul", xt_ps, lhsT=W[:], rhs=Xt[:], start=True, stop=True),
                             reads=[tW, tXt], writes=[txt])
                    nX, tnX = pR.get() if last else p32r.get()
                    P.op("act", I("activation", out=nX[:], in_=x_ps, func=AF.Copy), reads=[tx], writes=[tnX])
                    c["X"] = (nX, tnX)
                    if not last:
                        nXt, tnXt = p32r.get()
                        P.op("act" if lv % 2 else "dve", (I("activation", out=nXt[:], in_=xt_ps, func=AF.Copy) if lv % 2
                                                       else I("tensor_copy", out=nXt[:], in_=xt_ps)), reads=[txt], writes=[tnXt])
                        c["Xt"] = (nXt, tnXt)
                    yield
            for c in ch:
                c["R"] = c["X"]
            res["ch"] = ch
            return res

        def seq(d, n, res):
            lat = n >= 2
            kc = kT[:, n * C:(n + 1) * C]
            qc = qT[:, n * C:(n + 1) * C]
            sm, tsm = res["sm"]
            for v in range(2):
                ci = 2 * d + v
                c = res["ch"][v]
                ps1, t1 = pps.get()
                P.op("pe", I("matmul", ps1, lhsT=kc, rhs=Sbf[ci][:], start=True, stop=True), reads=[tk, tSbf[ci]], writes=[t1])
                if lat:
                    ps2, t2 = pps.get()
                    P.op("pe", I("matmul", ps2, lhsT=qc, rhs=Sbf[ci][:], start=True, stop=True), reads=[tq, tSbf[ci]], writes=[t2])
                yield
                r, tr = p32r.get()
                P.op("dve", I("scalar_tensor_tensor", out=r[:], in0=ps1, scalar=sm[:, 6 + v:7 + v], in1=vM[:, n, v, :],
                              op0=ALU.mult, op1=ALU.add), reads=[t1, tsm, tv], writes=[tr])
                yield
                R, tR = c["R"]
                ps3, t3 = pps.get()
                P.op("pe", I("matmul", ps3, lhsT=R[:], rhs=r[:], start=True, stop=True), reads=[tR, tr], writes=[t3])
                yield
                vn, tvn = pbf.get()
                P.op("act", I("activation", out=vn[:], in_=ps3, func=AF.Copy, scale=bM[:, n, ci:ci + 1]),
                     reads=[t3, tb], writes=[tvn])
                yield
                kd, tkd = c["kd"]
                ps5, t5 = pps.get()
                P.op("pe", I("matmul", ps5, lhsT=kd[:], rhs=vn[:], start=True, stop=True), reads=[tkd, tvn], writes=[t5])
                if lat:
                    at, tat = c["at"]
                    ps4, t4 = pps.get()
                    P.op("pe", I("matmul", ps4, lhsT=at[:], rhs=vn[:], start=True, stop=True), reads=[tat, tvn], writes=[t4])
                yield
                P.op("dve", I("scalar_tensor_tensor", out=S32[ci][:], in0=S32[ci][:], scalar=sm[:, 4 + v:5 + v], in1=ps5,
                              op0=ALU.mult, op1=ALU.add), reads=[tS32[ci], tsm, t5], writes=[tS32[ci]])
                P.op("act", I("activation", out=Sbf[ci][:], in_=S32[ci][:], func=AF.Copy), reads=[tS32[ci]], writes=[tSbf[ci]])
                yield
                if lat:
                    l = n - 2
                    ov = oacc[:, l, v, :]
                    if first_o[l][v]:
                        first_o[l][v] = False
                        P.op("dve", I("tensor_scalar", out=ov, in0=ps2, scalar1=sm[:, 8 + v:9 + v], scalar2=None, op0=ALU.mult),
                             reads=[t2, tsm], writes=[toacc[l][v]])
                    else:
                        P.op("dve", I("scalar_tensor_tensor", out=ov, in0=ps2, scalar=sm[:, 8 + v:9 + v], in1=ov,
                                      op0=ALU.mult, op1=ALU.add), reads=[t2, tsm, toacc[l][v]], writes=[toacc[l][v]])
                    P.op("dve", I("tensor_tensor", out=ov, in0=ov, in1=ps4, op=ALU.add),
                         reads=[toacc[l][v], t4], writes=[toacc[l][v]])
                    yield

        def drive(gens, reps=None):
            rets = [None] * len(gens)
            live = list(range(len(gens)))
            reps = reps or [1] * len(gens)
            while live:
                for gi in list(live):
                    for _ in range(reps[gi]):
                        try:
                            next(gens[gi])
                        except StopIteration as e:
                            rets[gi] = e.value
                            live.remove(gi)
                            break
            return rets

        cur = drive([pre(0, fo[0]), pre(1, bo[0])])
        for s in range(NCH):
            gens = [seq(0, fo[s], cur[0]), seq(1, bo[s], cur[1])]
            if s + 1 < NCH:
                gens += [pre(0, fo[s + 1]), pre(1, bo[s + 1])]
            r = drive(gens, [1, 1, PRE_REP, PRE_REP][:len(gens)])
            if s + 1 < NCH:
                cur = r[2:4]

        jk, tjk = p32.get()
        for l in range(NLAT):
            for v in range(2):
                P.op("act", I("activation", out=jk[:], in_=oacc[:, l, v, :], func=AF.Square,
                              accum_out=ss[:, 2 * l + v:2 * l + v + 1]), reads=[toacc[l][v]], writes=[tjk, tss])
        P.op("dve", I("tensor_scalar", out=ss[:], in0=ss[:], scalar1=1.0 / C, scalar2=1e-6, op0=ALU.mult, op1=ALU.add),
             reads=[tss], writes=[tss])
        P.op("act", I("activation", out=ss[:], in_=ss[:], func=AF.Sqrt), reads=[tss], writes=[tss])
        P.op("dve", I("reciprocal", out=ss[:], in_=ss[:]), reads=[tss], writes=[tss])
        for l in range(NLAT):
            for v in range(2):
                gz, tgz = p32.get()
                P.op("pool", I("tensor_tensor", out=gz[:], in0=sz[:, l, v, :], in1=ng[:], op=ALU.mult),
                     reads=[tsz, tng], writes=[tgz])
                P.op("dve", I("scalar_tensor_tensor", out=yout[:, l, v, :], in0=oacc[:, l, v, :],
                              scalar=ss[:, 2 * l + v:2 * l + v + 1], in1=gz[:], op0=ALU.mult, op1=ALU.mult),
                     reads=[toacc[l][v], tss, tgz], writes=[tyout])
        out_ids.append(P.op("sp", I("dma_start", out=y_d[:, qh, :, :, :], in_=yout[:]), reads=[tyout], dsem=dsem[9]))
    P.op("sp", I("nop"), after=out_ids)
    return P


TL = 1028
TCX = 68
TT = TL + TCX
NOUT = 1088
NCHK = 97
TILES = [(0, 512), (512, 512), (1024, TT - 1024)]


def build_inproj(nc, uT_d, w_d, cw_d, gbv_d, qk_d, v_d, sz_d, gb_d, only=None):
    P = Prog(nc)
    uT = P.sb("uT", [128, KC, TT], BF16)
    cw = P.sb("cw", [128, 64, 5], F32)
    gbv = P.sb("gbv", [128, 4], F32)
    ones = P.sb("ones", [128, 128], F32)
    wb = [P.sb("wb%d" % i, [128, KC, 512], BF16) for i in range(2)]
    pre = [P.sb("pre%d" % i, [128, TT], F32) for i in range(2)]
    acc = [P.sb("acc%d" % i, [128, TT], F32) for i in range(2)]
    sil = [P.sb("sil%d" % i, [128, NOUT], F32) for i in range(2)]
    sq = [P.sb("sq%d" % i, [128, NOUT], F32) for i in range(2)]
    rs = [P.sb("rs%d" % i, [128, NOUT], F32) for i in range(2)]
    ob = [P.sb("ob%d" % i, [128, NOUT], BF16) for i in range(3)]
    gbo = P.sb("gbo", [128, NOUT], F32)
    tuT, tcw, tgbv, tones, tgbo = [T() for _ in range(5)]
    twb = [T() for _ in range(2)]
    tpre, tacc, tsil, tsq, trs = [[T() for _ in range(2)] for _ in range(5)]
    tob = [T() for _ in range(3)]
    psA = [P.ps("psA%d" % i, [128, 512]) for i in range(6)]
    tpsA = [T() for _ in range(6)]
    psS = [P.ps("psS%d" % i, [128, 512]) for i in range(2)]
    tpsS = [T() for _ in range(2)]
    sw = [P.dsem("sw%d" % i) for i in range(2)]
    sio = [P.dsem("sio%d" % i) for i in range(8)]
    so = [P.dsem("so%d" % i) for i in range(3)]
    P.op("pool", I("memset", ones[:], 1.0), writes=[tones])
    P.op("sp", I("dma_start", out=cw[:], in_=cw_d), writes=[tcw], dsem=sio[0])
    P.op("sp", I("dma_start", out=gbv[:, 0:2], in_=gbv_d), writes=[tgbv], dsem=sio[1])
    for k in range(KC):
        P.op("sp", I("dma_start", out=uT[:, k, :], in_=uT_d[:, k, :]), writes=[tuT], dsem=sio[2 + k % 4])
    P.op("act", I("activation", out=gbv[:, 2:3], in_=gbv[:, 1:2], func=AF.Exp), reads=[tgbv], writes=[tgbv])
    P.op("dve", I("tensor_scalar", out=gbv[:, 2:3], in0=gbv[:, 2:3], scalar1=-1.0, scalar2=None, op0=ALU.mult),
         reads=[tgbv], writes=[tgbv])
    ws = w_d.rearrange("(k p) f -> p k f", p=128)
    out_ids = []
    cnt = [0, 0, 0, 0]

    def nxt(i, m):
        cnt[i] += 1
        return cnt[i] % m

    ngrp = (NCHK + 3) // 4
    for gi in range(ngrp):
        c0 = gi * 4
        ncol = min(4, NCHK - c0) * 128
        s = gi % 2
        P.op("pool", I("dma_start", out=wb[s][:, :, 0:ncol], in_=ws[:, :, c0 * 128:c0 * 128 + ncol]),
             writes=[twb[s]], dsem=sw[s])
        for cj in range(ncol // 128):
            c = c0 + cj
            if only is not None and c not in only:
                continue
            pi = nxt(0, 2)
            for ti, (t0, n) in enumerate(TILES):
                a = nxt(1, 6)
                for k in range(KC):
                    P.op("pe", I("matmul", psA[a][:, 0:n], lhsT=wb[s][:, k, cj * 128:(cj + 1) * 128], rhs=uT[:, k, t0:t0 + n],
                                 start=(k == 0), stop=(k == KC - 1)), reads=[twb[s], tuT], writes=[tpsA[a]])
                P.op("act", I("activation", out=pre[pi][:, t0:t0 + n], in_=psA[a][:, 0:n], func=AF.Copy),
                     reads=[tpsA[a]], writes=[tpre[pi]])
            oi = nxt(2, 3)
            if c < 64:
                ai = pi
                for (o0, n) in ((2, 1024), (TL + 2, 64)):
                    for w in range(5):
                        src = pre[pi][:, o0 - 2 + w:o0 - 2 + w + n]
                        if w == 0:
                            P.op("dve", I("tensor_scalar", out=acc[ai][:, o0:o0 + n], in0=src, scalar1=cw[:, c, 0:1], scalar2=None,
                                          op0=ALU.mult), reads=[tpre[pi], tcw], writes=[tacc[ai]])
                        else:
                            P.op("dve", I("scalar_tensor_tensor", out=acc[ai][:, o0:o0 + n], in0=src, scalar=cw[:, c, w:w + 1],
                                          in1=acc[ai][:, o0:o0 + n], op0=ALU.mult, op1=ALU.add),
                                 reads=[tpre[pi], tcw, tacc[ai]], writes=[tacc[ai]])
                if c < 32:
                    P.op("act", I("activation", out=sil[ai][:, 0:1024], in_=acc[ai][:, 2:1026], func=AF.Silu),
                         reads=[tacc[ai]], writes=[tsil[ai]])
                    P.op("act", I("activation", out=sil[ai][:, 1024:1088], in_=acc[ai][:, TL + 2:TL + 66], func=AF.Silu),
                         reads=[tacc[ai]], writes=[tsil[ai]])
                    P.op("pool", I("tensor_tensor", out=sq[ai][:], in0=sil[ai][:], in1=sil[ai][:], op=ALU.mult),
                         reads=[tsil[ai]], writes=[tsq[ai]])
                    for (t0, n) in ((0, 512), (512, 512), (1024, 64)):
                        si = nxt(3, 2)
                        P.op("pe", I("matmul", psS[si][:, 0:n], lhsT=ones[:], rhs=sq[ai][:, t0:t0 + n], start=True, stop=True),
                             reads=[tones, tsq[ai]], writes=[tpsS[si]])
                        P.op("dve", I("tensor_scalar", out=rs[ai][:, t0:t0 + n], in0=psS[si][:, 0:n], scalar1=1e-6, scalar2=None,
                                      op0=ALU.add), reads=[tpsS[si]], writes=[trs[ai]])
                    P.op("act", I("activation", out=rs[ai][:], in_=rs[ai][:], func=AF.Sqrt), reads=[trs[ai]], writes=[trs[ai]])
                    P.op("dve", I("reciprocal", out=rs[ai][:], in_=rs[ai][:]), reads=[trs[ai]], writes=[trs[ai]])
                    P.op("pool", I("tensor_tensor", out=ob[oi][:], in0=sil[ai][:], in1=rs[ai][:], op=ALU.mult),
                         reads=[tsil[ai], trs[ai]], writes=[tob[oi]])
                    out_ids.append(P.op("sp", I("dma_start", out=qk_d[:, c, :], in_=ob[oi][:]), reads=[tob[oi]], dsem=so[oi]))
                else:
                    P.op("act", I("activation", out=ob[oi][:, 0:1024], in_=acc[ai][:, 2:1026], func=AF.Silu),
                         reads=[tacc[ai]], writes=[tob[oi]])
                    P.op("act", I("activation", out=ob[oi][:, 1024:1088], in_=acc[ai][:, TL + 2:TL + 66], func=AF.Silu),
                         reads=[tacc[ai]], writes=[tob[oi]])
                    out_ids.append(P.op("sp", I("dma_start", out=v_d[:, c - 32, :], in_=ob[oi][:]), reads=[tob[oi]], dsem=so[oi]))
            elif c < 96:
                P.op("act", I("activation", out=ob[oi][:, 0:1024], in_=pre[pi][:, 2:1026], func=AF.Silu),
                     reads=[tpre[pi]], writes=[tob[oi]])
                out_ids.append(P.op("sp", I("dma_start", out=sz_d[:, c - 64, :], in_=ob[oi][:, 0:1024]), reads=[tob[oi]], dsem=so[oi]))
            else:
                for (o0, i0, n) in ((0, 2, 1024), (1024, TL + 2, 64)):
                    P.op("act", I("activation", out=gbo[0:64, o0:o0 + n], in_=pre[pi][0:64, i0:i0 + n], func=AF.Sigmoid),
                         reads=[tpre[pi]], writes=[tgbo])
                    P.op("act", I("activation", out=gbo[64:128, o0:o0 + n], in_=pre[pi][64:128, i0:i0 + n], func=AF.Exp,
                                  bias=gbv[64:128, 0:1]), reads=[tpre[pi], tgbv], writes=[tgbo])
                    P.op("act", I("activation", out=gbo[64:128, o0:o0 + n], in_=gbo[64:128, o0:o0 + n], func=AF.Ln, bias=1.0),
                         reads=[tgbo], writes=[tgbo])
                    P.op("dve", I("tensor_scalar", out=gbo[64:128, o0:o0 + n], in0=gbo[64:128, o0:o0 + n],
                                  scalar1=gbv[64:128, 2:3], scalar2=None, op0=ALU.mult), reads=[tgbo, tgbv], writes=[tgbo])
                out_ids.append(P.op("sp", I("dma_start", out=gb_d, in_=gbo[:]), reads=[tgbo], dsem=sio[6]))
    P.op("sp", I("nop"), after=out_ids)
    return P


L = 4096
TB = 32
FG = 512
TW = 256
NTW = L // TW


def build_fourier(nc, uT_d, cc_d, cs_d, y_d):
    P = Prog(nc)
    uT = P.sb("uT", [128, 4, L], BF16)
    cc = P.sb("cc", [128, 2, 4, FG], BF16)
    AB = P.sb("AB", [128, 2, TB, FG], BF16)
    cs = [P.sb("cs%d" % i, [128, 2, TB, TW], BF16) for i in range(2)]
    yo = [P.sb("yo%d" % i, [128, TW], BF16) for i in range(4)]
    tuT, tcc = T(), T()
    tAB = [[T() for _ in range(TB)] for _ in range(2)]
    tcs = [T() for _ in range(2)]
    tyo = [T() for _ in range(4)]
    psA = [P.ps("psA%d" % i, [128, 512]) for i in range(4)]
    tpsA = [T() for _ in range(4)]
    psY = [P.ps("psY%d" % i, [128, 512]) for i in range(4)]
    tpsY = [T() for _ in range(4)]
    sio = [P.dsem("sio%d" % i) for i in range(6)]
    scs = [P.dsem("scs%d" % i) for i in range(2)]
    so = [P.dsem("so%d" % i) for i in range(4)]
    P.op("sp", I("dma_start", out=cc[:], in_=cc_d), writes=[tcc], dsem=sio[0])
    for k in range(4):
        P.op("sp", I("dma_start", out=uT[:, k, :], in_=uT_d[:, k, :]), writes=[tuT], dsem=sio[1 + k])
    n = 0
    for tb in range(TB):
        for j in range(2):
            a = n % 4
            n += 1
            for k in range(4):
                P.op("pe", I("matmul", psA[a][:], lhsT=uT[:, k, tb * 128:(tb + 1) * 128], rhs=cc[:, j, k, :],
                             start=(k == 0), stop=(k == 3)), reads=[tuT, tcc], writes=[tpsA[a]])
            if j == 0:
                P.op("act", I("activation", out=AB[:, j, tb, :], in_=psA[a][:], func=AF.Copy), reads=[tpsA[a]], writes=[tAB[j][tb]])
            else:
                P.op("dve", I("tensor_copy", out=AB[:, j, tb, :], in_=psA[a][:]), reads=[tpsA[a]], writes=[tAB[j][tb]])
    out_ids = []
    m = 0
    for tw in range(NTW):
        s = tw % 2
        P.op("pool", I("dma_start", out=cs[s][:], in_=cs_d[tw]), writes=[tcs[s]], dsem=scs[s])
        for c in range(4):
            a = m % 4
            m += 1
            i = 0
            for j in range(2):
                for tb in range(TB):
                    P.op("pe", I("matmul", psY[a][:, 0:TW], lhsT=AB[:, j, tb, c * 128:(c + 1) * 128], rhs=cs[s][:, j, tb, :],
                                 start=(i == 0), stop=(i == 2 * TB - 1)), reads=[tAB[j][tb], tcs[s]], writes=[tpsY[a]])
                    i += 1
            if a % 2 == 0:
                P.op("act", I("activation", out=yo[a][:], in_=psY[a][:, 0:TW], func=AF.Copy), reads=[tpsY[a]], writes=[tyo[a]])
            else:
                P.op("dve", I("tensor_copy", out=yo[a][:], in_=psY[a][:, 0:TW]), reads=[tpsY[a]], writes=[tyo[a]])
            out_ids.append(P.op("sp", I("dma_start", out=y_d[:, c, tw * TW:(tw + 1) * TW], in_=yo[a][:]),
                                reads=[tyo[a]], dsem=so[a]))
    P.op("sp", I("nop"), after=out_ids)
    return P


NMC = 18432 // 8
MT = [(0, 512), (512, 512), (1024, 512), (1536, 512), (2048, 256)]


def build_mod(nc, ct_d, w_d, b_d, m_d):
    P = Prog(nc)
    ct = P.sb("ct", [128, KC, 3], F32)
    sc = P.sb("sc", [128, KC, 3], F32)
    bs = P.sb("bs", [3, 2, NMC], F32)
    mo = P.sb("mo", [3, 2, NMC], F32)
    wt = [P.sb("wt%d" % i, [128, NMC], F32) for i in range(4)]
    tct, tsc, tbs, tmo = [T() for _ in range(4)]
    twt = [T() for _ in range(4)]
    ps = [P.ps("ps%d" % i, [128, 512]) for i in range(5)]
    tps = [T() for _ in range(5)]
    sio = [P.dsem("sio%d" % i) for i in range(3)]
    sw = [P.dsem("sw%d" % i) for i in range(4)]
    P.op("sp", I("dma_start", out=ct[:], in_=ct_d), writes=[tct], dsem=sio[0])
    P.op("sp", I("dma_start", out=bs[:], in_=b_d), writes=[tbs], dsem=sio[1])
    P.op("act", I("activation", out=sc[:], in_=ct[:], func=AF.Silu), reads=[tct], writes=[tsc])
    n = 0
    for l in range(2):
        for k in range(KC):
            s = n % 4
            n += 1
            P.op("sp", I("dma_start", out=wt[s][:], in_=w_d[l, k * 128:(k + 1) * 128, :]), writes=[twt[s]], dsem=sw[s])
            for i, (c0, w) in enumerate(MT):
                P.op("pe", I("matmul", ps[i][0:3, 0:w], lhsT=sc[:, k, :], rhs=wt[s][:, c0:c0 + w], start=(k == 0), stop=(k == KC - 1)),
                     reads=[tsc, twt[s]], writes=[tps[i]])
        for i, (c0, w) in enumerate(MT):
            P.op("dve", I("tensor_tensor", out=mo[:, l, c0:c0 + w], in0=ps[i][0:3, 0:w], in1=bs[:, l, c0:c0 + w], op=ALU.add),
                 reads=[tps[i], tbs], writes=[tmo])
    o = P.op("sp", I("dma_start", out=m_d, in_=mo[:]), reads=[tmo], dsem=sio[2])
    P.op("sp", I("nop"), after=[o])
    return P


import ml_dtypes
BFN = ml_dtypes.bfloat16
NCORES = 8
_cache = {}


def _dt(nc, n, s, t, k="ExternalInput"):
    return nc.dram_tensor(n, list(s), t, kind=k).ap()


def _finish(P):
    P.emit()
    P.close()


def prog_mod():
    nc = bass.Bass("TRN2", target_bir_lowering=False)
    ct = _dt(nc, "ct", [128, KC, 3], F32); w = _dt(nc, "w", [2, 2048, NMC], F32); b = _dt(nc, "b", [3, 2, NMC], F32)
    m = _dt(nc, "m", [3, 2, NMC], F32, "ExternalOutput")
    _finish(build_mod(nc, ct, w, b, m))
    return nc


def _ffn_w(nc, tag):
    return (_dt(nc, "wg" + tag, [D, DFF], F32), _dt(nc, "wu" + tag, [D, DFF], F32), _dt(nc, "wd" + tag, [DFF, D], F32))


def prog_l1():
    nc = bass.Bass("TRN2", target_bir_lowering=False)
    NV = 18
    hin = _dt(nc, "hin", [128, KC, 1088], F32); mods = _dt(nc, "mods", [128, NV, KC], F32)
    W = _ffn_w(nc, "0")
    hout = _dt(nc, "hout", [128, KC, 1088], F32, "ExternalOutput")
    uout = _dt(nc, "uout", [128, KC, 1088], BF16, "ExternalOutput")
    P = Prog(nc)
    S = TokStage(P, nc, 1024, 64, NV)
    S.load_h(hin, mods)
    S.mod_gw(0, 2, 7); S.mod_gw(0, 5, 8); S.mod_scale(3, 9, 0.5); S.mod_scale(6, 10, 0.5)
    S.mod_gw(11, 13, 16); S.mod_gw(11, 15, 17)
    S.ffn(*W, 7, 1, 9, 8, 4, 10)
    S.rms_stats()
    S.ada_norm(S.xn, S.txn, 16, 12, 17, 14)
    ids = S.store(hout, S.h, S.th) + S.store(uout, S.xn, S.txn)
    P.op("sp", I("nop"), after=ids)
    _finish(P)
    return nc


def prog_l3():
    nc = bass.Bass("TRN2", target_bir_lowering=False)
    NV = 17
    hin = _dt(nc, "hin", [128, KC, 1024], F32); mods = _dt(nc, "mods", [128, NV, KC], F32)
    yin = _dt(nc, "yin", [128, 32, 1024], BF16); wo = _dt(nc, "wo", [4096, D], F32)
    W0 = _ffn_w(nc, "0"); W1 = _ffn_w(nc, "1")
    hout = _dt(nc, "hout", [128, KC, 1024], F32, "ExternalOutput")
    uout = _dt(nc, "uout", [128, KC, 1024], BF16, "ExternalOutput")
    P = Prog(nc)
    S = TokStage(P, nc, 1024, 0, NV)
    S.load_h(hin, mods)
    S.mod_gw(1, 3, 5); S.mod_scale(4, 6, 0.5); S.mod_gw(7, 9, 11); S.mod_scale(10, 12, 0.5); S.mod_gw(13, 15, 16)
    S.mix(yin, 32, wo, 0)
    S.ffn(*W0, 5, 2, 6)
    S.ffn(*W1, 11, 8, 12)
    S.rms_stats()
    S.ada_norm(S.xn, S.txn, 16, 14)
    ids = S.store(hout, S.h, S.th) + S.store(uout, S.xn, S.txn)
    P.op("sp", I("nop"), after=ids)
    _finish(P)
    return nc


def prog_l5():
    nc = bass.Bass("TRN2", target_bir_lowering=False)
    NV = 8
    hin = _dt(nc, "hin", [128, KC, 1024], F32); mods = _dt(nc, "mods", [128, NV, KC], F32)
    yin = _dt(nc, "yin", [128, 16, 1024], BF16); wo = _dt(nc, "wo", [2048, D], F32)
    W0 = _ffn_w(nc, "0")
    hout = _dt(nc, "hout", [128, KC, 1024], F32, "ExternalOutput")
    P = Prog(nc)
    S = TokStage(P, nc, 1024, 0, NV)
    S.load_h(hin, mods)
    S.mod_gw(1, 3, 5); S.mod_scale(4, 6, 0.5)
    S.mix(yin, 16, wo, 0)
    S.ffn(*W0, 5, 2, 6)
    S.rms_stats()
    S.ada_norm(S.h, S.th, 7, None)
    ids = S.store(hout, S.h, S.th)
    P.op("sp", I("nop"), after=ids)
    _finish(P)
    return nc


def prog_inproj():
    nc = bass.Bass("TRN2", target_bir_lowering=False)
    uT = _dt(nc, "uT", [128, KC, TT], BF16); w = _dt(nc, "w", [2048, 12416], F32)
    cw = _dt(nc, "cw", [128, 64, 5], F32); gbv = _dt(nc, "gbv", [128, 2], F32)
    qk = _dt(nc, "qk", [128, 32, NOUT], BF16, "ExternalOutput"); v = _dt(nc, "v", [128, 32, NOUT], BF16, "ExternalOutput")
    sz = _dt(nc, "sz", [128, 32, 1024], BF16, "ExternalOutput"); gb = _dt(nc, "gb", [128, NOUT], F32, "ExternalOutput")
    _finish(build_inproj(nc, uT, w, cw, gbv, qk, v, sz, gb))
    return nc


def prog_scan():
    nc = bass.Bass("TRN2", target_bir_lowering=False)
    qT = _dt(nc, "qT", [128, 4, 4352], BF16); kT = _dt(nc, "kT", [128, 4, 4352], BF16)
    kM = _dt(nc, "kM", [128, 4, 34, 128], BF16); vM = _dt(nc, "vM", [128, 4, 34, 2, 128], BF16)
    sz = _dt(nc, "sz", [128, 4, 32, 2, 128], BF16)
    g = _dt(nc, "g", [128, 4, 34, 4], F32); b = _dt(nc, "b", [128, 4, 34, 4], F32)
    ng = _dt(nc, "ng", [128, 128], F32); cst = _dt(nc, "cst", [128, NCONST, 128], F32)
    y = _dt(nc, "y", [128, 4, 32, 2, 128], BF16, "ExternalOutput")
    _finish(build_scan(nc, qT, kT, kM, vM, sz, g, b, ng, cst, y, NQH=4))
    return nc


def prog_fourier():
    nc = bass.Bass("TRN2", target_bir_lowering=False)
    uT = _dt(nc, "uT", [128, 4, L], BF16); cc = _dt(nc, "cc", [128, 2, 4, FG], BF16); cs = _dt(nc, "cs", [16, 128, 2, 32, 256], BF16)
    y = _dt(nc, "y", [128, 4, L], BF16, "ExternalOutput")
    _finish(build_fourier(nc, uT, cc, cs, y))
    return nc


def fm(a):
    t, d = a.shape
    return np.ascontiguousarray(a.reshape(t, d // 128, 128).transpose(2, 1, 0))


def tm(a):
    p, n, t = a.shape
    return np.ascontiguousarray(a.transpose(2, 1, 0)).reshape(t, n * 128)


def vec(v):
    return v.reshape(KC, 128).T


def scan_consts():
    m = np.arange(128)[:, None]; i = np.arange(128)[None, :]
    c = np.zeros((128, NCONST, 128), np.float32)
    c[:, LE] = m <= i; c[:, GE] = m >= i; c[:, GT] = m > i; c[:, LT] = m < i
    c[:, ONES] = 1; c[:, IDENT] = m == i
    c[:, NEGF] = np.where(i < m, NEG, 0); c[:, NEGB] = np.where(i > m, NEG, 0)
    for lv in range(7):
        c[:, NM0 + lv] = -(((m >> (lv + 1)) == (i >> (lv + 1))) & ((m >> lv) != (i >> lv))).astype(np.float32) + (0 if lv == 0 else (m == i))
    return c


def fourier_consts():
    c = np.arange(512)
    ang = 2 * np.pi * ((c[:, None] * c[None, :]) % 512) / 512
    CC = np.stack([np.cos(ang), np.sin(ang)], 0) / np.sqrt(512.0)
    cc = np.ascontiguousarray(CC.reshape(2, 4, 128, 512).transpose(2, 0, 1, 3)).astype(BFN)
    t = np.arange(4096, dtype=np.int64)
    ang = 2 * np.pi * ((t[:, None] * t[None, :]) % 4096) / 4096
    tab = np.stack([np.cos(ang), -np.sin(ang)], 0).astype(np.float32) / 64.0
    cs = np.ascontiguousarray(tab.reshape(2, 32, 128, 16, 256).transpose(3, 2, 0, 1, 4)).astype(BFN)
    return cc, cs


def halo(a, lo, hi):
    n = a.shape[0]
    out = np.zeros((hi - lo,) + a.shape[1:], a.dtype)
    s, e = max(lo, 0), min(hi, n)
    out[s - lo:e - lo] = a[s:e]
    return out


def scan_inputs(q, k, v, sz, g, beta, hg):
    qs = q[:, 4 * hg:4 * hg + 4]; ks = k[:, 4 * hg:4 * hg + 4]
    vs = v[:, 8 * hg:8 * hg + 8].reshape(34, 128, 4, 2, 128)
    szs = sz[:, 8 * hg:8 * hg + 8].reshape(32, 128, 4, 2, 128)
    gs = g[:, :, 8 * hg:8 * hg + 8].reshape(34, 128, 2, 4, 2)
    bs = beta[:, :, 8 * hg:8 * hg + 8].reshape(34, 128, 2, 4, 2)
    return {
        "qT": np.ascontiguousarray(qs.transpose(2, 1, 0)),
        "kT": np.ascontiguousarray(ks.transpose(2, 1, 0)),
        "kM": np.ascontiguousarray(ks.reshape(34, 128, 4, 128).transpose(1, 2, 0, 3)),
        "vM": np.ascontiguousarray(vs.transpose(1, 2, 0, 3, 4)),
        "sz": np.ascontiguousarray(szs.transpose(1, 2, 0, 3, 4)),
        "g": np.ascontiguousarray(gs.transpose(1, 3, 0, 2, 4)).reshape(128, 4, 34, 4),
        "b": np.ascontiguousarray(bs.transpose(1, 3, 0, 2, 4)).reshape(128, 4, 34, 4),
    }


def _run(name, builder, in_maps):
    if name not in _cache:
        _cache[name] = builder()
    res = run_bass_kernel_spmd(_cache[name], in_maps, core_ids=list(range(NCORES)))
    return res.results


def kernel(x, c, ctx, c_ctx, norm_g, mod_w, mod_b, ffn_w_gate, ffn_w_up, ffn_w_down, dn_w_in, dn_conv_w,
           dn_a_log, dn_dt_bias, dn_norm_g, dn_w_out, fn_w_out, final_norm_g):
    f32 = np.float32
    A = lambda a: np.asarray(a, dtype=f32)
    x, c, ctx, c_ctx, norm_g, mod_b = A(x), A(c), A(ctx), A(c_ctx), A(norm_g), A(mod_b)
    mod_w = np.asarray(mod_w); ffn_w_gate = np.asarray(ffn_w_gate); ffn_w_up = np.asarray(ffn_w_up)
    ffn_w_down = np.asarray(ffn_w_down)
    cores = range(NCORES)
    cond = np.stack([c[0], c[1], c_ctx], 0)
    ct = np.ascontiguousarray(cond.reshape(3, KC, 128).transpose(2, 1, 0))
    ims = []
    for ci in cores:
        sl = slice(ci * NMC, (ci + 1) * NMC)
        ims.append({"ct": ct, "w": np.ascontiguousarray(mod_w[:, :, sl]),
                    "b": np.ascontiguousarray(np.broadcast_to(mod_b[None, :, sl], (3, 2, NMC)))})
    r = _run("mod", prog_mod, ims)
    m = np.concatenate([r[ci]["m"] for ci in cores], 2)
    M = lambda l, row, j: m[row, l, j * 2048:(j + 1) * 2048]

    ims = []
    for ci in cores:
        b, q = ci // 4, ci % 4
        a = np.concatenate([x[b, q * 1024:(q + 1) * 1024], ctx[b, q * 64:(q + 1) * 64]], 0)
        md = np.zeros((128, 18, KC), f32)
        md[:, 0] = vec(norm_g[0, 0]); md[:, 1] = vec(M(0, b, 0)); md[:, 2] = vec(M(0, b, 1)); md[:, 3] = vec(M(0, b, 2))
        md[:, 4] = vec(M(0, 2, 0)); md[:, 5] = vec(M(0, 2, 1)); md[:, 6] = vec(M(0, 2, 2))
        md[:, 11] = vec(norm_g[0, 1]); md[:, 12] = vec(M(0, b, 3)); md[:, 13] = vec(M(0, b, 4))
        md[:, 14] = vec(M(0, 2, 3)); md[:, 15] = vec(M(0, 2, 4))
        ims.append({"hin": fm(a), "mods": md, "wg0": ffn_w_gate[0, 0], "wu0": ffn_w_up[0, 0], "wd0": ffn_w_down[0, 0]})
    r = _run("l1", prog_l1, ims)
    h_fm = [r[ci]["hout"][:, :, 0:1024] for ci in cores]
    u_lat = [np.concatenate([tm(r[b * 4 + q]["uout"][:, :, 0:1024]) for q in range(4)], 0) for b in range(2)]
    u_ctx = [np.concatenate([tm(r[b * 4 + q]["uout"][:, :, 1024:1088]) for q in range(4)], 0) for b in range(2)]

    cw = np.ascontiguousarray(A(dn_conv_w)[0].reshape(5, 64, 128).transpose(2, 1, 0))
    gbv = np.zeros((128, 2), f32)
    gbv[64:, 0] = A(dn_dt_bias)[0].reshape(64); gbv[64:, 1] = A(dn_a_log)[0].reshape(64)
    w_in = np.asarray(dn_w_in)[0]
    ims = []
    for ci in cores:
        b, q = ci // 4, ci % 4
        a = np.concatenate([halo(u_lat[b], q * 1024 - 2, q * 1024 + 1026), halo(u_ctx[b], q * 64 - 2, q * 64 + 66)], 0)
        ims.append({"uT": fm(a), "w": w_in, "cw": cw, "gbv": gbv})
    r = _run("inproj", prog_inproj, ims)
    cst = scan_consts()
    ng = np.ascontiguousarray(np.broadcast_to(A(dn_norm_g)[0][None, :], (128, 128)))
    ims = []
    for b in range(2):
        def gather(key, lo, hi):
            return np.concatenate([r[b * 4 + q][key][..., lo:hi] for q in range(4)], -1)
        qk = np.concatenate([gather("qk", 1024, 1088), gather("qk", 0, 1024)], -1)
        vv = np.concatenate([gather("v", 1024, 1088), gather("v", 0, 1024)], -1)
        szz = gather("sz", 0, 1024)
        gb = np.concatenate([gather("gb", 1024, 1088), gather("gb", 0, 1024)], -1)
        q_tm = qk[:, 0:16].transpose(2, 1, 0)
        k_tm = qk[:, 16:32].transpose(2, 1, 0)
        v_tm = vv.transpose(2, 1, 0)
        sz_tm = szz.transpose(2, 1, 0)
        beta_tm = gb[0:64].T.reshape(4352, 2, 32)
        g_tm = gb[64:128].T.reshape(4352, 2, 32)
        for hg in range(4):
            d = scan_inputs(q_tm, k_tm, v_tm, sz_tm, g_tm, beta_tm, hg)
            d["ng"] = ng; d["cst"] = cst
            ims.append(d)
    r = _run("scan", prog_scan, ims)
    ypre = []
    for b in range(2):
        yb = np.stack([r[b * 4 + hg]["y"] for hg in range(4)], 0)
        ypre.append(np.ascontiguousarray(yb.transpose(3, 1, 0, 2, 4, 5)).reshape(4096, 4096))

    ims = []
    for ci in cores:
        b, q = ci // 4, ci % 4
        md = np.zeros((128, 17, KC), f32)
        md[:, 0] = vec(M(0, b, 5))
        md[:, 1] = vec(norm_g[0, 2]); md[:, 2] = vec(M(0, b, 6)); md[:, 3] = vec(M(0, b, 7)); md[:, 4] = vec(M(0, b, 8))
        md[:, 7] = vec(norm_g[1, 0]); md[:, 8] = vec(M(1, b, 0)); md[:, 9] = vec(M(1, b, 1)); md[:, 10] = vec(M(1, b, 2))
        md[:, 13] = vec(norm_g[1, 1]); md[:, 14] = vec(M(1, b, 3)); md[:, 15] = vec(M(1, b, 4))
        ims.append({"hin": np.ascontiguousarray(h_fm[ci]), "mods": md, "yin": fm(ypre[b][q * 1024:(q + 1) * 1024]),
                    "wo": np.asarray(dn_w_out)[0],
                    "wg0": ffn_w_gate[0, 1], "wu0": ffn_w_up[0, 1], "wd0": ffn_w_down[0, 1],
                    "wg1": ffn_w_gate[1, 0], "wu1": ffn_w_up[1, 0], "wd1": ffn_w_down[1, 0]})
    r = _run("l3", prog_l3, ims)
    h_fm = [r[ci]["hout"] for ci in cores]
    u1 = [np.concatenate([tm(r[b * 4 + q]["uout"]) for q in range(4)], 0) for b in range(2)]

    cc, cs = fourier_consts()
    ims = []
    for ci in cores:
        b, g = ci // 4, ci % 4
        ims.append({"uT": fm(u1[b][:, g * 512:(g + 1) * 512]), "cc": cc, "cs": cs})
    r = _run("fourier", prog_fourier, ims)
    yf = [np.concatenate([tm(r[b * 4 + g]["y"]) for g in range(4)], 1) for b in range(2)]

    ims = []
    for ci in cores:
        b, q = ci // 4, ci % 4
        md = np.zeros((128, 8, KC), f32)
        md[:, 0] = vec(M(1, b, 5))
        md[:, 1] = vec(norm_g[1, 2]); md[:, 2] = vec(M(1, b, 6)); md[:, 3] = vec(M(1, b, 7)); md[:, 4] = vec(M(1, b, 8))
        md[:, 7] = vec(A(final_norm_g))
        ims.append({"hin": np.ascontiguousarray(h_fm[ci]), "mods": md, "yin": fm(yf[b][q * 1024:(q + 1) * 1024]),
                    "wo": np.asarray(fn_w_out)[0],
                    "wg0": ffn_w_gate[1, 1], "wu0": ffn_w_up[1, 1], "wd0": ffn_w_down[1, 1]})
    r = _run("l5", prog_l5, ims)
    out = np.zeros((2, 4096, 2048), f32)
    for ci in cores:
        b, q = ci // 4, ci % 4
        out[b, q * 1024:(q + 1) * 1024] = tm(r[ci]["hout"])
    return out
```

```python
import numpy as np
import concourse.bass as bass
import concourse.mybir as mybir
from concourse.bass_utils import run_bass_kernel_spmd

F32 = mybir.dt.float32
BF16 = mybir.dt.bfloat16
AF = mybir.ActivationFunctionType
ALU = mybir.AluOpType
AX = mybir.AxisListType


class T:
    __slots__ = ("name", "w", "rd")

    def __init__(self, name=""):
        self.name = name
        self.w = None
        self.rd = []


class DSem:
    __slots__ = ("h", "count", "last")

    def __init__(self, h):
        self.h = h
        self.count = 0
        self.last = None


class Prog:
    ENGS = ("pe", "act", "dve", "pool", "sp")

    def __init__(self, nc):
        self.nc = nc
        self.ins = []
        self.es = None
        self._ctx = []

    def enter(self, cm):
        v = cm.__enter__()
        self._ctx.append(cm)
        return v

    def sb(self, name, shape, dt):
        self.nalloc = getattr(self, "nalloc", 0) + 1
        cm = self.nc.sbuf_tensor("sb%d_%s" % (self.nalloc, name), list(shape), dt)
        v = cm.__enter__()
        self._stage = getattr(self, "_stage", [])
        self._stage.append(cm)
        return v

    def ps(self, name, shape, dt=F32):
        self.nalloc = getattr(self, "nalloc", 0) + 1
        cm = self.nc.psum_tensor("ps%d_%s" % (self.nalloc, name), list(shape), dt)
        v = cm.__enter__()
        self._stage = getattr(self, "_stage", [])
        self._stage.append(cm)
        return v

    def dsem(self, name):
        if not hasattr(self, "sems"):
            self.sems = []
            self.soff = 0
        if self.soff == len(self.sems):
            self.sems.append(DSem(self.enter(self.nc.semaphore("ds%d" % len(self.sems)))))
        self.soff += 1
        return self.sems[self.soff - 1]

    def barrier(self):
        last = {}
        for i, it in enumerate(self.ins):
            last[it["eng"]] = i
        deps = list(last.values()) + [d.last for d in getattr(self, "sems", []) if d.last is not None]
        for e in self.ENGS:
            self.op(e, lambda eng: eng.nop(), after=deps)
        st = getattr(self, "_stage", [])
        while st:
            st.pop().__exit__(None, None, None)
        self.soff = 0

    def close(self):
        st = getattr(self, "_stage", [])
        while st:
            st.pop().__exit__(None, None, None)
        while self._ctx:
            self._ctx.pop().__exit__(None, None, None)

    def op(self, eng, fn, reads=(), writes=(), dsem=None, after=(), inc=16):
        idx = len(self.ins)
        isdma = dsem is not None
        deps = set(after)
        if isdma and dsem.last is not None:
            deps.add(dsem.last)
        for t in reads:
            if t.w is not None:
                deps.add(t.w)
        for t in writes:
            if t.w is not None:
                deps.add(t.w)
            deps.update(t.rd)
        raw = set(t.w for t in reads if t.w is not None)
        keep = []
        for d in deps:
            di = self.ins[d]
            if (not di["dma"]) and (not isdma) and di["eng"] == eng and d not in raw:
                continue
            keep.append(d)
            di["sig"] = True
        ev = None
        if isdma:
            dsem.count += inc
            ev = (dsem.h, dsem.count)
        self.ins.append(dict(eng=eng, fn=fn, deps=keep, dma=isdma, sig=isdma, ev=ev, inc=inc))
        if isdma:
            dsem.last = idx
        for t in reads:
            t.rd.append(idx)
        for t in writes:
            t.w = idx
            t.rd = []
        return idx

    def emit(self):
        nc = self.nc
        es = {e: self.enter(nc.semaphore("es_" + e)) for e in self.ENGS}
        cnt = {e: 0 for e in self.ENGS}
        for it in self.ins:
            if not it["dma"] and it["sig"]:
                cnt[it["eng"]] += 1
                it["ev"] = (es[it["eng"]], cnt[it["eng"]])
        per = {e: [it for it in self.ins if it["eng"] == e] for e in self.ENGS}
        ins = self.ins

        def body(e):
            def f(eng):
                waited = {}
                for it in per[e]:
                    need = {}
                    for d in it["deps"]:
                        s, v = ins[d]["ev"]
                        k = id(s)
                        if waited.get(k, 0) < v and need.get(k, (None, 0))[1] < v:
                            need[k] = (s, v)
                    for k, (s, v) in need.items():
                        eng.wait_ge(s, v)
                        waited[k] = v
                    r = it["fn"](eng)
                    if it["sig"]:
                        s, v = it["ev"]
                        r.then_inc(s, it["inc"] if it["dma"] else 1)
            return f

        with nc.Block() as block:
            block.tensor(body("pe"))
            block.scalar(body("act"))
            block.vector(body("dve"))
            block.gpsimd(body("pool"))
            block.sync(body("sp"))


D = 2048
KC = 16
DFF = 5632
FC = 44
EPS = 1e-6
G = 2


def I(meth, *a, **kw):
    return lambda e: getattr(e, meth)(*a, **kw)


def ttiles(Tn):
    out = []
    t = 0
    while t < Tn:
        n = min(512, Tn - t)
        out.append((t, n))
        t += n
    return out


class TokStage:
    def __init__(self, P, nc, Tlat, Tctx, NV):
        self.P, self.nc = P, nc
        self.Tlat, self.Tctx = Tlat, Tctx
        self.T = Tlat + Tctx
        self.tiles = ttiles(Tlat) + ([(Tlat, Tctx)] if Tctx else [])
        self.nt = len(self.tiles)
        T_ = self.T
        self.h = P.sb("h", [128, KC, T_], F32)
        self.xn = P.sb("xn", [128, KC, T_], BF16)
        self.mods = P.sb("mods", [128, NV, KC], F32)
        self.rstd = P.sb("rstd", [128, T_], F32)
        self.ones = P.sb("ones", [128, 128], F32)
        self.sq = [P.sb("sq%d" % i, [128, 512], F32) for i in range(2)]
        self.tmp = [P.sb("tmp%d" % i, [128, 512], F32) for i in range(2)]
        self.sg = [P.sb("sg%d" % i, [128, 512], BF16) for i in range(2)]
        self.act = [P.sb("act%d" % i, [128, G, T_], BF16) for i in range(2)]
        self.wg = [P.sb("wg%d" % i, [128, KC, G * 128], BF16) for i in range(2)]
        self.wu = [P.sb("wu%d" % i, [128, KC, G * 128], BF16) for i in range(2)]
        self.wd = [P.sb("wd%d" % i, [128, G, D], BF16) for i in range(2)]
        self.psA = [P.ps("psA%d" % i, [128, 512]) for i in range(2)]
        self.psB = [P.ps("psB%d" % i, [128, 512]) for i in range(2)]
        self.psC = [P.ps("psC%d" % i, [128, 512]) for i in range(2)]
        self.psS = [P.ps("psS%d" % i, [128, 512]) for i in range(2)]
        self.th = [[T("h") for _ in range(self.nt)] for _ in range(KC)]
        self.txn = [[T("xn") for _ in range(self.nt)] for _ in range(KC)]
        self.tmods = T("mods")
        self.trstd = [T("rstd") for _ in range(self.nt)]
        self.tones = T("ones")
        self.tsq = [T("sq") for _ in range(2)]
        self.ttmp = [T("tmp") for _ in range(2)]
        self.tsg = [T("sg") for _ in range(2)]
        self.tact = [[[T("act") for _ in range(self.nt)] for _ in range(G)] for _ in range(2)]
        self.twg = [T("wg") for _ in range(2)]
        self.twu = [T("wu") for _ in range(2)]
        self.twd = [T("wd") for _ in range(2)]
        self.tpsA = [T("psA") for _ in range(2)]
        self.tpsB = [T("psB") for _ in range(2)]
        self.tpsC = [T("psC") for _ in range(2)]
        self.tpsS = [T("psS") for _ in range(2)]
        self.swg = [P.dsem("swg%d" % i) for i in range(2)]
        self.swu = [P.dsem("swu%d" % i) for i in range(2)]
        self.swd = [P.dsem("swd%d" % i) for i in range(2)]
        self.sio = [P.dsem("sio%d" % i) for i in range(9)]
        self.cnt = 0
        self.grp = 0
        P.op("pool", I("memset", self.ones[:], 1.0), writes=[self.tones])

    def rr(self):
        self.cnt += 1
        return self.cnt % 2

    def load_h(self, h_dram, mods_dram):
        P = self.P
        P.op("sp", I("dma_start", out=self.mods[:], in_=mods_dram), writes=[self.tmods], dsem=self.sio[0])
        for k in range(KC):
            P.op("sp", I("dma_start", out=self.h[:, k, :], in_=h_dram[:, k, :]),
                 writes=self.th[k], dsem=self.sio[1 + k % 8])

    def store(self, out_dram, sb, tl):
        P = self.P
        ids = []
        for k in range(KC):
            ids.append(P.op("sp", I("dma_start", out=out_dram[:, k, :], in_=sb[:, k, :]),
                            reads=tl[k], dsem=self.sio[1 + k % 8]))
        return ids

    def mod_gw(self, vg, vscale, vout):
        m = self.mods
        self.P.op("dve", I("scalar_tensor_tensor", out=m[:, vout, :], in0=m[:, vscale, :], scalar=1.0,
                           in1=m[:, vg, :], op0=ALU.add, op1=ALU.mult),
                  reads=[self.tmods], writes=[self.tmods])

    def mod_scale(self, vin, vout, c):
        m = self.mods
        self.P.op("dve", I("tensor_scalar", out=m[:, vout, :], in0=m[:, vin, :], scalar1=c, scalar2=None, op0=ALU.mult),
                  reads=[self.tmods], writes=[self.tmods])

    def rms_stats(self):
        P = self.P
        for ti, (t0, n) in enumerate(self.tiles):
            s = self.rr()
            for k in range(KC):
                q = self.rr()
                P.op("act", I("activation", out=self.sq[q][:, 0:n], in_=self.h[:, k, t0:t0 + n], func=AF.Square),
                     reads=[self.th[k][ti]], writes=[self.tsq[q]])
                P.op("pe", I("matmul", self.psS[s][:, 0:n], lhsT=self.ones[:], rhs=self.sq[q][:, 0:n],
                             start=(k == 0), stop=(k == KC - 1)),
                     reads=[self.tsq[q], self.tones], writes=[self.tpsS[s]])
            q = self.rr()
            P.op("dve", I("tensor_scalar", out=self.tmp[q][:, 0:n], in0=self.psS[s][:, 0:n],
                          scalar1=1.0 / D, scalar2=EPS, op0=ALU.mult, op1=ALU.add),
                 reads=[self.tpsS[s]], writes=[self.ttmp[q]])
            P.op("act", I("activation", out=self.tmp[q][:, 0:n], in_=self.tmp[q][:, 0:n], func=AF.Sqrt),
                 reads=[self.ttmp[q]], writes=[self.ttmp[q]])
            P.op("dve", I("reciprocal", out=self.rstd[:, t0:t0 + n], in_=self.tmp[q][:, 0:n]),
                 reads=[self.ttmp[q]], writes=[self.trstd[ti]])

    def ada_norm(self, out_sb, out_tiles, vgw, vshift, vgw_c=None, vshift_c=None):
        P = self.P
        m = self.mods
        for ti, (t0, n) in enumerate(self.tiles):
            isctx = self.Tctx and ti == self.nt - 1
            g_ = vgw_c if isctx else vgw
            s_ = vshift_c if isctx else vshift
            for k in range(KC):
                q = self.rr()
                P.op("dve", I("scalar_tensor_tensor", out=self.tmp[q][:, 0:n], in0=self.h[:, k, t0:t0 + n],
                              scalar=m[:, g_, k:k + 1], in1=self.rstd[:, t0:t0 + n], op0=ALU.mult, op1=ALU.mult),
                     reads=[self.th[k][ti], self.trstd[ti], self.tmods], writes=[self.ttmp[q]])
                if s_ is None:
                    P.op("act", I("activation", out=out_sb[:, k, t0:t0 + n], in_=self.tmp[q][:, 0:n], func=AF.Copy),
                         reads=[self.ttmp[q]], writes=[out_tiles[k][ti]])
                else:
                    P.op("act", I("activation", out=out_sb[:, k, t0:t0 + n], in_=self.tmp[q][:, 0:n],
                                  func=AF.Identity, bias=m[:, s_, k:k + 1]),
                         reads=[self.ttmp[q], self.tmods], writes=[out_tiles[k][ti]])

    def load_w(self, wg, wu, wd, g):
        P = self.P
        s = self.grp % 2
        f0 = g * G * 128
        wgs = wg.rearrange("(k p) f -> p k f", p=128)
        wus = wu.rearrange("(k p) f -> p k f", p=128)
        wds = wd.rearrange("(j p) d -> p j d", p=128)
        P.op("pool", I("dma_start", out=self.wg[s][:, :, :], in_=wgs[:, :, f0:f0 + G * 128]),
             writes=[self.twg[s]], dsem=self.swg[s])
        P.op("pool", I("dma_start", out=self.wu[s][:, :, :], in_=wus[:, :, f0:f0 + G * 128]),
             writes=[self.twu[s]], dsem=self.swu[s])
        P.op("pool", I("dma_start", out=self.wd[s][:, :, :], in_=wds[:, g * G:(g + 1) * G, :]),
             writes=[self.twd[s]], dsem=self.swd[s])
        self.grp += 1
        return s

    def ffn_p1(self, s):
        P = self.P
        for j in range(G):
            for ti, (t0, n) in enumerate(self.tiles):
                a = self.rr()
                for k in range(KC):
                    P.op("pe", I("matmul", self.psA[a][:, 0:n], lhsT=self.wg[s][:, k, j * 128:(j + 1) * 128],
                                 rhs=self.xn[:, k, t0:t0 + n], start=(k == 0), stop=(k == KC - 1)),
                         reads=[self.twg[s], self.txn[k][ti]], writes=[self.tpsA[a]])
                for k in range(KC):
                    P.op("pe", I("matmul", self.psB[a][:, 0:n], lhsT=self.wu[s][:, k, j * 128:(j + 1) * 128],
                                 rhs=self.xn[:, k, t0:t0 + n], start=(k == 0), stop=(k == KC - 1)),
                         reads=[self.twu[s], self.txn[k][ti]], writes=[self.tpsB[a]])
                P.op("act", I("activation", out=self.sg[a][:, 0:n], in_=self.psA[a][:, 0:n], func=AF.Silu),
                     reads=[self.tpsA[a]], writes=[self.tsg[a]])
                P.op("dve", I("tensor_tensor", out=self.act[s][:, j, t0:t0 + n], in0=self.sg[a][:, 0:n],
                              in1=self.psB[a][:, 0:n], op=ALU.mult),
                     reads=[self.tsg[a], self.tpsB[a]], writes=[self.tact[s][j][ti]])
                yield

    def ffn_p2(self, s, vgate, vgate_c):
        P = self.P
        m = self.mods
        for d in range(KC):
            for ti, (t0, n) in enumerate(self.tiles):
                isctx = self.Tctx and ti == self.nt - 1
                gv = vgate_c if isctx else vgate
                self.c4 = (getattr(self, "c4", 0) + 1) % 4
                pc, tpc = ((self.psC[0], self.tpsC[0]), (self.psC[1], self.tpsC[1]),
                           (self.psS[0], self.tpsS[0]), (self.psS[1], self.tpsS[1]))[self.c4]
                for j in range(G):
                    P.op("pe", I("matmul", pc[:, 0:n], lhsT=self.wd[s][:, j, d * 128:(d + 1) * 128],
                                 rhs=self.act[s][:, j, t0:t0 + n], start=(j == 0), stop=(j == G - 1)),
                         reads=[self.twd[s], self.tact[s][j][ti]], writes=[tpc])
                P.op("dve", I("scalar_tensor_tensor", out=self.h[:, d, t0:t0 + n], in0=pc[:, 0:n],
                              scalar=m[:, gv, d:d + 1], in1=self.h[:, d, t0:t0 + n], op0=ALU.mult, op1=ALU.add),
                     reads=[tpc, self.th[d][ti], self.tmods], writes=[self.th[d][ti]])
                yield

    def ffn(self, wg, wu, wd, vgw, vshift, vgate, vgw_c=None, vshift_c=None, vgate_c=None):
        self.rms_stats()
        self.ada_norm(self.xn, self.txn, vgw, vshift, vgw_c, vshift_c)
        NG = FC // G
        prev = None
        per = (KC * self.nt + G * self.nt - 1) // (G * self.nt)
        for g in range(NG):
            s = self.load_w(wg, wu, wd, g)
            g1 = self.ffn_p1(s)
            g2 = self.ffn_p2(prev, vgate, vgate_c) if prev is not None else iter(())
            for _ in g1:
                for _ in range(per):
                    next(g2, None)
            for _ in g2:
                pass
            prev = s
        for _ in self.ffn_p2(prev, vgate, vgate_c):
            pass

    def mix(self, y_d, nfc, w_d, vgate):
        P = self.P
        m = self.mods
        ws = w_d.rearrange("(j p) d -> p j d", p=128)
        bufs = [(self.wg[0], self.twg[0], self.swg[0]), (self.wu[0], self.twu[0], self.swu[0]),
                (self.wg[1], self.twg[1], self.swg[1]), (self.wu[1], self.twu[1], self.swu[1])]
        lat_tiles = [(ti, t0, n) for ti, (t0, n) in enumerate(self.tiles) if t0 < self.Tlat]
        nb = 0
        for half in range(nfc // KC):
            for k in range(KC):
                P.op("sp", I("dma_start", out=self.xn[:, k, 0:self.Tlat], in_=(y_d(half * KC + k) if callable(y_d) else y_d[:, half * KC + k, :])),
                     writes=[self.txn[k][ti] for ti, _, _ in lat_tiles], dsem=self.sio[1 + k % 8])
            for blk in range(D // (G * 128)):
                buf, tb, sb_ = bufs[nb % 4]
                nb += 1
                P.op("pool", I("dma_start", out=buf[:, :, :], in_=ws[:, half * KC:(half + 1) * KC, blk * G * 128:(blk + 1) * G * 128]),
                     writes=[tb], dsem=sb_)
                for dj in range(G):
                    d = blk * G + dj
                    for ti, t0, n in lat_tiles:
                        c = self.rr()
                        for k in range(KC):
                            P.op("pe", I("matmul", self.psC[c][:, 0:n], lhsT=buf[:, k, dj * 128:(dj + 1) * 128],
                                         rhs=self.xn[:, k, t0:t0 + n], start=(k == 0), stop=(k == KC - 1)),
                                 reads=[tb, self.txn[k][ti]], writes=[self.tpsC[c]])
                        P.op("dve", I("scalar_tensor_tensor", out=self.h[:, d, t0:t0 + n], in0=self.psC[c][:, 0:n],
                                      scalar=m[:, vgate, d:d + 1], in1=self.h[:, d, t0:t0 + n], op0=ALU.mult, op1=ALU.add),
                             reads=[self.tpsC[c], self.th[d][ti], self.tmods], writes=[self.th[d][ti]])


F32R = mybir.dt.float32r
C = 128
NCH = 34
NLAT = 32
DKS = 128 ** -0.5
NEG = -30000.0
LE, GE, GT, LT, ONES, IDENT, NEGF, NEGB, NM0 = range(9)
NCONST = 15
NSTAGE = 6
PRE_REP = 1


class RR:
    def __init__(self, items):
        self.items = items
        self.i = 0

    def get(self):
        it = self.items[self.i % len(self.items)]
        self.i += 1
        return it


def build_scan(nc, qT_d, kT_d, kM_d, vM_d, sz_d, g_d, b_d, ng_d, cst_d, y_d, NQH=4):
    P = Prog(nc)
    qT = P.sb("qT", [128, NCH * C], BF16)
    kT = P.sb("kT", [128, NCH * C], BF16)
    kM = P.sb("kM", [128, NCH, C], BF16)
    vM = P.sb("vM", [128, NCH, 2, C], BF16)
    sz = P.sb("sz", [128, NLAT, 2, C], BF16)
    gM = P.sb("gM", [128, NCH, 4], F32)
    bM = P.sb("bM", [128, NCH, 4], F32)
    oacc = P.sb("oacc", [128, NLAT, 2, C], F32)
    yout = P.sb("yout", [128, NLAT, 2, C], BF16)
    ss = P.sb("ss", [128, NLAT * 2], F32)
    cst = P.sb("cst", [128, NCONST, C], F32)
    ng = P.sb("ng", [128, C], F32)
    identr_ = P.sb("identr", [128, C], BF16)
    identr = identr_[:]
    S32 = [P.sb("S32_%d" % c, [128, C], F32) for c in range(4)]
    Sbf = [P.sb("Sbf_%d" % c, [128, C], BF16) for c in range(4)]
    tS32 = [T() for _ in range(4)]
    tSbf = [T() for _ in range(4)]
    tq, tk, tkM, tv, tsz, tg, tb, tcst, tng, tidr, tss, tyout = [T() for _ in range(12)]
    toacc = [[T() for _ in range(2)] for _ in range(NLAT)]
    dsem = [P.dsem("ld%d" % i) for i in range(12)]

    def mkpool(name, n, dt):
        return RR([(P.sb("%s%d" % (name, i), [128, C], dt), T()) for i in range(n)])

    p32 = mkpool("p32_", 32, F32)
    p32r = mkpool("p32r_", 48, BF16)
    pbf = mkpool("pbf_", 32, BF16)
    pU = mkpool("pU_", 8, BF16)
    pUt = mkpool("pUt_", 8, BF16)
    pR = mkpool("pR_", 12, BF16)
    psm = RR([(P.sb("sm%d" % i, [128, 16], F32), T()) for i in range(6)])
    banks = [P.ps("bank%d" % i, [128, 512]) for i in range(8)]
    pps = RR([(banks[i % 8][:, (i // 8) * C:(i // 8 + 1) * C], T()) for i in range(32)])

    def cs(i):
        return cst[:, i, :]

    P.op("sp", I("dma_start", out=cst[:], in_=cst_d), writes=[tcst], dsem=dsem[0])
    P.op("sp", I("dma_start", out=ng[:], in_=ng_d), writes=[tng], dsem=dsem[1])
    P.op("dve", I("tensor_copy", out=identr, in_=cs(IDENT)), reads=[tcst], writes=[tidr])

    fo = list(range(NCH))
    bo = [1, 0] + list(range(NCH - 1, 1, -1))
    out_ids = []

    for qh in range(NQH):
        P.op("sp", I("dma_start", out=qT[:], in_=qT_d[:, qh, :]), writes=[tq], dsem=dsem[2])
        P.op("sp", I("dma_start", out=kT[:], in_=kT_d[:, qh, :]), writes=[tk], dsem=dsem[3])
        P.op("sp", I("dma_start", out=kM[:], in_=kM_d[:, qh, :, :]), writes=[tkM], dsem=dsem[4])
        P.op("sp", I("dma_start", out=vM[:], in_=vM_d[:, qh, :, :, :]), writes=[tv], dsem=dsem[5])
        P.op("sp", I("dma_start", out=sz[:], in_=sz_d[:, qh, :, :, :]), writes=[tsz], dsem=dsem[6])
        P.op("sp", I("dma_start", out=gM[:], in_=g_d[:, qh, :, :]), writes=[tg], dsem=dsem[7])
        P.op("sp", I("dma_start", out=bM[:], in_=b_d[:, qh, :, :]), writes=[tb], dsem=dsem[8])
        for c in range(4):
            P.op("pool", I("memset", S32[c][:], 0.0), writes=[tS32[c]])
            P.op("pool", I("memset", Sbf[c][:], 0.0), writes=[tSbf[c]])
        first_o = [[True, True] for _ in range(NLAT)]

        def pre(d, n):
            lat = n >= 2
            kc = kT[:, n * C:(n + 1) * C]
            qc = qT[:, n * C:(n + 1) * C]
            res = {}
            kk, tkk = pps.get()
            P.op("pe", I("matmul", kk, lhsT=kc, rhs=kc, start=True, stop=True), reads=[tk], writes=[tkk])
            yield
            kks, tkks = p32.get()
            P.op("dve", I("tensor_tensor", out=kks[:], in0=kk, in1=cs(LT if d == 0 else GT), op=ALU.mult),
                 reads=[tkk, tcst], writes=[tkks])
            yield
            if lat:
                qk, tqk = pps.get()
                P.op("pe", I("matmul", qk, lhsT=kc, rhs=qc, start=True, stop=True), reads=[tk, tq], writes=[tqk])
                yield
                qks, tqks = p32.get()
                P.op("act", I("activation", out=qks[:], in_=qk, func=AF.Copy, scale=DKS), reads=[tqk], writes=[tqks])
                yield
            gcols = gM[:, n, 2 * d:2 * d + 2]
            st, tst = pps.get()
            P.op("pe", I("matmul", st[:, 0:2], lhsT=cs(LE if d == 0 else GE), rhs=gcols, start=True, stop=True),
                 reads=[tcst, tg], writes=[tst])
            P.op("pe", I("matmul", st[:, 2:4], lhsT=cs(GT if d == 0 else LT), rhs=gcols, start=True, stop=True),
                 reads=[tcst, tg], writes=[tst])
            P.op("pe", I("matmul", st[:, 4:6], lhsT=cs(ONES), rhs=gcols, start=True, stop=True),
                 reads=[tcst, tg], writes=[tst])
            yield
            sm, tsm = psm.get()
            P.op("act", I("activation", out=sm[:, 0:6], in_=st[:, 0:6], func=AF.Exp), reads=[tst], writes=[tsm])
            P.op("pool", I("tensor_scalar", out=sm[:, 6:8], in0=sm[:, 0:2], scalar1=-1.0, scalar2=0.0, op0=ALU.mult, op1=ALU.add),
                 reads=[tsm], writes=[tsm])
            P.op("pool", I("tensor_scalar", out=sm[:, 8:10], in0=sm[:, 0:2], scalar1=DKS, scalar2=0.0, op0=ALU.mult, op1=ALU.add),
                 reads=[tsm], writes=[tsm])
            P.op("pool", I("tensor_scalar", out=sm[:, 10:12], in0=bM[:, n, 2 * d:2 * d + 2], scalar1=-1.0, scalar2=0.0,
                           op0=ALU.mult, op1=ALU.add), reads=[tb], writes=[tsm])
            yield
            res["sm"] = (sm, tsm)
            ch = []
            for v in range(2):
                gx, tgx = p32.get()
                P.op("pool", I("tensor_scalar", out=gx[:], in0=cs(LE if d == 0 else GE), scalar1=gM[:, n, 2 * d + v:2 * d + v + 1],
                               scalar2=0.0, op0=ALU.mult, op1=ALU.add), reads=[tcst, tg], writes=[tgx])
                yield
                dt_, tdt = pps.get()
                P.op("pe", I("matmul", dt_, lhsT=cs(GT if d == 0 else LT), rhs=gx[:], start=True, stop=False),
                     reads=[tcst, tgx], writes=[tdt])
                P.op("pe", I("matmul", dt_, lhsT=cs(IDENT), rhs=cs(NEGF if d == 0 else NEGB), start=False, stop=True),
                     reads=[tcst], writes=[tdt])
                yield
                dec, tdec = p32.get()
                P.op("act", I("activation", out=dec[:], in_=dt_, func=AF.Exp), reads=[tdt], writes=[tdec])
                yield
                M, tM = pU.get()
                P.op("dve", I("scalar_tensor_tensor", out=M[:], in0=kks[:], scalar=bM[:, n, 2 * d + v:2 * d + v + 1], in1=dec[:],
                              op0=ALU.mult, op1=ALU.mult), reads=[tkks, tb, tdec], writes=[tM])
                yield
                c = dict(M=(M, tM))
                if lat:
                    at, tat = pbf.get()
                    P.op("pool", I("tensor_tensor", out=at[:], in0=qks[:], in1=dec[:], op=ALU.mult),
                         reads=[tqks, tdec], writes=[tat])
                    c["at"] = (at, tat)
                    yield
                kd, tkd = pbf.get()
                P.op("pool", I("tensor_scalar", out=kd[:], in0=kM[:, n, :], scalar1=sm[:, 2 + v:3 + v], scalar2=0.0,
                               op0=ALU.mult, op1=ALU.add), reads=[tkM, tsm], writes=[tkd])
                c["kd"] = (kd, tkd)
                yield
                ch.append(c)
            for c in ch:
                U, tU = c["M"]
                ut_ps, tut = pps.get()
                P.op("pe", I("matmul", ut_ps, lhsT=U[:], rhs=identr, start=True, stop=True), reads=[tU, tidr], writes=[tut])
                Gt, tGt = pUt.get()
                P.op("dve", I("tensor_tensor", out=Gt[:], in0=ut_ps, in1=cs(IDENT), op=ALU.add), reads=[tut, tcst], writes=[tGt])
                yield
                X, tX = p32r.get()
                Xt, tXt = p32r.get()
                x0, tx0 = p32.get()
                P.op("pool", I("tensor_tensor", out=x0[:], in0=U[:], in1=cs(NM0), op=ALU.mult), reads=[tU, tcst], writes=[tx0])
                P.op("dve", I("tensor_tensor", out=X[:], in0=x0[:], in1=cs(IDENT), op=ALU.add), reads=[tx0, tcst], writes=[tX])
                yield
                x1, tx1 = p32.get()
                P.op("pool", I("tensor_tensor", out=x1[:], in0=Gt[:], in1=cs(NM0), op=ALU.mult), reads=[tGt, tcst], writes=[tx1])
                P.op("dve", I("tensor_tensor", out=Xt[:], in0=x1[:], in1=cs(IDENT), op=ALU.add), reads=[tx1, tcst], writes=[tXt])
                yield
                c["Gt"], c["X"], c["Xt"] = (Gt, tGt), (X, tX), (Xt, tXt)
            for lv in range(1, 7):
                last = lv == 6
                for c in ch:
                    Gt, tGt = c["Gt"]
                    X, tX = c["X"]
                    y_ps, ty = pps.get()
                    P.op("pe", I("matmul", y_ps, lhsT=Gt[:], rhs=X[:], start=True, stop=True),
                         reads=[tGt, tX], writes=[ty])
                    W, tW = p32r.get()
                    P.op("dve", I("tensor_tensor", out=W[:], in0=y_ps, in1=cs(NM0 + lv), op=ALU.mult),
                         reads=[ty, tcst], writes=[tW])
                    c["W"] = (W, tW)
                    yield
                for c in ch:
                    X, tX = c["X"]
                    Xt, tXt = c["Xt"]
                    W, tW = c["W"]
                    x_ps, tx = pps.get()
                    P.op("pe", I("matmul", x_ps, lhsT=Xt[:], rhs=W[:], start=True, stop=True),
                         reads=[tXt, tW], writes=[tx])
                    if not last:
                        xt_ps, txt = pps.get()
                        P.op("pe", I("matmul", xt_ps, lhsT=W[:], rhs=Xt[:], start=True, stop=True),
                             reads=[tW, tXt], writes=[txt])
                    nX, tnX = pR.get() if last else p32r.get()
                    P.op("act", I("activation", out=nX[:], in_=x_ps, func=AF.Copy), reads=[tx], writes=[tnX])
                    c["X"] = (nX, tnX)
                    if not last:
                        nXt, tnXt = p32r.get()
                        P.op("act" if lv % 2 else "dve", (I("activation", out=nXt[:], in_=xt_ps, func=AF.Copy) if lv % 2
                                                       else I("tensor_copy", out=nXt[:], in_=xt_ps)), reads=[txt], writes=[tnXt])
                        c["Xt"] = (nXt, tnXt)
                    yield
            for c in ch:
                c["R"] = c["X"]
            res["ch"] = ch
            return res

        def seq(d, n, res):
            lat = n >= 2
            kc = kT[:, n * C:(n + 1) * C]
            qc = qT[:, n * C:(n + 1) * C]
            sm, tsm = res["sm"]
            for v in range(2):
                ci = 2 * d + v
                c = res["ch"][v]
                ps1, t1 = pps.get()
                P.op("pe", I("matmul", ps1, lhsT=kc, rhs=Sbf[ci][:], start=True, stop=True), reads=[tk, tSbf[ci]], writes=[t1])
                if lat:
                    ps2, t2 = pps.get()
                    P.op("pe", I("matmul", ps2, lhsT=qc, rhs=Sbf[ci][:], start=True, stop=True), reads=[tq, tSbf[ci]], writes=[t2])
                yield
                r, tr = p32r.get()
                P.op("dve", I("scalar_tensor_tensor", out=r[:], in0=ps1, scalar=sm[:, 6 + v:7 + v], in1=vM[:, n, v, :],
                              op0=ALU.mult, op1=ALU.add), reads=[t1, tsm, tv], writes=[tr])
                yield
                R, tR = c["R"]
                ps3, t3 = pps.get()
                P.op("pe", I("matmul", ps3, lhsT=R[:], rhs=r[:], start=True, stop=True), reads=[tR, tr], writes=[t3])
                yield
                vn, tvn = pbf.get()
                P.op("act", I("activation", out=vn[:], in_=ps3, func=AF.Copy, scale=bM[:, n, ci:ci + 1]),
                     reads=[t3, tb], writes=[tvn])
                yield
                kd, tkd = c["kd"]
                ps5, t5 = pps.get()
                P.op("pe", I("matmul", ps5, lhsT=kd[:], rhs=vn[:], start=True, stop=True), reads=[tkd, tvn], writes=[t5])
                if lat:
                    at, tat = c["at"]
                    ps4, t4 = pps.get()
                    P.op("pe", I("matmul", ps4, lhsT=at[:], rhs=vn[:], start=True, stop=True), reads=[tat, tvn], writes=[t4])
                yield
                P.op("dve", I("scalar_tensor_tensor", out=S32[ci][:], in0=S32[ci][:], scalar=sm[:, 4 + v:5 + v], in1=ps5,
                              op0=ALU.mult, op1=ALU.add), reads=[tS32[ci], tsm, t5], writes=[tS32[ci]])
                P.op("act", I("activation", out=Sbf[ci][:], in_=S32[ci][:], func=AF.Copy), reads=[tS32[ci]], writes=[tSbf[ci]])
                yield
                if lat:
                    l = n - 2
                    ov = oacc[:, l, v, :]
                    if first_o[l][v]:
                        first_o[l][v] = False
                        P.op("dve", I("tensor_scalar", out=ov, in0=ps2, scalar1=sm[:, 8 + v:9 + v], scalar2=None, op0=ALU.mult),
                             reads=[t2, tsm], writes=[toacc[l][v]])
                    else:
                        P.op("dve", I("scalar_tensor_tensor", out=ov, in0=ps2, scalar=sm[:, 8 + v:9 + v], in1=ov,
                                      op0=ALU.mult, op1=ALU.add), reads=[t2, tsm, toacc[l][v]], writes=[toacc[l][v]])
                    P.op("dve", I("tensor_tensor", out=ov, in0=ov, in1=ps4, op=ALU.add),
                         reads=[toacc[l][v], t4], writes=[toacc[l][v]])
                    yield

        def drive(gens, reps=None):
            rets = [None] * len(gens)
            live = list(range(len(gens)))
            reps = reps or [1] * len(gens)
            while live:
                for gi in list(live):
                    for _ in range(reps[gi]):
                        try:
                            next(gens[gi])
                        except StopIteration as e:
                            rets[gi] = e.value
                            live.remove(gi)
                            break
            return rets

        cur = drive([pre(0, fo[0]), pre(1, bo[0])])
        for s in range(NCH):
            gens = [seq(0, fo[s], cur[0]), seq(1, bo[s], cur[1])]
            if s + 1 < NCH:
                gens += [pre(0, fo[s + 1]), pre(1, bo[s + 1])]
            r = drive(gens, [1, 1, PRE_REP, PRE_REP][:len(gens)])
            if s + 1 < NCH:
                cur = r[2:4]

        jk, tjk = p32.get()
        for l in range(NLAT):
            for v in range(2):
                P.op("act", I("activation", out=jk[:], in_=oacc[:, l, v, :], func=AF.Square,
                              accum_out=ss[:, 2 * l + v:2 * l + v + 1]), reads=[toacc[l][v]], writes=[tjk, tss])
        P.op("dve", I("tensor_scalar", out=ss[:], in0=ss[:], scalar1=1.0 / C, scalar2=1e-6, op0=ALU.mult, op1=ALU.add),
             reads=[tss], writes=[tss])
        P.op("act", I("activation", out=ss[:], in_=ss[:], func=AF.Sqrt), reads=[tss], writes=[tss])
        P.op("dve", I("reciprocal", out=ss[:], in_=ss[:]), reads=[tss], writes=[tss])
        for l in range(NLAT):
            for v in range(2):
                gz, tgz = p32.get()
                P.op("pool", I("tensor_tensor", out=gz[:], in0=sz[:, l, v, :], in1=ng[:], op=ALU.mult),
                     reads=[tsz, tng], writes=[tgz])
                P.op("dve", I("scalar_tensor_tensor", out=yout[:, l, v, :], in0=oacc[:, l, v, :],
                              scalar=ss[:, 2 * l + v:2 * l + v + 1], in1=gz[:], op0=ALU.mult, op1=ALU.mult),
                     reads=[toacc[l][v], tss, tgz], writes=[tyout])
        out_ids.append(P.op("sp", I("dma_start", out=y_d[:, qh, :, :, :], in_=yout[:]), reads=[tyout], dsem=dsem[9]))
    P.op("sp", I("nop"), after=out_ids)
    return P


TL = 1028
TCX = 68
TT = TL + TCX
NOUT = 1088
NCHK = 97
TILES = [(0, 512), (512, 512), (1024, TT - 1024)]


def build_inproj(nc, uT_d, w_d, cw_d, gbv_d, qk_d, v_d, sz_d, gb_d, only=None):
    P = Prog(nc)
    uT = P.sb("uT", [128, KC, TT], BF16)
    cw = P.sb("cw", [128, 64, 5], F32)
    gbv = P.sb("gbv", [128, 4], F32)
    ones = P.sb("ones", [128, 128], F32)
    wb = [P.sb("wb%d" % i, [128, KC, 512], BF16) for i in range(2)]
    pre = [P.sb("pre%d" % i, [128, TT], F32) for i in range(2)]
    acc = [P.sb("acc%d" % i, [128, TT], F32) for i in range(2)]
    sil = [P.sb("sil%d" % i, [128, NOUT], F32) for i in range(2)]
    sq = [P.sb("sq%d" % i, [128, NOUT], F32) for i in range(2)]
    rs = [P.sb("rs%d" % i, [128, NOUT], F32) for i in range(2)]
    ob = [P.sb("ob%d" % i, [128, NOUT], BF16) for i in range(3)]
    gbo = P.sb("gbo", [128, NOUT], F32)
    tuT, tcw, tgbv, tones, tgbo = [T() for _ in range(5)]
    twb = [T() for _ in range(2)]
    tpre, tacc, tsil, tsq, trs = [[T() for _ in range(2)] for _ in range(5)]
    tob = [T() for _ in range(3)]
    psA = [P.ps("psA%d" % i, [128, 512]) for i in range(6)]
    tpsA = [T() for _ in range(6)]
    psS = [P.ps("psS%d" % i, [128, 512]) for i in range(2)]
    tpsS = [T() for _ in range(2)]
    sw = [P.dsem("sw%d" % i) for i in range(2)]
    sio = [P.dsem("sio%d" % i) for i in range(8)]
    so = [P.dsem("so%d" % i) for i in range(3)]
    P.op("pool", I("memset", ones[:], 1.0), writes=[tones])
    P.op("sp", I("dma_start", out=cw[:], in_=cw_d), writes=[tcw], dsem=sio[0])
    P.op("sp", I("dma_start", out=gbv[:, 0:2], in_=gbv_d), writes=[tgbv], dsem=sio[1])
    for k in range(KC):
        P.op("sp", I("dma_start", out=uT[:, k, :], in_=uT_d[:, k, :]), writes=[tuT], dsem=sio[2 + k % 4])
    P.op("act", I("activation", out=gbv[:, 2:3], in_=gbv[:, 1:2], func=AF.Exp), reads=[tgbv], writes=[tgbv])
    P.op("dve", I("tensor_scalar", out=gbv[:, 2:3], in0=gbv[:, 2:3], scalar1=-1.0, scalar2=None, op0=ALU.mult),
         reads=[tgbv], writes=[tgbv])
    ws = w_d.rearrange("(k p) f -> p k f", p=128)
    out_ids = []
    cnt = [0, 0, 0, 0]

    def nxt(i, m):
        cnt[i] += 1
        return cnt[i] % m

    ngrp = (NCHK + 3) // 4

    def load_grp(gi):
        c0_ = gi * 4
        ncol_ = min(4, NCHK - c0_) * 128
        P.op("pool", I("dma_start", out=wb[gi % 2][:, :, 0:ncol_], in_=ws[:, :, c0_ * 128:c0_ * 128 + ncol_]),
             writes=[twb[gi % 2]], dsem=sw[gi % 2])

    load_grp(0)
    for gi in range(ngrp):
        c0 = gi * 4
        ncol = min(4, NCHK - c0) * 128
        s = gi % 2
        if gi + 1 < ngrp:
            load_grp(gi + 1)
        for cj in range(ncol // 128):
            c = c0 + cj
            if only is not None and c not in only:
                continue
            pi = nxt(0, 2)
            for ti, (t0, n) in enumerate(TILES):
                a = nxt(1, 6)
                for k in range(KC):
                    P.op("pe", I("matmul", psA[a][:, 0:n], lhsT=wb[s][:, k, cj * 128:(cj + 1) * 128], rhs=uT[:, k, t0:t0 + n],
                                 start=(k == 0), stop=(k == KC - 1)), reads=[twb[s], tuT], writes=[tpsA[a]])
                P.op("act", I("activation", out=pre[pi][:, t0:t0 + n], in_=psA[a][:, 0:n], func=AF.Copy),
                     reads=[tpsA[a]], writes=[tpre[pi]])
            oi = nxt(2, 3)
            if c < 64:
                ai = pi
                for (o0, n) in ((2, 1024), (TL + 2, 64)):
                    for w in range(5):
                        src = pre[pi][:, o0 - 2 + w:o0 - 2 + w + n]
                        if w == 0:
                            P.op("dve", I("tensor_scalar", out=acc[ai][:, o0:o0 + n], in0=src, scalar1=cw[:, c, 0:1], scalar2=None,
                                          op0=ALU.mult), reads=[tpre[pi], tcw], writes=[tacc[ai]])
                        else:
                            P.op("dve", I("scalar_tensor_tensor", out=acc[ai][:, o0:o0 + n], in0=src, scalar=cw[:, c, w:w + 1],
                                          in1=acc[ai][:, o0:o0 + n], op0=ALU.mult, op1=ALU.add),
                                 reads=[tpre[pi], tcw, tacc[ai]], writes=[tacc[ai]])
                if c < 32:
                    P.op("act", I("activation", out=sil[ai][:, 0:1024], in_=acc[ai][:, 2:1026], func=AF.Silu),
                         reads=[tacc[ai]], writes=[tsil[ai]])
                    P.op("act", I("activation", out=sil[ai][:, 1024:1088], in_=acc[ai][:, TL + 2:TL + 66], func=AF.Silu),
                         reads=[tacc[ai]], writes=[tsil[ai]])
                    P.op("pool", I("tensor_tensor", out=sq[ai][:], in0=sil[ai][:], in1=sil[ai][:], op=ALU.mult),
                         reads=[tsil[ai]], writes=[tsq[ai]])
                    for (t0, n) in ((0, 512), (512, 512), (1024, 64)):
                        si = nxt(3, 2)
                        P.op("pe", I("matmul", psS[si][:, 0:n], lhsT=ones[:], rhs=sq[ai][:, t0:t0 + n], start=True, stop=True),
                             reads=[tones, tsq[ai]], writes=[tpsS[si]])
                        P.op("dve", I("tensor_scalar", out=rs[ai][:, t0:t0 + n], in0=psS[si][:, 0:n], scalar1=1e-6, scalar2=None,
                                      op0=ALU.add), reads=[tpsS[si]], writes=[trs[ai]])
                    P.op("act", I("activation", out=rs[ai][:], in_=rs[ai][:], func=AF.Ln), reads=[trs[ai]], writes=[trs[ai]])
                    P.op("act", I("activation", out=rs[ai][:], in_=rs[ai][:], func=AF.Exp, scale=-0.5), reads=[trs[ai]], writes=[trs[ai]])
                    P.op("pool", I("tensor_tensor", out=ob[oi][:], in0=sil[ai][:], in1=rs[ai][:], op=ALU.mult),
                         reads=[tsil[ai], trs[ai]], writes=[tob[oi]])
                    out_ids.append(P.op("sp", I("dma_start", out=qk_d[:, c, :], in_=ob[oi][:]), reads=[tob[oi]], dsem=so[oi]))
                else:
                    P.op("act", I("activation", out=ob[oi][:, 0:1024], in_=acc[ai][:, 2:1026], func=AF.Silu),
                         reads=[tacc[ai]], writes=[tob[oi]])
                    P.op("act", I("activation", out=ob[oi][:, 1024:1088], in_=acc[ai][:, TL + 2:TL + 66], func=AF.Silu),
                         reads=[tacc[ai]], writes=[tob[oi]])
                    out_ids.append(P.op("sp", I("dma_start", out=v_d[:, c - 32, :], in_=ob[oi][:]), reads=[tob[oi]], dsem=so[oi]))
            elif c < 96:
                P.op("act", I("activation", out=ob[oi][:, 0:1024], in_=pre[pi][:, 2:1026], func=AF.Silu),
                     reads=[tpre[pi]], writes=[tob[oi]])
                out_ids.append(P.op("sp", I("dma_start", out=sz_d[:, c - 64, :], in_=ob[oi][:, 0:1024]), reads=[tob[oi]], dsem=so[oi]))
            else:
                for (o0, i0, n) in ((0, 2, 1024), (1024, TL + 2, 64)):
                    P.op("act", I("activation", out=gbo[0:64, o0:o0 + n], in_=pre[pi][0:64, i0:i0 + n], func=AF.Sigmoid),
                         reads=[tpre[pi]], writes=[tgbo])
                    P.op("act", I("activation", out=gbo[64:128, o0:o0 + n], in_=pre[pi][64:128, i0:i0 + n], func=AF.Exp,
                                  bias=gbv[64:128, 0:1]), reads=[tpre[pi], tgbv], writes=[tgbo])
                    P.op("act", I("activation", out=gbo[64:128, o0:o0 + n], in_=gbo[64:128, o0:o0 + n], func=AF.Ln, bias=1.0),
                         reads=[tgbo], writes=[tgbo])
                    P.op("dve", I("tensor_scalar", out=gbo[64:128, o0:o0 + n], in0=gbo[64:128, o0:o0 + n],
                                  scalar1=gbv[64:128, 2:3], scalar2=None, op0=ALU.mult), reads=[tgbo, tgbv], writes=[tgbo])
                out_ids.append(P.op("sp", I("dma_start", out=gb_d, in_=gbo[:]), reads=[tgbo], dsem=sio[6]))
    P.op("sp", I("nop"), after=out_ids)
    return P


L = 4096
TB = 32
FG = 512
TW = 256
NTW = L // TW


def build_fourier(nc, uT_d, cc_d, cs_d, y_d):
    P = Prog(nc)
    uT = P.sb("uT", [128, 4, L], BF16)
    cc = P.sb("cc", [128, 2, 4, FG], BF16)
    AB = P.sb("AB", [128, 2, TB, FG], BF16)
    cs = [P.sb("cs%d" % i, [128, 2, TB, TW], BF16) for i in range(2)]
    yo = [P.sb("yo%d" % i, [128, TW], BF16) for i in range(4)]
    tuT, tcc = T(), T()
    tAB = [[T() for _ in range(TB)] for _ in range(2)]
    tcs = [T() for _ in range(2)]
    tyo = [T() for _ in range(4)]
    psA = [P.ps("psA%d" % i, [128, 512]) for i in range(4)]
    tpsA = [T() for _ in range(4)]
    psY = [P.ps("psY%d" % i, [128, 512]) for i in range(4)]
    tpsY = [T() for _ in range(4)]
    sio = [P.dsem("sio%d" % i) for i in range(6)]
    scs = [P.dsem("scs%d" % i) for i in range(2)]
    so = [P.dsem("so%d" % i) for i in range(4)]
    P.op("sp", I("dma_start", out=cc[:], in_=cc_d), writes=[tcc], dsem=sio[0])
    for k in range(4):
        P.op("sp", I("dma_start", out=uT[:, k, :], in_=uT_d[:, k, :]), writes=[tuT], dsem=sio[1 + k])
    n = 0
    for tb in range(TB):
        for j in range(2):
            a = n % 4
            n += 1
            for k in range(4):
                P.op("pe", I("matmul", psA[a][:], lhsT=uT[:, k, tb * 128:(tb + 1) * 128], rhs=cc[:, j, k, :],
                             start=(k == 0), stop=(k == 3)), reads=[tuT, tcc], writes=[tpsA[a]])
            if j == 0:
                P.op("act", I("activation", out=AB[:, j, tb, :], in_=psA[a][:], func=AF.Copy), reads=[tpsA[a]], writes=[tAB[j][tb]])
            else:
                P.op("dve", I("tensor_copy", out=AB[:, j, tb, :], in_=psA[a][:]), reads=[tpsA[a]], writes=[tAB[j][tb]])
    out_ids = []
    m = 0
    for tw in range(NTW):
        s = tw % 2
        P.op("pool", I("dma_start", out=cs[s][:], in_=cs_d[tw]), writes=[tcs[s]], dsem=scs[s])
        for c in range(4):
            a = m % 4
            m += 1
            i = 0
            for j in range(2):
                for tb in range(TB):
                    P.op("pe", I("matmul", psY[a][:, 0:TW], lhsT=AB[:, j, tb, c * 128:(c + 1) * 128], rhs=cs[s][:, j, tb, :],
                                 start=(i == 0), stop=(i == 2 * TB - 1)), reads=[tAB[j][tb], tcs[s]], writes=[tpsY[a]])
                    i += 1
            if a % 2 == 0:
                P.op("act", I("activation", out=yo[a][:], in_=psY[a][:, 0:TW], func=AF.Copy), reads=[tpsY[a]], writes=[tyo[a]])
            else:
                P.op("dve", I("tensor_copy", out=yo[a][:], in_=psY[a][:, 0:TW]), reads=[tpsY[a]], writes=[tyo[a]])
            out_ids.append(P.op("sp", I("dma_start", out=y_d[:, c, tw * TW:(tw + 1) * TW], in_=yo[a][:]),
                                reads=[tyo[a]], dsem=so[a]))
    P.op("sp", I("nop"), after=out_ids)
    return P


NMC = 18432 // 8
MT = [(0, 512), (512, 512), (1024, 512), (1536, 512), (2048, 256)]


def build_mod(nc, ct_d, w_d, b_d, m_d):
    P = Prog(nc)
    ct = P.sb("ct", [128, KC, 3], F32)
    sc = P.sb("sc", [128, KC, 3], F32)
    bs = P.sb("bs", [3, 2, NMC], F32)
    mo = P.sb("mo", [3, 2, NMC], F32)
    wt = [P.sb("wt%d" % i, [128, NMC], F32) for i in range(4)]
    tct, tsc, tbs, tmo = [T() for _ in range(4)]
    twt = [T() for _ in range(4)]
    ps = [P.ps("ps%d" % i, [128, 512]) for i in range(5)]
    tps = [T() for _ in range(5)]
    sio = [P.dsem("sio%d" % i) for i in range(3)]
    sw = [P.dsem("sw%d" % i) for i in range(4)]
    P.op("sp", I("dma_start", out=ct[:], in_=ct_d), writes=[tct], dsem=sio[0])
    P.op("sp", I("dma_start", out=bs[:], in_=b_d), writes=[tbs], dsem=sio[1])
    P.op("act", I("activation", out=sc[:], in_=ct[:], func=AF.Silu), reads=[tct], writes=[tsc])
    n = 0
    for l in range(2):
        for k in range(KC):
            s = n % 4
            n += 1
            P.op("sp", I("dma_start", out=wt[s][:], in_=w_d[l, k * 128:(k + 1) * 128, :]), writes=[twt[s]], dsem=sw[s])
            for i, (c0, w) in enumerate(MT):
                P.op("pe", I("matmul", ps[i][0:3, 0:w], lhsT=sc[:, k, :], rhs=wt[s][:, c0:c0 + w], start=(k == 0), stop=(k == KC - 1)),
                     reads=[tsc, twt[s]], writes=[tps[i]])
        for i, (c0, w) in enumerate(MT):
            P.op("dve", I("tensor_tensor", out=mo[:, l, c0:c0 + w], in0=ps[i][0:3, 0:w], in1=bs[:, l, c0:c0 + w], op=ALU.add),
                 reads=[tps[i], tbs], writes=[tmo])
    o = P.op("sp", I("dma_start", out=m_d, in_=mo[:]), reads=[tmo], dsem=sio[2])
    P.op("sp", I("nop"), after=[o])
    return P


import ml_dtypes
BFN = ml_dtypes.bfloat16
NCORES = 8
_cache = {}


def _dt(nc, n, s, t, k="ExternalInput"):
    return nc.dram_tensor(n, list(s), t, kind=k).ap()


def _finish(P):
    P.emit()
    P.close()


def prog_mod():
    nc = bass.Bass("TRN2", target_bir_lowering=False)
    ct = _dt(nc, "ct", [128, KC, 3], F32); w = _dt(nc, "w", [2, 2048, NMC], F32); b = _dt(nc, "b", [3, 2, NMC], F32)
    m = _dt(nc, "m", [3, 2, NMC], F32, "ExternalOutput")
    _finish(build_mod(nc, ct, w, b, m))
    return nc


def _ffn_w(nc, tag):
    return (_dt(nc, "wg" + tag, [D, DFF], F32), _dt(nc, "wu" + tag, [D, DFF], F32), _dt(nc, "wd" + tag, [DFF, D], F32))


def prog_l1():
    nc = bass.Bass("TRN2", target_bir_lowering=False)
    NV = 18
    hin = _dt(nc, "hin", [128, KC, 1088], F32); mods = _dt(nc, "mods", [128, NV, KC], F32)
    W = _ffn_w(nc, "0")
    hout = _dt(nc, "hout", [128, KC, 1088], F32, "ExternalOutput")
    uout = _dt(nc, "uout", [128, KC, 1088], BF16, "ExternalOutput")
    P = Prog(nc)
    S = TokStage(P, nc, 1024, 64, NV)
    S.load_h(hin, mods)
    S.mod_gw(0, 2, 7); S.mod_gw(0, 5, 8); S.mod_scale(3, 9, 0.5); S.mod_scale(6, 10, 0.5)
    S.mod_gw(11, 13, 16); S.mod_gw(11, 15, 17)
    S.ffn(*W, 7, 1, 9, 8, 4, 10)
    S.rms_stats()
    S.ada_norm(S.xn, S.txn, 16, 12, 17, 14)
    ids = S.store(hout, S.h, S.th) + S.store(uout, S.xn, S.txn)
    P.op("sp", I("nop"), after=ids)
    _finish(P)
    return nc


def prog_l3():
    nc = bass.Bass("TRN2", target_bir_lowering=False)
    NV = 17
    hin = _dt(nc, "hin", [128, KC, 1024], F32); mods = _dt(nc, "mods", [128, NV, KC], F32)
    yin = _dt(nc, "yin", [128, 32, 1024], BF16); wo = _dt(nc, "wo", [4096, D], F32)
    W0 = _ffn_w(nc, "0"); W1 = _ffn_w(nc, "1")
    hout = _dt(nc, "hout", [128, KC, 1024], F32, "ExternalOutput")
    uout = _dt(nc, "uout", [128, KC, 1024], BF16, "ExternalOutput")
    P = Prog(nc)
    S = TokStage(P, nc, 1024, 0, NV)
    S.load_h(hin, mods)
    S.mod_gw(1, 3, 5); S.mod_scale(4, 6, 0.5); S.mod_gw(7, 9, 11); S.mod_scale(10, 12, 0.5); S.mod_gw(13, 15, 16)
    S.mix(yin, 32, wo, 0)
    S.ffn(*W0, 5, 2, 6)
    S.ffn(*W1, 11, 8, 12)
    S.rms_stats()
    S.ada_norm(S.xn, S.txn, 16, 14)
    ids = S.store(hout, S.h, S.th) + S.store(uout, S.xn, S.txn)
    P.op("sp", I("nop"), after=ids)
    _finish(P)
    return nc


def prog_l5():
    nc = bass.Bass("TRN2", target_bir_lowering=False)
    NV = 8
    hin = _dt(nc, "hin", [128, KC, 1024], F32); mods = _dt(nc, "mods", [128, NV, KC], F32)
    yin = _dt(nc, "yin", [128, 16, 1024], BF16); wo = _dt(nc, "wo", [2048, D], F32)
    W0 = _ffn_w(nc, "0")
    hout = _dt(nc, "hout", [128, KC, 1024], F32, "ExternalOutput")
    P = Prog(nc)
    S = TokStage(P, nc, 1024, 0, NV)
    S.load_h(hin, mods)
    S.mod_gw(1, 3, 5); S.mod_scale(4, 6, 0.5)
    S.mix(yin, 16, wo, 0)
    S.ffn(*W0, 5, 2, 6)
    S.rms_stats()
    S.ada_norm(S.h, S.th, 7, None)
    ids = S.store(hout, S.h, S.th)
    P.op("sp", I("nop"), after=ids)
    _finish(P)
    return nc


def prog_inproj():
    nc = bass.Bass("TRN2", target_bir_lowering=False)
    uT = _dt(nc, "uT", [128, KC, TT], BF16); w = _dt(nc, "w", [2048, 12416], F32)
    cw = _dt(nc, "cw", [128, 64, 5], F32); gbv = _dt(nc, "gbv", [128, 2], F32)
    qk = _dt(nc, "qk", [128, 32, NOUT], BF16, "ExternalOutput"); v = _dt(nc, "v", [128, 32, NOUT], BF16, "ExternalOutput")
    sz = _dt(nc, "sz", [128, 32, 1024], BF16, "ExternalOutput"); gb = _dt(nc, "gb", [128, NOUT], F32, "ExternalOutput")
    _finish(build_inproj(nc, uT, w, cw, gbv, qk, v, sz, gb))
    return nc


def prog_scan():
    nc = bass.Bass("TRN2", target_bir_lowering=False)
    qT = _dt(nc, "qT", [128, 4, 4352], BF16); kT = _dt(nc, "kT", [128, 4, 4352], BF16)
    kM = _dt(nc, "kM", [128, 4, 34, 128], BF16); vM = _dt(nc, "vM", [128, 4, 34, 2, 128], BF16)
    sz = _dt(nc, "sz", [128, 4, 32, 2, 128], BF16)
    g = _dt(nc, "g", [128, 4, 34, 4], F32); b = _dt(nc, "b", [128, 4, 34, 4], F32)
    ng = _dt(nc, "ng", [128, 128], F32); cst = _dt(nc, "cst", [128, NCONST, 128], F32)
    y = _dt(nc, "y", [128, 4, 32, 2, 128], BF16, "ExternalOutput")
    _finish(build_scan(nc, qT, kT, kM, vM, sz, g, b, ng, cst, y, NQH=4))
    return nc


def prog_fourier():
    nc = bass.Bass("TRN2", target_bir_lowering=False)
    uT = _dt(nc, "uT", [128, 4, L], BF16); cc = _dt(nc, "cc", [128, 2, 4, FG], BF16); cs = _dt(nc, "cs", [16, 128, 2, 32, 256], BF16)
    y = _dt(nc, "y", [128, 4, L], BF16, "ExternalOutput")
    _finish(build_fourier(nc, uT, cc, cs, y))
    return nc


def fm(a):
    t, d = a.shape
    return np.ascontiguousarray(a.reshape(t, d // 128, 128).transpose(2, 1, 0))


def tm(a):
    p, n, t = a.shape
    return np.ascontiguousarray(a.transpose(2, 1, 0)).reshape(t, n * 128)


def vec(v):
    return v.reshape(KC, 128).T


def scan_consts():
    m = np.arange(128)[:, None]; i = np.arange(128)[None, :]
    c = np.zeros((128, NCONST, 128), np.float32)
    c[:, LE] = m <= i; c[:, GE] = m >= i; c[:, GT] = m > i; c[:, LT] = m < i
    c[:, ONES] = 1; c[:, IDENT] = m == i
    c[:, NEGF] = np.where(i < m, NEG, 0); c[:, NEGB] = np.where(i > m, NEG, 0)
    for lv in range(7):
        c[:, NM0 + lv] = -(((m >> (lv + 1)) == (i >> (lv + 1))) & ((m >> lv) != (i >> lv))).astype(np.float32) + (0 if lv == 0 else (m == i))
    return c


def fourier_consts():
    c = np.arange(512)
    ang = 2 * np.pi * ((c[:, None] * c[None, :]) % 512) / 512
    CC = np.stack([np.cos(ang), np.sin(ang)], 0) / np.sqrt(512.0)
    cc = np.ascontiguousarray(CC.reshape(2, 4, 128, 512).transpose(2, 0, 1, 3)).astype(BFN)
    t = np.arange(4096, dtype=np.int64)
    ang = 2 * np.pi * ((t[:, None] * t[None, :]) % 4096) / 4096
    tab = np.stack([np.cos(ang), -np.sin(ang)], 0).astype(np.float32) / 64.0
    cs = np.ascontiguousarray(tab.reshape(2, 32, 128, 16, 256).transpose(3, 2, 0, 1, 4)).astype(BFN)
    return cc, cs


def halo(a, lo, hi):
    n = a.shape[0]
    out = np.zeros((hi - lo,) + a.shape[1:], a.dtype)
    s, e = max(lo, 0), min(hi, n)
    out[s - lo:e - lo] = a[s:e]
    return out


def scan_inputs(q, k, v, sz, g, beta, hg):
    qs = q[:, 4 * hg:4 * hg + 4]; ks = k[:, 4 * hg:4 * hg + 4]
    vs = v[:, 8 * hg:8 * hg + 8].reshape(34, 128, 4, 2, 128)
    szs = sz[:, 8 * hg:8 * hg + 8].reshape(32, 128, 4, 2, 128)
    gs = g[:, :, 8 * hg:8 * hg + 8].reshape(34, 128, 2, 4, 2)
    bs = beta[:, :, 8 * hg:8 * hg + 8].reshape(34, 128, 2, 4, 2)
    return {
        "qT": np.ascontiguousarray(qs.transpose(2, 1, 0)),
        "kT": np.ascontiguousarray(ks.transpose(2, 1, 0)),
        "kM": np.ascontiguousarray(ks.reshape(34, 128, 4, 128).transpose(1, 2, 0, 3)),
        "vM": np.ascontiguousarray(vs.transpose(1, 2, 0, 3, 4)),
        "sz": np.ascontiguousarray(szs.transpose(1, 2, 0, 3, 4)),
        "g": np.ascontiguousarray(gs.transpose(1, 3, 0, 2, 4)).reshape(128, 4, 34, 4),
        "b": np.ascontiguousarray(bs.transpose(1, 3, 0, 2, 4)).reshape(128, 4, 34, 4),
    }


def _run(name, builder, in_maps):
    if name not in _cache:
        _cache[name] = builder()
    res = run_bass_kernel_spmd(_cache[name], in_maps, core_ids=list(range(NCORES)))
    return res.results


def kernel(x, c, ctx, c_ctx, norm_g, mod_w, mod_b, ffn_w_gate, ffn_w_up, ffn_w_down, dn_w_in, dn_conv_w,
           dn_a_log, dn_dt_bias, dn_norm_g, dn_w_out, fn_w_out, final_norm_g):
    f32 = np.float32
    A = lambda a: np.asarray(a, dtype=f32)
    x, c, ctx, c_ctx, norm_g, mod_b = A(x), A(c), A(ctx), A(c_ctx), A(norm_g), A(mod_b)
    mod_w = np.asarray(mod_w); ffn_w_gate = np.asarray(ffn_w_gate); ffn_w_up = np.asarray(ffn_w_up)
    ffn_w_down = np.asarray(ffn_w_down)
    cores = range(NCORES)
    cond = np.stack([c[0], c[1], c_ctx], 0)
    ct = np.ascontiguousarray(cond.reshape(3, KC, 128).transpose(2, 1, 0))
    ims = []
    for ci in cores:
        sl = slice(ci * NMC, (ci + 1) * NMC)
        ims.append({"ct": ct, "w": np.ascontiguousarray(mod_w[:, :, sl]),
                    "b": np.ascontiguousarray(np.broadcast_to(mod_b[None, :, sl], (3, 2, NMC)))})
    r = _run("mod", prog_mod, ims)
    m = np.concatenate([r[ci]["m"] for ci in cores], 2)
    M = lambda l, row, j: m[row, l, j * 2048:(j + 1) * 2048]

    ims = []
    for ci in cores:
        b, q = ci // 4, ci % 4
        a = np.concatenate([x[b, q * 1024:(q + 1) * 1024], ctx[b, q * 64:(q + 1) * 64]], 0)
        md = np.zeros((128, 18, KC), f32)
        md[:, 0] = vec(norm_g[0, 0]); md[:, 1] = vec(M(0, b, 0)); md[:, 2] = vec(M(0, b, 1)); md[:, 3] = vec(M(0, b, 2))
        md[:, 4] = vec(M(0, 2, 0)); md[:, 5] = vec(M(0, 2, 1)); md[:, 6] = vec(M(0, 2, 2))
        md[:, 11] = vec(norm_g[0, 1]); md[:, 12] = vec(M(0, b, 3)); md[:, 13] = vec(M(0, b, 4))
        md[:, 14] = vec(M(0, 2, 3)); md[:, 15] = vec(M(0, 2, 4))
        ims.append({"hin": fm(a), "mods": md, "wg0": ffn_w_gate[0, 0], "wu0": ffn_w_up[0, 0], "wd0": ffn_w_down[0, 0]})
    r = _run("l1", prog_l1, ims)
    h_fm = [r[ci]["hout"][:, :, 0:1024] for ci in cores]
    u_lat = [np.concatenate([tm(r[b * 4 + q]["uout"][:, :, 0:1024]) for q in range(4)], 0) for b in range(2)]
    u_ctx = [np.concatenate([tm(r[b * 4 + q]["uout"][:, :, 1024:1088]) for q in range(4)], 0) for b in range(2)]

    cw = np.ascontiguousarray(A(dn_conv_w)[0].reshape(5, 64, 128).transpose(2, 1, 0))
    gbv = np.zeros((128, 2), f32)
    gbv[64:, 0] = A(dn_dt_bias)[0].reshape(64); gbv[64:, 1] = A(dn_a_log)[0].reshape(64)
    w_in = np.asarray(dn_w_in)[0]
    ims = []
    for ci in cores:
        b, q = ci // 4, ci % 4
        a = np.concatenate([halo(u_lat[b], q * 1024 - 2, q * 1024 + 1026), halo(u_ctx[b], q * 64 - 2, q * 64 + 66)], 0)
        ims.append({"uT": fm(a), "w": w_in, "cw": cw, "gbv": gbv})
    r = _run("inproj", prog_inproj, ims)
    cst = scan_consts()
    ng = np.ascontiguousarray(np.broadcast_to(A(dn_norm_g)[0][None, :], (128, 128)))
    ims = []
    for b in range(2):
        def gather(key, lo, hi):
            return np.concatenate([r[b * 4 + q][key][..., lo:hi] for q in range(4)], -1)
        qk = np.concatenate([gather("qk", 1024, 1088), gather("qk", 0, 1024)], -1)
        vv = np.concatenate([gather("v", 1024, 1088), gather("v", 0, 1024)], -1)
        szz = gather("sz", 0, 1024)
        gb = np.concatenate([gather("gb", 1024, 1088), gather("gb", 0, 1024)], -1)
        q_tm = qk[:, 0:16].transpose(2, 1, 0)
        k_tm = qk[:, 16:32].transpose(2, 1, 0)
        v_tm = vv.transpose(2, 1, 0)
        sz_tm = szz.transpose(2, 1, 0)
        beta_tm = gb[0:64].T.reshape(4352, 2, 32)
        g_tm = gb[64:128].T.reshape(4352, 2, 32)
        for hg in range(4):
            d = scan_inputs(q_tm, k_tm, v_tm, sz_tm, g_tm, beta_tm, hg)
            d["ng"] = ng; d["cst"] = cst
            ims.append(d)
    r = _run("scan", prog_scan, ims)
    ypre = []
    for b in range(2):
        yb = np.stack([r[b * 4 + hg]["y"] for hg in range(4)], 0)
        ypre.append(np.ascontiguousarray(yb.transpose(3, 1, 0, 2, 4, 5)).reshape(4096, 4096))

    ims = []
    for ci in cores:
        b, q = ci // 4, ci % 4
        md = np.zeros((128, 17, KC), f32)
        md[:, 0] = vec(M(0, b, 5))
        md[:, 1] = vec(norm_g[0, 2]); md[:, 2] = vec(M(0, b, 6)); md[:, 3] = vec(M(0, b, 7)); md[:, 4] = vec(M(0, b, 8))
        md[:, 7] = vec(norm_g[1, 0]); md[:, 8] = vec(M(1, b, 0)); md[:, 9] = vec(M(1, b, 1)); md[:, 10] = vec(M(1, b, 2))
        md[:, 13] = vec(norm_g[1, 1]); md[:, 14] = vec(M(1, b, 3)); md[:, 15] = vec(M(1, b, 4))
        ims.append({"hin": np.ascontiguousarray(h_fm[ci]), "mods": md, "yin": fm(ypre[b][q * 1024:(q + 1) * 1024]),
                    "wo": np.asarray(dn_w_out)[0],
                    "wg0": ffn_w_gate[0, 1], "wu0": ffn_w_up[0, 1], "wd0": ffn_w_down[0, 1],
                    "wg1": ffn_w_gate[1, 0], "wu1": ffn_w_up[1, 0], "wd1": ffn_w_down[1, 0]})
    r = _run("l3", prog_l3, ims)
    h_fm = [r[ci]["hout"] for ci in cores]
    u1 = [np.concatenate([tm(r[b * 4 + q]["uout"]) for q in range(4)], 0) for b in range(2)]

    cc, cs = fourier_consts()
    ims = []
    for ci in cores:
        b, g = ci // 4, ci % 4
        ims.append({"uT": fm(u1[b][:, g * 512:(g + 1) * 512]), "cc": cc, "cs": cs})
    r = _run("fourier", prog_fourier, ims)
    yf = [np.concatenate([tm(r[b * 4 + g]["y"]) for g in range(4)], 1) for b in range(2)]

    ims = []
    for ci in cores:
        b, q = ci // 4, ci % 4
        md = np.zeros((128, 8, KC), f32)
        md[:, 0] = vec(M(1, b, 5))
        md[:, 1] = vec(norm_g[1, 2]); md[:, 2] = vec(M(1, b, 6)); md[:, 3] = vec(M(1, b, 7)); md[:, 4] = vec(M(1, b, 8))
        md[:, 7] = vec(A(final_norm_g))
        ims.append({"hin": np.ascontiguousarray(h_fm[ci]), "mods": md, "yin": fm(yf[b][q * 1024:(q + 1) * 1024]),
                    "wo": np.asarray(fn_w_out)[0],
                    "wg0": ffn_w_gate[1, 1], "wu0": ffn_w_up[1, 1], "wd0": ffn_w_down[1, 1]})
    r = _run("l5", prog_l5, ims)
    out = np.zeros((2, 4096, 2048), f32)
    for ci in cores:
        b, q = ci // 4, ci % 4
        out[b, q * 1024:(q + 1) * 1024] = tm(r[ci]["hout"])
    return out
```

```python
import numpy as np
import concourse.bass as bass
import concourse.mybir as mybir
from concourse.bass_utils import run_bass_kernel_spmd

F32 = mybir.dt.float32
BF16 = mybir.dt.bfloat16
AF = mybir.ActivationFunctionType
ALU = mybir.AluOpType
AX = mybir.AxisListType


class T:
    __slots__ = ("name", "w", "rd")

    def __init__(self, name=""):
        self.name = name
        self.w = None
        self.rd = []


class DSem:
    __slots__ = ("h", "count", "last")

    def __init__(self, h):
        self.h = h
        self.count = 0
        self.last = None


class Prog:
    ENGS = ("pe", "act", "dve", "pool", "sp")

    def __init__(self, nc):
        self.nc = nc
        self.ins = []
        self.es = None
        self._ctx = []

    def enter(self, cm):
        v = cm.__enter__()
        self._ctx.append(cm)
        return v

    def sb(self, name, shape, dt):
        self.nalloc = getattr(self, "nalloc", 0) + 1
        cm = self.nc.sbuf_tensor("sb%d_%s" % (self.nalloc, name), list(shape), dt)
        v = cm.__enter__()
        self._stage = getattr(self, "_stage", [])
        self._stage.append(cm)
        return v

    def ps(self, name, shape, dt=F32):
        self.nalloc = getattr(self, "nalloc", 0) + 1
        cm = self.nc.psum_tensor("ps%d_%s" % (self.nalloc, name), list(shape), dt)
        v = cm.__enter__()
        self._stage = getattr(self, "_stage", [])
        self._stage.append(cm)
        return v

    def dsem(self, name):
        if not hasattr(self, "sems"):
            self.sems = []
            self.soff = 0
        if self.soff == len(self.sems):
            self.sems.append(DSem(self.enter(self.nc.semaphore("ds%d" % len(self.sems)))))
        self.soff += 1
        return self.sems[self.soff - 1]

    def barrier(self):
        last = {}
        for i, it in enumerate(self.ins):
            last[it["eng"]] = i
        deps = list(last.values()) + [d.last for d in getattr(self, "sems", []) if d.last is not None]
        for e in self.ENGS:
            self.op(e, lambda eng: eng.nop(), after=deps)
        st = getattr(self, "_stage", [])
        while st:
            st.pop().__exit__(None, None, None)
        self.soff = 0

    def close(self):
        st = getattr(self, "_stage", [])
        while st:
            st.pop().__exit__(None, None, None)
        while self._ctx:
            self._ctx.pop().__exit__(None, None, None)

    def op(self, eng, fn, reads=(), writes=(), dsem=None, after=(), inc=16):
        idx = len(self.ins)
        isdma = dsem is not None
        deps = set(after)
        if isdma and dsem.last is not None:
            deps.add(dsem.last)
        for t in reads:
            if t.w is not None:
                deps.add(t.w)
        for t in writes:
            if t.w is not None:
                deps.add(t.w)
            deps.update(t.rd)
        raw = set(t.w for t in reads if t.w is not None)
        keep = []
        for d in deps:
            di = self.ins[d]
            if (not di["dma"]) and (not isdma) and di["eng"] == eng and d not in raw:
                continue
            keep.append(d)
            di["sig"] = True
        ev = None
        if isdma:
            dsem.count += inc
            ev = (dsem.h, dsem.count)
        self.ins.append(dict(eng=eng, fn=fn, deps=keep, dma=isdma, sig=isdma, ev=ev, inc=inc))
        if isdma:
            dsem.last = idx
        for t in reads:
            t.rd.append(idx)
        for t in writes:
            t.w = idx
            t.rd = []
        return idx

    def emit(self):
        nc = self.nc
        es = {e: self.enter(nc.semaphore("es_" + e)) for e in self.ENGS}
        cnt = {e: 0 for e in self.ENGS}
        for it in self.ins:
            if not it["dma"] and it["sig"]:
                cnt[it["eng"]] += 1
                it["ev"] = (es[it["eng"]], cnt[it["eng"]])
        per = {e: [it for it in self.ins if it["eng"] == e] for e in self.ENGS}
        ins = self.ins

        def body(e):
            def f(eng):
                waited = {}
                for it in per[e]:
                    need = {}
                    for d in it["deps"]:
                        s, v = ins[d]["ev"]
                        k = id(s)
                        if waited.get(k, 0) < v and need.get(k, (None, 0))[1] < v:
                            need[k] = (s, v)
                    for k, (s, v) in need.items():
                        eng.wait_ge(s, v)
                        waited[k] = v
                    r = it["fn"](eng)
                    if it["sig"]:
                        s, v = it["ev"]
                        r.then_inc(s, it["inc"] if it["dma"] else 1)
            return f

        with nc.Block() as block:
            block.tensor(body("pe"))
            block.scalar(body("act"))
            block.vector(body("dve"))
            block.gpsimd(body("pool"))
            block.sync(body("sp"))


D = 2048
KC = 16
DFF = 5632
FC = 44
EPS = 1e-6
G = 2


def I(meth, *a, **kw):
    return lambda e: getattr(e, meth)(*a, **kw)


def ttiles(Tn):
    out = []
    t = 0
    while t < Tn:
        n = min(512, Tn - t)
        out.append((t, n))
        t += n
    return out


class TokStage:
    def __init__(self, P, nc, Tlat, Tctx, NV):
        self.P, self.nc = P, nc
        self.Tlat, self.Tctx = Tlat, Tctx
        self.T = Tlat + Tctx
        self.tiles = ttiles(Tlat) + ([(Tlat, Tctx)] if Tctx else [])
        self.nt = len(self.tiles)
        T_ = self.T
        self.h = P.sb("h", [128, KC, T_], F32)
        self.xn = P.sb("xn", [128, KC, T_], BF16)
        self.mods = P.sb("mods", [128, NV, KC], F32)
        self.rstd = P.sb("rstd", [128, T_], F32)
        self.ones = P.sb("ones", [128, 128], F32)
        self.sq = [P.sb("sq%d" % i, [128, 512], F32) for i in range(2)]
        self.tmp = [P.sb("tmp%d" % i, [128, 512], F32) for i in range(2)]
        self.sg = [P.sb("sg%d" % i, [128, 512], BF16) for i in range(2)]
        self.act = [P.sb("act%d" % i, [128, G, T_], BF16) for i in range(2)]
        self.wg = [P.sb("wg%d" % i, [128, KC, G * 128], BF16) for i in range(2)]
        self.wu = [P.sb("wu%d" % i, [128, KC, G * 128], BF16) for i in range(2)]
        self.wd = [P.sb("wd%d" % i, [128, G, D], BF16) for i in range(2)]
        self.psA = [P.ps("psA%d" % i, [128, 512]) for i in range(2)]
        self.psB = [P.ps("psB%d" % i, [128, 512]) for i in range(2)]
        self.psC = [P.ps("psC%d" % i, [128, 512]) for i in range(2)]
        self.psS = [P.ps("psS%d" % i, [128, 512]) for i in range(2)]
        self.th = [[T("h") for _ in range(self.nt)] for _ in range(KC)]
        self.txn = [[T("xn") for _ in range(self.nt)] for _ in range(KC)]
        self.tmods = T("mods")
        self.trstd = [T("rstd") for _ in range(self.nt)]
        self.tones = T("ones")
        self.tsq = [T("sq") for _ in range(2)]
        self.ttmp = [T("tmp") for _ in range(2)]
        self.tsg = [T("sg") for _ in range(2)]
        self.tact = [[[T("act") for _ in range(self.nt)] for _ in range(G)] for _ in range(2)]
        self.twg = [T("wg") for _ in range(2)]
        self.twu = [T("wu") for _ in range(2)]
        self.twd = [T("wd") for _ in range(2)]
        self.tpsA = [T("psA") for _ in range(2)]
        self.tpsB = [T("psB") for _ in range(2)]
        self.tpsC = [T("psC") for _ in range(2)]
        self.tpsS = [T("psS") for _ in range(2)]
        self.swg = [P.dsem("swg%d" % i) for i in range(2)]
        self.swu = [P.dsem("swu%d" % i) for i in range(2)]
        self.swd = [P.dsem("swd%d" % i) for i in range(2)]
        self.sio = [P.dsem("sio%d" % i) for i in range(9)]
        self.cnt = 0
        self.grp = 0
        P.op("pool", I("memset", self.ones[:], 1.0), writes=[self.tones])

    def rr(self):
        self.cnt += 1
        return self.cnt % 2

    def load_h(self, h_dram, mods_dram):
        P = self.P
        P.op("sp", I("dma_start", out=self.mods[:], in_=mods_dram), writes=[self.tmods], dsem=self.sio[0])
        for k in range(KC):
            P.op("sp", I("dma_start", out=self.h[:, k, :], in_=h_dram[:, k, :]),
                 writes=self.th[k], dsem=self.sio[1 + k % 8])

    def store(self, out_dram, sb, tl):
        P = self.P
        ids = []
        for k in range(KC):
            ids.append(P.op("sp", I("dma_start", out=out_dram[:, k, :], in_=sb[:, k, :]),
                            reads=tl[k], dsem=self.sio[1 + k % 8]))
        return ids

    def mod_gw(self, vg, vscale, vout):
        m = self.mods
        self.P.op("dve", I("scalar_tensor_tensor", out=m[:, vout, :], in0=m[:, vscale, :], scalar=1.0,
                           in1=m[:, vg, :], op0=ALU.add, op1=ALU.mult),
                  reads=[self.tmods], writes=[self.tmods])

    def mod_scale(self, vin, vout, c):
        m = self.mods
        self.P.op("dve", I("tensor_scalar", out=m[:, vout, :], in0=m[:, vin, :], scalar1=c, scalar2=None, op0=ALU.mult),
                  reads=[self.tmods], writes=[self.tmods])

    def rms_stats(self):
        P = self.P
        for ti, (t0, n) in enumerate(self.tiles):
            s = self.rr()
            for k in range(KC):
                q = self.rr()
                P.op("act", I("activation", out=self.sq[q][:, 0:n], in_=self.h[:, k, t0:t0 + n], func=AF.Square),
                     reads=[self.th[k][ti]], writes=[self.tsq[q]])
                P.op("pe", I("matmul", self.psS[s][:, 0:n], lhsT=self.ones[:], rhs=self.sq[q][:, 0:n],
                             start=(k == 0), stop=(k == KC - 1)),
                     reads=[self.tsq[q], self.tones], writes=[self.tpsS[s]])
            q = self.rr()
            P.op("dve", I("tensor_scalar", out=self.tmp[q][:, 0:n], in0=self.psS[s][:, 0:n],
                          scalar1=1.0 / D, scalar2=EPS, op0=ALU.mult, op1=ALU.add),
                 reads=[self.tpsS[s]], writes=[self.ttmp[q]])
            P.op("act", I("activation", out=self.tmp[q][:, 0:n], in_=self.tmp[q][:, 0:n], func=AF.Sqrt),
                 reads=[self.ttmp[q]], writes=[self.ttmp[q]])
            P.op("dve", I("reciprocal", out=self.rstd[:, t0:t0 + n], in_=self.tmp[q][:, 0:n]),
                 reads=[self.ttmp[q]], writes=[self.trstd[ti]])

    def ada_norm(self, out_sb, out_tiles, vgw, vshift, vgw_c=None, vshift_c=None):
        P = self.P
        m = self.mods
        for ti, (t0, n) in enumerate(self.tiles):
            isctx = self.Tctx and ti == self.nt - 1
            g_ = vgw_c if isctx else vgw
            s_ = vshift_c if isctx else vshift
            for k in range(KC):
                q = self.rr()
                P.op("dve", I("scalar_tensor_tensor", out=self.tmp[q][:, 0:n], in0=self.h[:, k, t0:t0 + n],
                              scalar=m[:, g_, k:k + 1], in1=self.rstd[:, t0:t0 + n], op0=ALU.mult, op1=ALU.mult),
                     reads=[self.th[k][ti], self.trstd[ti], self.tmods], writes=[self.ttmp[q]])
                if s_ is None:
                    P.op("act", I("activation", out=out_sb[:, k, t0:t0 + n], in_=self.tmp[q][:, 0:n], func=AF.Copy),
                         reads=[self.ttmp[q]], writes=[out_tiles[k][ti]])
                else:
                    P.op("act", I("activation", out=out_sb[:, k, t0:t0 + n], in_=self.tmp[q][:, 0:n],
                                  func=AF.Identity, bias=m[:, s_, k:k + 1]),
                         reads=[self.ttmp[q], self.tmods], writes=[out_tiles[k][ti]])

    def load_w(self, wg, wu, wd, g):
        P = self.P
        s = self.grp % 2
        f0 = g * G * 128
        wgs = wg.rearrange("(k p) f -> p k f", p=128)
        wus = wu.rearrange("(k p) f -> p k f", p=128)
        wds = wd.rearrange("(j p) d -> p j d", p=128)
        P.op("pool", I("dma_start", out=self.wg[s][:, :, :], in_=wgs[:, :, f0:f0 + G * 128]),
             writes=[self.twg[s]], dsem=self.swg[s])
        P.op("pool", I("dma_start", out=self.wu[s][:, :, :], in_=wus[:, :, f0:f0 + G * 128]),
             writes=[self.twu[s]], dsem=self.swu[s])
        P.op("pool", I("dma_start", out=self.wd[s][:, :, :], in_=wds[:, g * G:(g + 1) * G, :]),
             writes=[self.twd[s]], dsem=self.swd[s])
        self.grp += 1
        return s

    def ffn_p1(self, s):
        P = self.P
        for j in range(G):
            for ti, (t0, n) in enumerate(self.tiles):
                a = self.rr()
                for k in range(KC):
                    P.op("pe", I("matmul", self.psA[a][:, 0:n], lhsT=self.wg[s][:, k, j * 128:(j + 1) * 128],
                                 rhs=self.xn[:, k, t0:t0 + n], start=(k == 0), stop=(k == KC - 1)),
                         reads=[self.twg[s], self.txn[k][ti]], writes=[self.tpsA[a]])
                for k in range(KC):
                    P.op("pe", I("matmul", self.psB[a][:, 0:n], lhsT=self.wu[s][:, k, j * 128:(j + 1) * 128],
                                 rhs=self.xn[:, k, t0:t0 + n], start=(k == 0), stop=(k == KC - 1)),
                         reads=[self.twu[s], self.txn[k][ti]], writes=[self.tpsB[a]])
                P.op("act", I("activation", out=self.sg[a][:, 0:n], in_=self.psA[a][:, 0:n], func=AF.Silu),
                     reads=[self.tpsA[a]], writes=[self.tsg[a]])
                P.op("dve", I("tensor_tensor", out=self.act[s][:, j, t0:t0 + n], in0=self.sg[a][:, 0:n],
                              in1=self.psB[a][:, 0:n], op=ALU.mult),
                     reads=[self.tsg[a], self.tpsB[a]], writes=[self.tact[s][j][ti]])
                yield

    def ffn_p2(self, s, vgate, vgate_c):
        P = self.P
        m = self.mods
        for d in range(KC):
            for ti, (t0, n) in enumerate(self.tiles):
                isctx = self.Tctx and ti == self.nt - 1
                gv = vgate_c if isctx else vgate
                self.c4 = (getattr(self, "c4", 0) + 1) % 4
                pc, tpc = ((self.psC[0], self.tpsC[0]), (self.psC[1], self.tpsC[1]),
                           (self.psS[0], self.tpsS[0]), (self.psS[1], self.tpsS[1]))[self.c4]
                for j in range(G):
                    P.op("pe", I("matmul", pc[:, 0:n], lhsT=self.wd[s][:, j, d * 128:(d + 1) * 128],
                                 rhs=self.act[s][:, j, t0:t0 + n], start=(j == 0), stop=(j == G - 1)),
                         reads=[self.twd[s], self.tact[s][j][ti]], writes=[tpc])
                P.op("dve", I("scalar_tensor_tensor", out=self.h[:, d, t0:t0 + n], in0=pc[:, 0:n],
                              scalar=m[:, gv, d:d + 1], in1=self.h[:, d, t0:t0 + n], op0=ALU.mult, op1=ALU.add),
                     reads=[tpc, self.th[d][ti], self.tmods], writes=[self.th[d][ti]])
                yield

    def ffn(self, wg, wu, wd, vgw, vshift, vgate, vgw_c=None, vshift_c=None, vgate_c=None):
        self.rms_stats()
        self.ada_norm(self.xn, self.txn, vgw, vshift, vgw_c, vshift_c)
        NG = FC // G
        prev = None
        per = (KC * self.nt + G * self.nt - 1) // (G * self.nt)
        for g in range(NG):
            s = self.load_w(wg, wu, wd, g)
            g1 = self.ffn_p1(s)
            g2 = self.ffn_p2(prev, vgate, vgate_c) if prev is not None else iter(())
            for _ in g1:
                for _ in range(per):
                    next(g2, None)
            for _ in g2:
                pass
            prev = s
        for _ in self.ffn_p2(prev, vgate, vgate_c):
            pass

    def mix(self, y_d, nfc, w_d, vgate):
        P = self.P
        m = self.mods
        ws = w_d.rearrange("(j p) d -> p j d", p=128)
        bufs = [(self.wg[0], self.twg[0], self.swg[0]), (self.wu[0], self.twu[0], self.swu[0]),
                (self.wg[1], self.twg[1], self.swg[1]), (self.wu[1], self.twu[1], self.swu[1])]
        lat_tiles = [(ti, t0, n) for ti, (t0, n) in enumerate(self.tiles) if t0 < self.Tlat]
        nb = 0
        for half in range(nfc // KC):
            for k in range(KC):
                P.op("sp", I("dma_start", out=self.xn[:, k, 0:self.Tlat], in_=(y_d(half * KC + k) if callable(y_d) else y_d[:, half * KC + k, :])),
                     writes=[self.txn[k][ti] for ti, _, _ in lat_tiles], dsem=self.sio[1 + k % 8])
            for blk in range(D // (G * 128)):
                buf, tb, sb_ = bufs[nb % 4]
                nb += 1
                P.op("pool", I("dma_start", out=buf[:, :, :], in_=ws[:, half * KC:(half + 1) * KC, blk * G * 128:(blk + 1) * G * 128]),
                     writes=[tb], dsem=sb_)
                for dj in range(G):
                    d = blk * G + dj
                    for ti, t0, n in lat_tiles:
                        c = self.rr()
                        for k in range(KC):
                            P.op("pe", I("matmul", self.psC[c][:, 0:n], lhsT=buf[:, k, dj * 128:(dj + 1) * 128],
                                         rhs=self.xn[:, k, t0:t0 + n], start=(k == 0), stop=(k == KC - 1)),
                                 reads=[tb, self.txn[k][ti]], writes=[self.tpsC[c]])
                        P.op("dve", I("scalar_tensor_tensor", out=self.h[:, d, t0:t0 + n], in0=self.psC[c][:, 0:n],
                                      scalar=m[:, vgate, d:d + 1], in1=self.h[:, d, t0:t0 + n], op0=ALU.mult, op1=ALU.add),
                             reads=[self.tpsC[c], self.th[d][ti], self.tmods], writes=[self.th[d][ti]])


F32R = mybir.dt.float32r
C = 128
NCH = 34
NLAT = 32
DKS = 128 ** -0.5
NEG = -30000.0
LE, GE, GT, LT, ONES, IDENT, NEGF, NEGB, NM0 = range(9)
NCONST = 15
NSTAGE = 6
PRE_REP = 1


class RR:
    def __init__(self, items):
        self.items = items
        self.i = 0

    def get(self):
        it = self.items[self.i % len(self.items)]
        self.i += 1
        return it


def build_scan(nc, qT_d, kT_d, kM_d, vM_d, sz_d, g_d, b_d, ng_d, cst_d, y_d, NQH=4):
    P = Prog(nc)
    qT = P.sb("qT", [128, NCH * C], BF16)
    kT = P.sb("kT", [128, NCH * C], BF16)
    kM = P.sb("kM", [128, NCH, C], BF16)
    vM = P.sb("vM", [128, NCH, 2, C], BF16)
    sz = P.sb("sz", [128, NLAT, 2, C], BF16)
    gM = P.sb("gM", [128, NCH, 4], F32)
    bM = P.sb("bM", [128, NCH, 4], F32)
    oacc = P.sb("oacc", [128, NLAT, 2, C], F32)
    yout = P.sb("yout", [128, NLAT, 2, C], BF16)
    ss = P.sb("ss", [128, NLAT * 2], F32)
    cst = P.sb("cst", [128, NCONST, C], F32)
    ng = P.sb("ng", [128, C], F32)
    identr_ = P.sb("identr", [128, C], BF16)
    identr = identr_[:]
    S32 = [P.sb("S32_%d" % c, [128, C], F32) for c in range(4)]
    Sbf = [P.sb("Sbf_%d" % c, [128, C], BF16) for c in range(4)]
    tS32 = [T() for _ in range(4)]
    tSbf = [T() for _ in range(4)]
    tq, tk, tkM, tv, tsz, tg, tb, tcst, tng, tidr, tss, tyout = [T() for _ in range(12)]
    toacc = [[T() for _ in range(2)] for _ in range(NLAT)]
    dsem = [P.dsem("ld%d" % i) for i in range(12)]

    def mkpool(name, n, dt):
        return RR([(P.sb("%s%d" % (name, i), [128, C], dt), T()) for i in range(n)])

    p32 = mkpool("p32_", 32, F32)
    p32r = mkpool("p32r_", 48, BF16)
    pbf = mkpool("pbf_", 32, BF16)
    pU = mkpool("pU_", 8, BF16)
    pUt = mkpool("pUt_", 8, BF16)
    pR = mkpool("pR_", 12, BF16)
    psm = RR([(P.sb("sm%d" % i, [128, 16], F32), T()) for i in range(6)])
    banks = [P.ps("bank%d" % i, [128, 512]) for i in range(8)]
    pps = RR([(banks[i % 8][:, (i // 8) * C:(i // 8 + 1) * C], T()) for i in range(32)])

    def cs(i):
        return cst[:, i, :]

    P.op("sp", I("dma_start", out=cst[:], in_=cst_d), writes=[tcst], dsem=dsem[0])
    P.op("sp", I("dma_start", out=ng[:], in_=ng_d), writes=[tng], dsem=dsem[1])
    P.op("dve", I("tensor_copy", out=identr, in_=cs(IDENT)), reads=[tcst], writes=[tidr])

    fo = list(range(NCH))
    bo = [1, 0] + list(range(NCH - 1, 1, -1))
    out_ids = []

    for qh in range(NQH):
        P.op("sp", I("dma_start", out=qT[:], in_=qT_d[:, qh, :]), writes=[tq], dsem=dsem[2])
        P.op("sp", I("dma_start", out=kT[:], in_=kT_d[:, qh, :]), writes=[tk], dsem=dsem[3])
        P.op("sp", I("dma_start", out=kM[:], in_=kM_d[:, qh, :, :]), writes=[tkM], dsem=dsem[4])
        P.op("sp", I("dma_start", out=vM[:], in_=vM_d[:, qh, :, :, :]), writes=[tv], dsem=dsem[5])
        P.op("sp", I("dma_start", out=sz[:], in_=sz_d[:, qh, :, :, :]), writes=[tsz], dsem=dsem[6])
        P.op("sp", I("dma_start", out=gM[:], in_=g_d[:, qh, :, :]), writes=[tg], dsem=dsem[7])
        P.op("sp", I("dma_start", out=bM[:], in_=b_d[:, qh, :, :]), writes=[tb], dsem=dsem[8])
        for c in range(4):
            P.op("pool", I("memset", S32[c][:], 0.0), writes=[tS32[c]])
            P.op("pool", I("memset", Sbf[c][:], 0.0), writes=[tSbf[c]])
        first_o = [[True, True] for _ in range(NLAT)]

        def pre(d, n):
            lat = n >= 2
            kc = kT[:, n * C:(n + 1) * C]
            qc = qT[:, n * C:(n + 1) * C]
            res = {}
            kk, tkk = pps.get()
            P.op("pe", I("matmul", kk, lhsT=kc, rhs=kc, start=True, stop=True), reads=[tk], writes=[tkk])
            yield
            kks, tkks = p32.get()
            P.op("dve", I("tensor_tensor", out=kks[:], in0=kk, in1=cs(LT if d == 0 else GT), op=ALU.mult),
                 reads=[tkk, tcst], writes=[tkks])
            yield
            if lat:
                qk, tqk = pps.get()
                P.op("pe", I("matmul", qk, lhsT=kc, rhs=qc, start=True, stop=True), reads=[tk, tq], writes=[tqk])
                yield
                qks, tqks = p32.get()
                P.op("act", I("activation", out=qks[:], in_=qk, func=AF.Copy, scale=DKS), reads=[tqk], writes=[tqks])
                yield
            gcols = gM[:, n, 2 * d:2 * d + 2]
            st, tst = pps.get()
            P.op("pe", I("matmul", st[:, 0:2], lhsT=cs(LE if d == 0 else GE), rhs=gcols, start=True, stop=True),
                 reads=[tcst, tg], writes=[tst])
            P.op("pe", I("matmul", st[:, 2:4], lhsT=cs(GT if d == 0 else LT), rhs=gcols, start=True, stop=True),
                 reads=[tcst, tg], writes=[tst])
            P.op("pe", I("matmul", st[:, 4:6], lhsT=cs(ONES), rhs=gcols, start=True, stop=True),
                 reads=[tcst, tg], writes=[tst])
            yield
            sm, tsm = psm.get()
            P.op("act", I("activation", out=sm[:, 0:6], in_=st[:, 0:6], func=AF.Exp), reads=[tst], writes=[tsm])
            P.op("pool", I("tensor_scalar", out=sm[:, 6:8], in0=sm[:, 0:2], scalar1=-1.0, scalar2=0.0, op0=ALU.mult, op1=ALU.add),
                 reads=[tsm], writes=[tsm])
            P.op("pool", I("tensor_scalar", out=sm[:, 8:10], in0=sm[:, 0:2], scalar1=DKS, scalar2=0.0, op0=ALU.mult, op1=ALU.add),
                 reads=[tsm], writes=[tsm])
            P.op("pool", I("tensor_scalar", out=sm[:, 10:12], in0=bM[:, n, 2 * d:2 * d + 2], scalar1=-1.0, scalar2=0.0,
                           op0=ALU.mult, op1=ALU.add), reads=[tb], writes=[tsm])
            yield
            res["sm"] = (sm, tsm)
            ch = []
            for v in range(2):
                gx, tgx = p32.get()
                P.op("pool", I("tensor_scalar", out=gx[:], in0=cs(LE if d == 0 else GE), scalar1=gM[:, n, 2 * d + v:2 * d + v + 1],
                               scalar2=0.0, op0=ALU.mult, op1=ALU.add), reads=[tcst, tg], writes=[tgx])
                yield
                dt_, tdt = pps.get()
                P.op("pe", I("matmul", dt_, lhsT=cs(GT if d == 0 else LT), rhs=gx[:], start=True, stop=False),
                     reads=[tcst, tgx], writes=[tdt])
                P.op("pe", I("matmul", dt_, lhsT=cs(IDENT), rhs=cs(NEGF if d == 0 else NEGB), start=False, stop=True),
                     reads=[tcst], writes=[tdt])
                yield
                dec, tdec = p32.get()
                P.op("act", I("activation", out=dec[:], in_=dt_, func=AF.Exp), reads=[tdt], writes=[tdec])
                yield
                M, tM = pU.get()
                P.op("dve", I("scalar_tensor_tensor", out=M[:], in0=kks[:], scalar=bM[:, n, 2 * d + v:2 * d + v + 1], in1=dec[:],
                              op0=ALU.mult, op1=ALU.mult), reads=[tkks, tb, tdec], writes=[tM])
                yield
                c = dict(M=(M, tM))
                if lat:
                    at, tat = pbf.get()
                    P.op("pool", I("tensor_tensor", out=at[:], in0=qks[:], in1=dec[:], op=ALU.mult),
                         reads=[tqks, tdec], writes=[tat])
                    c["at"] = (at, tat)
                    yield
                kd, tkd = pbf.get()
                P.op("pool", I("tensor_scalar", out=kd[:], in0=kM[:, n, :], scalar1=sm[:, 2 + v:3 + v], scalar2=0.0,
                               op0=ALU.mult, op1=ALU.add), reads=[tkM, tsm], writes=[tkd])
                c["kd"] = (kd, tkd)
                yield
                ch.append(c)
            for c in ch:
                U, tU = c["M"]
                ut_ps, tut = pps.get()
                P.op("pe", I("matmul", ut_ps, lhsT=U[:], rhs=identr, start=True, stop=True), reads=[tU, tidr], writes=[tut])
                Gt, tGt = pUt.get()
                P.op("dve", I("tensor_tensor", out=Gt[:], in0=ut_ps, in1=cs(IDENT), op=ALU.add), reads=[tut, tcst], writes=[tGt])
                yield
                X, tX = p32r.get()
                Xt, tXt = p32r.get()
                x0, tx0 = p32.get()
                P.op("pool", I("tensor_tensor", out=x0[:], in0=U[:], in1=cs(NM0), op=ALU.mult), reads=[tU, tcst], writes=[tx0])
                P.op("dve", I("tensor_tensor", out=X[:], in0=x0[:], in1=cs(IDENT), op=ALU.add), reads=[tx0, tcst], writes=[tX])
                yield
                x1, tx1 = p32.get()
                P.op("pool", I("tensor_tensor", out=x1[:], in0=Gt[:], in1=cs(NM0), op=ALU.mult), reads=[tGt, tcst], writes=[tx1])
                P.op("dve", I("tensor_tensor", out=Xt[:], in0=x1[:], in1=cs(IDENT), op=ALU.add), reads=[tx1, tcst], writes=[tXt])
                yield
                c["Gt"], c["X"], c["Xt"] = (Gt, tGt), (X, tX), (Xt, tXt)
            for lv in range(1, 7):
                last = lv == 6
                for c in ch:
                    Gt, tGt = c["Gt"]
                    X, tX = c["X"]
                    y_ps, ty = pps.get()
                    P.op("pe", I("matmul", y_ps, lhsT=Gt[:], rhs=X[:], start=True, stop=True),
                         reads=[tGt, tX], writes=[ty])
                    W, tW = p32r.get()
                    P.op("dve", I("tensor_tensor", out=W[:], in0=y_ps, in1=cs(NM0 + lv), op=ALU.mult),
                         reads=[ty, tcst], writes=[tW])
                    c["W"] = (W, tW)
                    yield
                for c in ch:
                    X, tX = c["X"]
                    Xt, tXt = c["Xt"]
                    W, tW = c["W"]
                    x_ps, tx = pps.get()
                    P.op("pe", I("matmul", x_ps, lhsT=Xt[:], rhs=W[:], start=True, stop=True),
                         reads=[tXt, tW], writes=[tx])
                    if not last:
                        xt_ps, txt = pps.get()
                        P.op("pe", I("matmul", xt_ps, lhsT=W[:], rhs=Xt[:], start=True, stop=True),
                             reads=[tW, tXt], writes=[txt])
                    nX, tnX = pR.get() if last else p32r.get()
                    P.op("act", I("activation", out=nX[:], in_=x_ps, func=AF.Copy), reads=[tx], writes=[tnX])
                    c["X"] = (nX, tnX)
                    if not last:
                        nXt, tnXt = p32r.get()
                        P.op("act" if lv % 2 else "dve", (I("activation", out=nXt[:], in_=xt_ps, func=AF.Copy) if lv % 2
                                                       else I("tensor_copy", out=nXt[:], in_=xt_ps)), reads=[txt], writes=[tnXt])
                        c["Xt"] = (nXt, tnXt)
                    yield
            for c in ch:
                c["R"] = c["X"]
            res["ch"] = ch
            return res

        def seq(d, n, res):
            lat = n >= 2
            kc = kT[:, n * C:(n + 1) * C]
            qc = qT[:, n * C:(n + 1) * C]
            sm, tsm = res["sm"]
            for v in range(2):
                ci = 2 * d + v
                c = res["ch"][v]
                ps1, t1 = pps.get()
                P.op("pe", I("matmul", ps1, lhsT=kc, rhs=Sbf[ci][:], start=True, stop=True), reads=[tk, tSbf[ci]], writes=[t1])
                if lat:
                    ps2, t2 = pps.get()
                    P.op("pe", I("matmul", ps2, lhsT=qc, rhs=Sbf[ci][:], start=True, stop=True), reads=[tq, tSbf[ci]], writes=[t2])
                yield
                r, tr = p32r.get()
                P.op("dve", I("scalar_tensor_tensor", out=r[:], in0=ps1, scalar=sm[:, 6 + v:7 + v], in1=vM[:, n, v, :],
                              op0=ALU.mult, op1=ALU.add), reads=[t1, tsm, tv], writes=[tr])
                yield
                R, tR = c["R"]
                ps3, t3 = pps.get()
                P.op("pe", I("matmul", ps3, lhsT=R[:], rhs=r[:], start=True, stop=True), reads=[tR, tr], writes=[t3])
                yield
                vn, tvn = pbf.get()
                P.op("act", I("activation", out=vn[:], in_=ps3, func=AF.Copy, scale=bM[:, n, ci:ci + 1]),
                     reads=[t3, tb], writes=[tvn])
                yield
                kd, tkd = c["kd"]
                ps5, t5 = pps.get()
                P.op("pe", I("matmul", ps5, lhsT=kd[:], rhs=vn[:], start=True, stop=True), reads=[tkd, tvn], writes=[t5])
                if lat:
                    at, tat = c["at"]
                    ps4, t4 = pps.get()
                    P.op("pe", I("matmul", ps4, lhsT=at[:], rhs=vn[:], start=True, stop=True), reads=[tat, tvn], writes=[t4])
                yield
                P.op("dve", I("scalar_tensor_tensor", out=S32[ci][:], in0=S32[ci][:], scalar=sm[:, 4 + v:5 + v], in1=ps5,
                              op0=ALU.mult, op1=ALU.add), reads=[tS32[ci], tsm, t5], writes=[tS32[ci]])
                P.op("act", I("activation", out=Sbf[ci][:], in_=S32[ci][:], func=AF.Copy), reads=[tS32[ci]], writes=[tSbf[ci]])
                yield
                if lat:
                    l = n - 2
                    ov = oacc[:, l, v, :]
                    if first_o[l][v]:
                        first_o[l][v] = False
                        P.op("dve", I("tensor_scalar", out=ov, in0=ps2, scalar1=sm[:, 8 + v:9 + v], scalar2=None, op0=ALU.mult),
                             reads=[t2, tsm], writes=[toacc[l][v]])
                    else:
                        P.op("dve", I("scalar_tensor_tensor", out=ov, in0=ps2, scalar=sm[:, 8 + v:9 + v], in1=ov,
                                      op0=ALU.mult, op1=ALU.add), reads=[t2, tsm, toacc[l][v]], writes=[toacc[l][v]])
                    P.op("dve", I("tensor_tensor", out=ov, in0=ov, in1=ps4, op=ALU.add),
                         reads=[toacc[l][v], t4], writes=[toacc[l][v]])
                    yield

        def drive(gens, reps=None):
            rets = [None] * len(gens)
            live = list(range(len(gens)))
            reps = reps or [1] * len(gens)
            while live:
                for gi in list(live):
                    for _ in range(reps[gi]):
                        try:
                            next(gens[gi])
                        except StopIteration as e:
                            rets[gi] = e.value
                            live.remove(gi)
                            break
            return rets

        cur = drive([pre(0, fo[0]), pre(1, bo[0])])
        for s in range(NCH):
            gens = [seq(0, fo[s], cur[0]), seq(1, bo[s], cur[1])]
            if s + 1 < NCH:
                gens += [pre(0, fo[s + 1]), pre(1, bo[s + 1])]
            r = drive(gens, [1, 1, PRE_REP, PRE_REP][:len(gens)])
            if s + 1 < NCH:
                cur = r[2:4]

        jk, tjk = p32.get()
        for l in range(NLAT):
            for v in range(2):
                P.op("act", I("activation", out=jk[:], in_=oacc[:, l, v, :], func=AF.Square,
                              accum_out=ss[:, 2 * l + v:2 * l + v + 1]), reads=[toacc[l][v]], writes=[tjk, tss])
        P.op("dve", I("tensor_scalar", out=ss[:], in0=ss[:], scalar1=1.0 / C, scalar2=1e-6, op0=ALU.mult, op1=ALU.add),
             reads=[tss], writes=[tss])
        P.op("act", I("activation", out=ss[:], in_=ss[:], func=AF.Sqrt), reads=[tss], writes=[tss])
        P.op("dve", I("reciprocal", out=ss[:], in_=ss[:]), reads=[tss], writes=[tss])
        for l in range(NLAT):
            for v in range(2):
                gz, tgz = p32.get()
                P.op("pool", I("tensor_tensor", out=gz[:], in0=sz[:, l, v, :], in1=ng[:], op=ALU.mult),
                     reads=[tsz, tng], writes=[tgz])
                P.op("dve", I("scalar_tensor_tensor", out=yout[:, l, v, :], in0=oacc[:, l, v, :],
                              scalar=ss[:, 2 * l + v:2 * l + v + 1], in1=gz[:], op0=ALU.mult, op1=ALU.mult),
                     reads=[toacc[l][v], tss, tgz], writes=[tyout])
        out_ids.append(P.op("sp", I("dma_start", out=y_d[:, qh, :, :, :], in_=yout[:]), reads=[tyout], dsem=dsem[9]))
    P.op("sp", I("nop"), after=out_ids)
    return P


TL = 1028
TCX = 68
TT = TL + TCX
NOUT = 1088
NCHK = 97
TILES = [(0, 512), (512, 512), (1024, TT - 1024)]


def build_inproj(nc, uT_d, w_d, cw_d, gbv_d, qk_d, v_d, sz_d, gb_d, only=None):
    P = Prog(nc)
    uT = P.sb("uT", [128, KC, TT], BF16)
    cw = P.sb("cw", [128, 64, 5], F32)
    gbv = P.sb("gbv", [128, 4], F32)
    ones = P.sb("ones", [128, 128], F32)
    wb = [P.sb("wb%d" % i, [128, KC, 512], BF16) for i in range(2)]
    pre = [P.sb("pre%d" % i, [128, TT], F32) for i in range(3)]
    acc = [P.sb("acc%d" % i, [128, TT], F32) for i in range(3)]
    sil = [P.sb("sil%d" % i, [128, NOUT], F32) for i in range(3)]
    sq = [P.sb("sq%d" % i, [128, NOUT], F32) for i in range(3)]
    rs = [P.sb("rs%d" % i, [128, NOUT], F32) for i in range(3)]
    ob = [P.sb("ob%d" % i, [128, NOUT], BF16) for i in range(3)]
    gbo = P.sb("gbo", [128, NOUT], F32)
    tuT, tcw, tgbv, tones, tgbo = [T() for _ in range(5)]
    twb = [T() for _ in range(2)]
    tpre, tacc, tsil, tsq, trs = [[T() for _ in range(3)] for _ in range(5)]
    tob = [T() for _ in range(3)]
    psA = [P.ps("psA%d" % i, [128, 512]) for i in range(6)]
    tpsA = [T() for _ in range(6)]
    psS = [P.ps("psS%d" % i, [128, 512]) for i in range(2)]
    tpsS = [T() for _ in range(2)]
    sw = [P.dsem("sw%d" % i) for i in range(2)]
    sio = [P.dsem("sio%d" % i) for i in range(8)]
    so = [P.dsem("so%d" % i) for i in range(3)]
    P.op("pool", I("memset", ones[:], 1.0), writes=[tones])
    P.op("sp", I("dma_start", out=cw[:], in_=cw_d), writes=[tcw], dsem=sio[0])
    P.op("sp", I("dma_start", out=gbv[:, 0:2], in_=gbv_d), writes=[tgbv], dsem=sio[1])
    for k in range(KC):
        P.op("sp", I("dma_start", out=uT[:, k, :], in_=uT_d[:, k, :]), writes=[tuT], dsem=sio[2 + k % 4])
    P.op("act", I("activation", out=gbv[:, 2:3], in_=gbv[:, 1:2], func=AF.Exp), reads=[tgbv], writes=[tgbv])
    P.op("dve", I("tensor_scalar", out=gbv[:, 2:3], in0=gbv[:, 2:3], scalar1=-1.0, scalar2=None, op0=ALU.mult),
         reads=[tgbv], writes=[tgbv])
    ws = w_d.rearrange("(k p) f -> p k f", p=128)
    out_ids = []
    cnt = [0, 0, 0, 0]

    def nxt(i, m):
        cnt[i] += 1
        return cnt[i] % m

    ngrp = (NCHK + 3) // 4
    pending = [None]

    def load_grp(gi):
        c0_ = gi * 4
        ncol_ = min(4, NCHK - c0_) * 128
        P.op("pool", I("dma_start", out=wb[gi % 2][:, :, 0:ncol_], in_=ws[:, :, c0_ * 128:c0_ * 128 + ncol_]),
             writes=[twb[gi % 2]], dsem=sw[gi % 2])

    load_grp(0)
    for gi in range(ngrp):
        c0 = gi * 4
        ncol = min(4, NCHK - c0) * 128
        s = gi % 2
        if gi + 1 < ngrp:
            load_grp(gi + 1)
        for cj in range(ncol // 128):
            c = c0 + cj
            if only is not None and c not in only:
                continue
            pi = nxt(0, 3)
            for ti, (t0, n) in enumerate(TILES):
                a = nxt(1, 6)
                for k in range(KC):
                    P.op("pe", I("matmul", psA[a][:, 0:n], lhsT=wb[s][:, k, cj * 128:(cj + 1) * 128], rhs=uT[:, k, t0:t0 + n],
                                 start=(k == 0), stop=(k == KC - 1)), reads=[twb[s], tuT], writes=[tpsA[a]])
                P.op("act", I("activation", out=pre[pi][:, t0:t0 + n], in_=psA[a][:, 0:n], func=AF.Copy),
                     reads=[tpsA[a]], writes=[tpre[pi]])
            if pending[0] is not None:
                pending[0]()

            def post(c=c, pi=pi):
                oi = nxt(2, 3)
                if c < 64:
                    ai = pi
                    for (o0, n) in ((2, 1024), (TL + 2, 64)):
                        for w in range(5):
                            src = pre[pi][:, o0 - 2 + w:o0 - 2 + w + n]
                            if w == 0:
                                P.op("dve", I("tensor_scalar", out=acc[ai][:, o0:o0 + n], in0=src, scalar1=cw[:, c, 0:1], scalar2=None,
                                              op0=ALU.mult), reads=[tpre[pi], tcw], writes=[tacc[ai]])
                            else:
                                P.op("dve", I("scalar_tensor_tensor", out=acc[ai][:, o0:o0 + n], in0=src, scalar=cw[:, c, w:w + 1],
                                              in1=acc[ai][:, o0:o0 + n], op0=ALU.mult, op1=ALU.add),
                                     reads=[tpre[pi], tcw, tacc[ai]], writes=[tacc[ai]])
                    if c < 32:
                        P.op("act", I("activation", out=sil[ai][:, 0:1024], in_=acc[ai][:, 2:1026], func=AF.Silu),
                             reads=[tacc[ai]], writes=[tsil[ai]])
                        P.op("act", I("activation", out=sil[ai][:, 1024:1088], in_=acc[ai][:, TL + 2:TL + 66], func=AF.Silu),
                             reads=[tacc[ai]], writes=[tsil[ai]])
                        P.op("pool", I("tensor_tensor", out=sq[ai][:], in0=sil[ai][:], in1=sil[ai][:], op=ALU.mult),
                             reads=[tsil[ai]], writes=[tsq[ai]])
                        for (t0, n) in ((0, 512), (512, 512), (1024, 64)):
                            si = nxt(3, 2)
                            P.op("pe", I("matmul", psS[si][:, 0:n], lhsT=ones[:], rhs=sq[ai][:, t0:t0 + n], start=True, stop=True),
                                 reads=[tones, tsq[ai]], writes=[tpsS[si]])
                            P.op("dve", I("tensor_scalar", out=rs[ai][:, t0:t0 + n], in0=psS[si][:, 0:n], scalar1=1e-6, scalar2=None,
                                          op0=ALU.add), reads=[tpsS[si]], writes=[trs[ai]])
                        P.op("act", I("activation", out=rs[ai][:], in_=rs[ai][:], func=AF.Ln), reads=[trs[ai]], writes=[trs[ai]])
                        P.op("act", I("activation", out=rs[ai][:], in_=rs[ai][:], func=AF.Exp, scale=-0.5), reads=[trs[ai]], writes=[trs[ai]])
                        P.op("pool", I("tensor_tensor", out=ob[oi][:], in0=sil[ai][:], in1=rs[ai][:], op=ALU.mult),
                             reads=[tsil[ai], trs[ai]], writes=[tob[oi]])
                        out_ids.append(P.op("sp", I("dma_start", out=qk_d[:, c, :], in_=ob[oi][:]), reads=[tob[oi]], dsem=so[oi]))
                    else:
                        P.op("act", I("activation", out=ob[oi][:, 0:1024], in_=acc[ai][:, 2:1026], func=AF.Silu),
                             reads=[tacc[ai]], writes=[tob[oi]])
                        P.op("act", I("activation", out=ob[oi][:, 1024:1088], in_=acc[ai][:, TL + 2:TL + 66], func=AF.Silu),
                             reads=[tacc[ai]], writes=[tob[oi]])
                        out_ids.append(P.op("sp", I("dma_start", out=v_d[:, c - 32, :], in_=ob[oi][:]), reads=[tob[oi]], dsem=so[oi]))
                elif c < 96:
                    P.op("act", I("activation", out=ob[oi][:, 0:1024], in_=pre[pi][:, 2:1026], func=AF.Silu),
                         reads=[tpre[pi]], writes=[tob[oi]])
                    out_ids.append(P.op("sp", I("dma_start", out=sz_d[:, c - 64, :], in_=ob[oi][:, 0:1024]), reads=[tob[oi]], dsem=so[oi]))
                else:
                    for (o0, i0, n) in ((0, 2, 1024), (1024, TL + 2, 64)):
                        P.op("act", I("activation", out=gbo[0:64, o0:o0 + n], in_=pre[pi][0:64, i0:i0 + n], func=AF.Sigmoid),
                             reads=[tpre[pi]], writes=[tgbo])
                        P.op("act", I("activation", out=gbo[64:128, o0:o0 + n], in_=pre[pi][64:128, i0:i0 + n], func=AF.Exp,
                                      bias=gbv[64:128, 0:1]), reads=[tpre[pi], tgbv], writes=[tgbo])
                        P.op("act", I("activation", out=gbo[64:128, o0:o0 + n], in_=gbo[64:128, o0:o0 + n], func=AF.Ln, bias=1.0),
                             reads=[tgbo], writes=[tgbo])
                        P.op("dve", I("tensor_scalar", out=gbo[64:128, o0:o0 + n], in0=gbo[64:128, o0:o0 + n],
                                      scalar1=gbv[64:128, 2:3], scalar2=None, op0=ALU.mult), reads=[tgbo, tgbv], writes=[tgbo])
                    out_ids.append(P.op("sp", I("dma_start", out=gb_d, in_=gbo[:]), reads=[tgbo], dsem=sio[6]))

            pending[0] = post
    if pending[0] is not None:
        pending[0]()
    P.op("sp", I("nop"), after=out_ids)
    return P


L = 4096
TB = 32
FG = 512
TW = 256
NTW = L // TW


def build_fourier(nc, uT_d, cc_d, cs_d, y_d):
    P = Prog(nc)
    uT = P.sb("uT", [128, 4, L], BF16)
    cc = P.sb("cc", [128, 2, 4, FG], BF16)
    AB = P.sb("AB", [128, 2, TB, FG], BF16)
    cs = [P.sb("cs%d" % i, [128, 2, TB, TW], BF16) for i in range(2)]
    yo = [P.sb("yo%d" % i, [128, TW], BF16) for i in range(4)]
    tuT, tcc = T(), T()
    tAB = [[T() for _ in range(TB)] for _ in range(2)]
    tcs = [T() for _ in range(2)]
    tyo = [T() for _ in range(4)]
    psA = [P.ps("psA%d" % i, [128, 512]) for i in range(4)]
    tpsA = [T() for _ in range(4)]
    psY = [P.ps("psY%d" % i, [128, 512]) for i in range(4)]
    tpsY = [T() for _ in range(4)]
    sio = [P.dsem("sio%d" % i) for i in range(6)]
    scs = [P.dsem("scs%d" % i) for i in range(2)]
    so = [P.dsem("so%d" % i) for i in range(4)]
    P.op("sp", I("dma_start", out=cc[:], in_=cc_d), writes=[tcc], dsem=sio[0])
    for k in range(4):
        P.op("sp", I("dma_start", out=uT[:, k, :], in_=uT_d[:, k, :]), writes=[tuT], dsem=sio[1 + k])
    n = 0
    for tb in range(TB):
        for j in range(2):
            a = n % 4
            n += 1
            for k in range(4):
                P.op("pe", I("matmul", psA[a][:], lhsT=uT[:, k, tb * 128:(tb + 1) * 128], rhs=cc[:, j, k, :],
                             start=(k == 0), stop=(k == 3)), reads=[tuT, tcc], writes=[tpsA[a]])
            if j == 0:
                P.op("act", I("activation", out=AB[:, j, tb, :], in_=psA[a][:], func=AF.Copy), reads=[tpsA[a]], writes=[tAB[j][tb]])
            else:
                P.op("dve", I("tensor_copy", out=AB[:, j, tb, :], in_=psA[a][:]), reads=[tpsA[a]], writes=[tAB[j][tb]])
    out_ids = []
    m = 0
    for tw in range(NTW):
        s = tw % 2
        P.op("pool", I("dma_start", out=cs[s][:], in_=cs_d[tw]), writes=[tcs[s]], dsem=scs[s])
        for c in range(4):
            a = m % 4
            m += 1
            i = 0
            for j in range(2):
                for tb in range(TB):
                    P.op("pe", I("matmul", psY[a][:, 0:TW], lhsT=AB[:, j, tb, c * 128:(c + 1) * 128], rhs=cs[s][:, j, tb, :],
                                 start=(i == 0), stop=(i == 2 * TB - 1)), reads=[tAB[j][tb], tcs[s]], writes=[tpsY[a]])
                    i += 1
            if a % 2 == 0:
                P.op("act", I("activation", out=yo[a][:], in_=psY[a][:, 0:TW], func=AF.Copy), reads=[tpsY[a]], writes=[tyo[a]])
            else:
                P.op("dve", I("tensor_copy", out=yo[a][:], in_=psY[a][:, 0:TW]), reads=[tpsY[a]], writes=[tyo[a]])
            out_ids.append(P.op("sp", I("dma_start", out=y_d[:, c, tw * TW:(tw + 1) * TW], in_=yo[a][:]),
                                reads=[tyo[a]], dsem=so[a]))
    P.op("sp", I("nop"), after=out_ids)
    return P


NMC = 18432 // 8
MT = [(0, 512), (512, 512), (1024, 512), (1536, 512), (2048, 256)]


def build_mod(nc, ct_d, w_d, b_d, m_d):
    P = Prog(nc)
    ct = P.sb("ct", [128, KC, 3], F32)
    sc = P.sb("sc", [128, KC, 3], F32)
    bs = P.sb("bs", [3, 2, NMC], F32)
    mo = P.sb("mo", [3, 2, NMC], F32)
    wt = [P.sb("wt%d" % i, [128, NMC], F32) for i in range(4)]
    tct, tsc, tbs, tmo = [T() for _ in range(4)]
    twt = [T() for _ in range(4)]
    ps = [P.ps("ps%d" % i, [128, 512]) for i in range(5)]
    tps = [T() for _ in range(5)]
    sio = [P.dsem("sio%d" % i) for i in range(3)]
    sw = [P.dsem("sw%d" % i) for i in range(4)]
    P.op("sp", I("dma_start", out=ct[:], in_=ct_d), writes=[tct], dsem=sio[0])
    P.op("sp", I("dma_start", out=bs[:], in_=b_d), writes=[tbs], dsem=sio[1])
    P.op("act", I("activation", out=sc[:], in_=ct[:], func=AF.Silu), reads=[tct], writes=[tsc])
    n = 0
    for l in range(2):
        for k in range(KC):
            s = n % 4
            n += 1
            P.op("sp", I("dma_start", out=wt[s][:], in_=w_d[l, k * 128:(k + 1) * 128, :]), writes=[twt[s]], dsem=sw[s])
            for i, (c0, w) in enumerate(MT):
                P.op("pe", I("matmul", ps[i][0:3, 0:w], lhsT=sc[:, k, :], rhs=wt[s][:, c0:c0 + w], start=(k == 0), stop=(k == KC - 1)),
                     reads=[tsc, twt[s]], writes=[tps[i]])
        for i, (c0, w) in enumerate(MT):
            P.op("dve", I("tensor_tensor", out=mo[:, l, c0:c0 + w], in0=ps[i][0:3, 0:w], in1=bs[:, l, c0:c0 + w], op=ALU.add),
                 reads=[tps[i], tbs], writes=[tmo])
    o = P.op("sp", I("dma_start", out=m_d, in_=mo[:]), reads=[tmo], dsem=sio[2])
    P.op("sp", I("nop"), after=[o])
    return P


import ml_dtypes
BFN = ml_dtypes.bfloat16
NCORES = 8
_cache = {}


def _dt(nc, n, s, t, k="ExternalInput"):
    return nc.dram_tensor(n, list(s), t, kind=k).ap()


def _finish(P):
    P.emit()
    P.close()


def prog_mod():
    nc = bass.Bass("TRN2", target_bir_lowering=False)
    ct = _dt(nc, "ct", [128, KC, 3], F32); w = _dt(nc, "w", [2, 2048, NMC], F32); b = _dt(nc, "b", [3, 2, NMC], F32)
    m = _dt(nc, "m", [3, 2, NMC], F32, "ExternalOutput")
    _finish(build_mod(nc, ct, w, b, m))
    return nc


def _ffn_w(nc, tag):
    return (_dt(nc, "wg" + tag, [D, DFF], F32), _dt(nc, "wu" + tag, [D, DFF], F32), _dt(nc, "wd" + tag, [DFF, D], F32))


def prog_l1():
    nc = bass.Bass("TRN2", target_bir_lowering=False)
    NV = 18
    hin = _dt(nc, "hin", [128, KC, 1088], F32); mods = _dt(nc, "mods", [128, NV, KC], F32)
    W = _ffn_w(nc, "0")
    hout = _dt(nc, "hout", [128, KC, 1088], F32, "ExternalOutput")
    uout = _dt(nc, "uout", [128, KC, 1088], BF16, "ExternalOutput")
    P = Prog(nc)
    S = TokStage(P, nc, 1024, 64, NV)
    S.load_h(hin, mods)
    S.mod_gw(0, 2, 7); S.mod_gw(0, 5, 8); S.mod_scale(3, 9, 0.5); S.mod_scale(6, 10, 0.5)
    S.mod_gw(11, 13, 16); S.mod_gw(11, 15, 17)
    S.ffn(*W, 7, 1, 9, 8, 4, 10)
    S.rms_stats()
    S.ada_norm(S.xn, S.txn, 16, 12, 17, 14)
    ids = S.store(hout, S.h, S.th) + S.store(uout, S.xn, S.txn)
    P.op("sp", I("nop"), after=ids)
    _finish(P)
    return nc


def prog_l3():
    nc = bass.Bass("TRN2", target_bir_lowering=False)
    NV = 17
    hin = _dt(nc, "hin", [128, KC, 1024], F32); mods = _dt(nc, "mods", [128, NV, KC], F32)
    yin = _dt(nc, "yin", [128, 32, 1024], BF16); wo = _dt(nc, "wo", [4096, D], F32)
    W0 = _ffn_w(nc, "0"); W1 = _ffn_w(nc, "1")
    hout = _dt(nc, "hout", [128, KC, 1024], F32, "ExternalOutput")
    uout = _dt(nc, "uout", [128, KC, 1024], BF16, "ExternalOutput")
    P = Prog(nc)
    S = TokStage(P, nc, 1024, 0, NV)
    S.load_h(hin, mods)
    S.mod_gw(1, 3, 5); S.mod_scale(4, 6, 0.5); S.mod_gw(7, 9, 11); S.mod_scale(10, 12, 0.5); S.mod_gw(13, 15, 16)
    S.mix(yin, 32, wo, 0)
    S.ffn(*W0, 5, 2, 6)
    S.ffn(*W1, 11, 8, 12)
    S.rms_stats()
    S.ada_norm(S.xn, S.txn, 16, 14)
    ids = S.store(hout, S.h, S.th) + S.store(uout, S.xn, S.txn)
    P.op("sp", I("nop"), after=ids)
    _finish(P)
    return nc


def prog_l5():
    nc = bass.Bass("TRN2", target_bir_lowering=False)
    NV = 8
    hin = _dt(nc, "hin", [128, KC, 1024], F32); mods = _dt(nc, "mods", [128, NV, KC], F32)
    yin = _dt(nc, "yin", [128, 16, 1024], BF16); wo = _dt(nc, "wo", [2048, D], F32)
    W0 = _ffn_w(nc, "0")
    hout = _dt(nc, "hout", [128, KC, 1024], F32, "ExternalOutput")
    P = Prog(nc)
    S = TokStage(P, nc, 1024, 0, NV)
    S.load_h(hin, mods)
    S.mod_gw(1, 3, 5); S.mod_scale(4, 6, 0.5)
    S.mix(yin, 16, wo, 0)
    S.ffn(*W0, 5, 2, 6)
    S.rms_stats()
    S.ada_norm(S.h, S.th, 7, None)
    ids = S.store(hout, S.h, S.th)
    P.op("sp", I("nop"), after=ids)
    _finish(P)
    return nc


def prog_inproj():
    nc = bass.Bass("TRN2", target_bir_lowering=False)
    uT = _dt(nc, "uT", [128, KC, TT], BF16); w = _dt(nc, "w", [2048, 12416], F32)
    cw = _dt(nc, "cw", [128, 64, 5], F32); gbv = _dt(nc, "gbv", [128, 2], F32)
    qk = _dt(nc, "qk", [128, 32, NOUT], BF16, "ExternalOutput"); v = _dt(nc, "v", [128, 32, NOUT], BF16, "ExternalOutput")
    sz = _dt(nc, "sz", [128, 32, 1024], BF16, "ExternalOutput"); gb = _dt(nc, "gb", [128, NOUT], F32, "ExternalOutput")
    _finish(build_inproj(nc, uT, w, cw, gbv, qk, v, sz, gb))
    return nc


def prog_scan():
    nc = bass.Bass("TRN2", target_bir_lowering=False)
    qT = _dt(nc, "qT", [128, 4, 4352], BF16); kT = _dt(nc, "kT", [128, 4, 4352], BF16)
    kM = _dt(nc, "kM", [128, 4, 34, 128], BF16); vM = _dt(nc, "vM", [128, 4, 34, 2, 128], BF16)
    sz = _dt(nc, "sz", [128, 4, 32, 2, 128], BF16)
    g = _dt(nc, "g", [128, 4, 34, 4], F32); b = _dt(nc, "b", [128, 4, 34, 4], F32)
    ng = _dt(nc, "ng", [128, 128], F32); cst = _dt(nc, "cst", [128, NCONST, 128], F32)
    y = _dt(nc, "y", [128, 4, 32, 2, 128], BF16, "ExternalOutput")
    _finish(build_scan(nc, qT, kT, kM, vM, sz, g, b, ng, cst, y, NQH=4))
    return nc


def prog_fourier():
    nc = bass.Bass("TRN2", target_bir_lowering=False)
    uT = _dt(nc, "uT", [128, 4, L], BF16); cc = _dt(nc, "cc", [128, 2, 4, FG], BF16); cs = _dt(nc, "cs", [16, 128, 2, 32, 256], BF16)
    y = _dt(nc, "y", [128, 4, L], BF16, "ExternalOutput")
    _finish(build_fourier(nc, uT, cc, cs, y))
    return nc


def fm(a):
    t, d = a.shape
    return np.ascontiguousarray(a.reshape(t, d // 128, 128).transpose(2, 1, 0))


def tm(a):
    p, n, t = a.shape
    return np.ascontiguousarray(a.transpose(2, 1, 0)).reshape(t, n * 128)


def vec(v):
    return v.reshape(KC, 128).T


def scan_consts():
    m = np.arange(128)[:, None]; i = np.arange(128)[None, :]
    c = np.zeros((128, NCONST, 128), np.float32)
    c[:, LE] = m <= i; c[:, GE] = m >= i; c[:, GT] = m > i; c[:, LT] = m < i
    c[:, ONES] = 1; c[:, IDENT] = m == i
    c[:, NEGF] = np.where(i < m, NEG, 0); c[:, NEGB] = np.where(i > m, NEG, 0)
    for lv in range(7):
        c[:, NM0 + lv] = -(((m >> (lv + 1)) == (i >> (lv + 1))) & ((m >> lv) != (i >> lv))).astype(np.float32) + (0 if lv == 0 else (m == i))
    return c


def fourier_consts():
    c = np.arange(512)
    ang = 2 * np.pi * ((c[:, None] * c[None, :]) % 512) / 512
    CC = np.stack([np.cos(ang), np.sin(ang)], 0) / np.sqrt(512.0)
    cc = np.ascontiguousarray(CC.reshape(2, 4, 128, 512).transpose(2, 0, 1, 3)).astype(BFN)
    t = np.arange(4096, dtype=np.int64)
    ang = 2 * np.pi * ((t[:, None] * t[None, :]) % 4096) / 4096
    tab = np.stack([np.cos(ang), -np.sin(ang)], 0).astype(np.float32) / 64.0
    cs = np.ascontiguousarray(tab.reshape(2, 32, 128, 16, 256).transpose(3, 2, 0, 1, 4)).astype(BFN)
    return cc, cs


def halo(a, lo, hi):
    n = a.shape[0]
    out = np.zeros((hi - lo,) + a.shape[1:], a.dtype)
    s, e = max(lo, 0), min(hi, n)
    out[s - lo:e - lo] = a[s:e]
    return out


def scan_inputs(q, k, v, sz, g, beta, hg):
    qs = q[:, 4 * hg:4 * hg + 4]; ks = k[:, 4 * hg:4 * hg + 4]
    vs = v[:, 8 * hg:8 * hg + 8].reshape(34, 128, 4, 2, 128)
    szs = sz[:, 8 * hg:8 * hg + 8].reshape(32, 128, 4, 2, 128)
    gs = g[:, :, 8 * hg:8 * hg + 8].reshape(34, 128, 2, 4, 2)
    bs = beta[:, :, 8 * hg:8 * hg + 8].reshape(34, 128, 2, 4, 2)
    return {
        "qT": np.ascontiguousarray(qs.transpose(2, 1, 0)),
        "kT": np.ascontiguousarray(ks.transpose(2, 1, 0)),
        "kM": np.ascontiguousarray(ks.reshape(34, 128, 4, 128).transpose(1, 2, 0, 3)),
        "vM": np.ascontiguousarray(vs.transpose(1, 2, 0, 3, 4)),
        "sz": np.ascontiguousarray(szs.transpose(1, 2, 0, 3, 4)),
        "g": np.ascontiguousarray(gs.transpose(1, 3, 0, 2, 4)).reshape(128, 4, 34, 4),
        "b": np.ascontiguousarray(bs.transpose(1, 3, 0, 2, 4)).reshape(128, 4, 34, 4),
    }


def _run(name, builder, in_maps):
    if name not in _cache:
        _cache[name] = builder()
    res = run_bass_kernel_spmd(_cache[name], in_maps, core_ids=list(range(NCORES)))
    return res.results


def kernel(x, c, ctx, c_ctx, norm_g, mod_w, mod_b, ffn_w_gate, ffn_w_up, ffn_w_down, dn_w_in, dn_conv_w,
           dn_a_log, dn_dt_bias, dn_norm_g, dn_w_out, fn_w_out, final_norm_g):
    f32 = np.float32
    A = lambda a: np.asarray(a, dtype=f32)
    x, c, ctx, c_ctx, norm_g, mod_b = A(x), A(c), A(ctx), A(c_ctx), A(norm_g), A(mod_b)
    mod_w = np.asarray(mod_w); ffn_w_gate = np.asarray(ffn_w_gate); ffn_w_up = np.asarray(ffn_w_up)
    ffn_w_down = np.asarray(ffn_w_down)
    cores = range(NCORES)
    cond = np.stack([c[0], c[1], c_ctx], 0)
    ct = np.ascontiguousarray(cond.reshape(3, KC, 128).transpose(2, 1, 0))
    ims = []
    for ci in cores:
        sl = slice(ci * NMC, (ci + 1) * NMC)
        ims.append({"ct": ct, "w": np.ascontiguousarray(mod_w[:, :, sl]),
                    "b": np.ascontiguousarray(np.broadcast_to(mod_b[None, :, sl], (3, 2, NMC)))})
    r = _run("mod", prog_mod, ims)
    m = np.concatenate([r[ci]["m"] for ci in cores], 2)
    M = lambda l, row, j: m[row, l, j * 2048:(j + 1) * 2048]

    ims = []
    for ci in cores:
        b, q = ci // 4, ci % 4
        a = np.concatenate([x[b, q * 1024:(q + 1) * 1024], ctx[b, q * 64:(q + 1) * 64]], 0)
        md = np.zeros((128, 18, KC), f32)
        md[:, 0] = vec(norm_g[0, 0]); md[:, 1] = vec(M(0, b, 0)); md[:, 2] = vec(M(0, b, 1)); md[:, 3] = vec(M(0, b, 2))
        md[:, 4] = vec(M(0, 2, 0)); md[:, 5] = vec(M(0, 2, 1)); md[:, 6] = vec(M(0, 2, 2))
        md[:, 11] = vec(norm_g[0, 1]); md[:, 12] = vec(M(0, b, 3)); md[:, 13] = vec(M(0, b, 4))
        md[:, 14] = vec(M(0, 2, 3)); md[:, 15] = vec(M(0, 2, 4))
        ims.append({"hin": fm(a), "mods": md, "wg0": ffn_w_gate[0, 0], "wu0": ffn_w_up[0, 0], "wd0": ffn_w_down[0, 0]})
    r = _run("l1", prog_l1, ims)
    h_fm = [r[ci]["hout"][:, :, 0:1024] for ci in cores]
    u_lat = [np.concatenate([tm(r[b * 4 + q]["uout"][:, :, 0:1024]) for q in range(4)], 0) for b in range(2)]
    u_ctx = [np.concatenate([tm(r[b * 4 + q]["uout"][:, :, 1024:1088]) for q in range(4)], 0) for b in range(2)]

    cw = np.ascontiguousarray(A(dn_conv_w)[0].reshape(5, 64, 128).transpose(2, 1, 0))
    gbv = np.zeros((128, 2), f32)
    gbv[64:, 0] = A(dn_dt_bias)[0].reshape(64); gbv[64:, 1] = A(dn_a_log)[0].reshape(64)
    w_in = np.asarray(dn_w_in)[0]
    ims = []
    for ci in cores:
        b, q = ci // 4, ci % 4
        a = np.concatenate([halo(u_lat[b], q * 1024 - 2, q * 1024 + 1026), halo(u_ctx[b], q * 64 - 2, q * 64 + 66)], 0)
        ims.append({"uT": fm(a), "w": w_in, "cw": cw, "gbv": gbv})
    r = _run("inproj", prog_inproj, ims)
    cst = scan_consts()
    ng = np.ascontiguousarray(np.broadcast_to(A(dn_norm_g)[0][None, :], (128, 128)))
    ims = []
    for b in range(2):
        def gather(key, lo, hi):
            return np.concatenate([r[b * 4 + q][key][..., lo:hi] for q in range(4)], -1)
        qk = np.concatenate([gather("qk", 1024, 1088), gather("qk", 0, 1024)], -1)
        vv = np.concatenate([gather("v", 1024, 1088), gather("v", 0, 1024)], -1)
        szz = gather("sz", 0, 1024)
        gb = np.concatenate([gather("gb", 1024, 1088), gather("gb", 0, 1024)], -1)
        q_tm = qk[:, 0:16].transpose(2, 1, 0)
        k_tm = qk[:, 16:32].transpose(2, 1, 0)
        v_tm = vv.transpose(2, 1, 0)
        sz_tm = szz.transpose(2, 1, 0)
        beta_tm = gb[0:64].T.reshape(4352, 2, 32)
        g_tm = gb[64:128].T.reshape(4352, 2, 32)
        for hg in range(4):
            d = scan_inputs(q_tm, k_tm, v_tm, sz_tm, g_tm, beta_tm, hg)
            d["ng"] = ng; d["cst"] = cst
            ims.append(d)
    r = _run("scan", prog_scan, ims)
    ypre = []
    for b in range(2):
        yb = np.stack([r[b * 4 + hg]["y"] for hg in range(4)], 0)
        ypre.append(np.ascontiguousarray(yb.transpose(3, 1, 0, 2, 4, 5)).reshape(4096, 4096))

    ims = []
    for ci in cores:
        b, q = ci // 4, ci % 4
        md = np.zeros((128, 17, KC), f32)
        md[:, 0] = vec(M(0, b, 5))
        md[:, 1] = vec(norm_g[0, 2]); md[:, 2] = vec(M(0, b, 6)); md[:, 3] = vec(M(0, b, 7)); md[:, 4] = vec(M(0, b, 8))
        md[:, 7] = vec(norm_g[1, 0]); md[:, 8] = vec(M(1, b, 0)); md[:, 9] = vec(M(1, b, 1)); md[:, 10] = vec(M(1, b, 2))
        md[:, 13] = vec(norm_g[1, 1]); md[:, 14] = vec(M(1, b, 3)); md[:, 15] = vec(M(1, b, 4))
        ims.append({"hin": np.ascontiguousarray(h_fm[ci]), "mods": md, "yin": fm(ypre[b][q * 1024:(q + 1) * 1024]),
                    "wo": np.asarray(dn_w_out)[0],
                    "wg0": ffn_w_gate[0, 1], "wu0": ffn_w_up[0, 1], "wd0": ffn_w_down[0, 1],
                    "wg1": ffn_w_gate[1, 0], "wu1": ffn_w_up[1, 0], "wd1": ffn_w_down[1, 0]})
    r = _run("l3", prog_l3, ims)
    h_fm = [r[ci]["hout"] for ci in cores]
    u1 = [np.concatenate([tm(r[b * 4 + q]["uout"]) for q in range(4)], 0) for b in range(2)]

    cc, cs = fourier_consts()
    ims = []
    for ci in cores:
        b, g = ci // 4, ci % 4
        ims.append({"uT": fm(u1[b][:, g * 512:(g + 1) * 512]), "cc": cc, "cs": cs})
    r = _run("fourier", prog_fourier, ims)
    yf = [np.concatenate([tm(r[b * 4 + g]["y"]) for g in range(4)], 1) for b in range(2)]

    ims = []
    for ci in cores:
        b, q = ci // 4, ci % 4
        md = np.zeros((128, 8, KC), f32)
        md[:, 0] = vec(M(1, b, 5))
        md[:, 1] = vec(norm_g[1, 2]); md[:, 2] = vec(M(1, b, 6)); md[:, 3] = vec(M(1, b, 7)); md[:, 4] = vec(M(1, b, 8))
        md[:, 7] = vec(A(final_norm_g))
        ims.append({"hin": np.ascontiguousarray(h_fm[ci]), "mods": md, "yin": fm(yf[b][q * 1024:(q + 1) * 1024]),
                    "wo": np.asarray(fn_w_out)[0],
                    "wg0": ffn_w_gate[1, 1], "wu0": ffn_w_up[1, 1], "wd0": ffn_w_down[1, 1]})
    r = _run("l5", prog_l5, ims)
    out = np.zeros((2, 4096, 2048), f32)
    for ci in cores:
        b, q = ci // 4, ci % 4
        out[b, q * 1024:(q + 1) * 1024] = tm(r[ci]["hout"])
    return out
```
